# Optimizing a Trainium2 kernel written in Bass

```python
import jax, jax.numpy as jnp
from jax import lax
import numpy as np

D_MODEL = 2048
BATCH = 8
SEQ = 4096
DEPTH = 2
DEC_BATCH = 8
DEC_SEQ = 64
PAST_LEN = 1024

CHUNK = 64
Q_BLOCK = 128
EPS = 1e-6
ROPE_THETA = 10000.0
MLSTM_H = 4
MLSTM_DK = 128
MLSTM_DV = 256
MLA_H = 8
MLA_NOPE = 128
MLA_ROPE = 64
MLA_V = 128
MLA_Q_LORA = 512
MLA_KV_LORA = 512
RET_H = 4
RET_DK = 128
RET_DV = 256
D_FF = ((8 * D_MODEL + 3 * 256 - 1) // (3 * 256)) * 256
IN_SPLITS = (
    MLSTM_H * MLSTM_DK, MLSTM_H * MLSTM_DK, MLSTM_H * MLSTM_DV, MLSTM_H * MLSTM_DV, MLSTM_H, MLSTM_H,
    MLA_Q_LORA, MLA_KV_LORA, MLA_ROPE,
    RET_H * RET_DK, RET_H * RET_DK, RET_H * RET_DV, RET_H * RET_DV,
    D_MODEL, D_MODEL, D_MODEL,
)
D_IN = sum(IN_SPLITS)

kernel_name = 'hybrid_mlstm_mla_retention_stream_step'


def rmsnorm(x, g):
    x32 = x.astype(jnp.float32)
    y = x32 * lax.rsqrt(jnp.mean(x32 * x32, axis=-1, keepdims=True) + EPS)
    return (y * g.astype(jnp.float32)).astype(x.dtype)


def head_norm(x, g, center):
    if center:
        x = x - jnp.mean(x, axis=-1, keepdims=True)
    x = x * lax.rsqrt(jnp.mean(x * x, axis=-1, keepdims=True) + EPS)
    return x.reshape(x.shape[:2] + (-1,)) * g.astype(jnp.float32)


def rope(x, pos):
    d = x.shape[-1]
    inv = 1.0 / (ROPE_THETA ** (jnp.arange(0, d, 2, dtype=jnp.float32) / d))
    ang = pos.astype(jnp.float32)[:, None] * inv[None, :]
    ang = ang.reshape((ang.shape[0],) + (1,) * (x.ndim - 3) + (d // 2,))
    cos, sin = jnp.cos(ang), jnp.sin(ang)
    x32 = x.astype(jnp.float32)
    x1, x2 = x32[..., : d // 2], x32[..., d // 2:]
    return jnp.concatenate([x1 * cos - x2 * sin, x1 * sin + x2 * cos], axis=-1).astype(x.dtype)


def run_chunks(step, carry, xs):
    S = xs[0].shape[1]
    if S <= CHUNK:
        return step(carry, xs)
    nc = S // CHUNK
    xs_c = tuple(jnp.moveaxis(a.reshape((a.shape[0], nc, CHUNK) + a.shape[2:]), 1, 0) for a in xs)
    carry, ys = lax.scan(step, carry, xs_c)
    ys = jnp.moveaxis(ys, 0, 1)
    return carry, ys.reshape((ys.shape[0], S) + ys.shape[3:])


def mlstm_step(carry, xs):
    c, n, m = carry
    q, k, v, ig, lf = xs
    L = q.shape[1]
    b = jnp.cumsum(lf, axis=1).transpose(0, 2, 1)
    i_t = ig.transpose(0, 2, 1)
    causal = jnp.tril(jnp.ones((L, L), dtype=bool))
    log_d = jnp.where(causal, b[..., :, None] - b[..., None, :] + i_t[..., None, :], -jnp.inf)
    log_prev = b + m[..., None]
    m_t = jnp.maximum(log_prev, jnp.max(log_d, axis=-1))
    dmat = jnp.exp(log_d - m_t[..., None])
    prev_scale = jnp.exp(log_prev - m_t)
    w = jnp.einsum('blhd,bshd->bhls', q, k) * dmat
    num = jnp.einsum('bhls,bshv->blhv', w, v) + jnp.einsum('bhl,bhvd,blhd->blhv', prev_scale, c, q)
    den = jnp.sum(w, axis=-1) + prev_scale * jnp.einsum('bhd,blhd->bhl', n, q)
    h = num / jnp.maximum(jnp.abs(den), jnp.exp(-m_t)).transpose(0, 2, 1)[..., None]
    b_last = b[..., -1]
    log_s = b_last[..., None] - b + i_t
    m_new = jnp.maximum(b_last + m, jnp.max(log_s, axis=-1))
    ws = jnp.exp(log_s - m_new[..., None])
    carry_scale = jnp.exp(b_last + m - m_new)
    c_new = carry_scale[..., None, None] * c + jnp.einsum('bhs,bshv,bshd->bhvd', ws, v, k)
    n_new = carry_scale[..., None] * n + jnp.einsum('bhs,bshd->bhd', ws, k)
    return (c_new, n_new, m_new), h


def ret_step(r, xs):
    q, k, v = xs
    L = q.shape[1]
    lg = jnp.log1p(-jnp.exp2(-5.0 - jnp.arange(RET_H, dtype=jnp.float32)))
    idx = jnp.arange(L, dtype=jnp.float32)
    diff = idx[:, None] - idx[None, :]
    dmat = jnp.where(diff >= 0, jnp.exp(jnp.maximum(diff, 0.0)[None] * lg[:, None, None]), 0.0)
    inner = jnp.einsum('blhd,bshd->bhls', q, k) * dmat[None]
    xi = jnp.exp((idx[:, None] + 1.0) * lg[None, :])
    out = jnp.einsum('bhls,bshv->blhv', inner, v) + jnp.einsum('blhd,bhdv->blhv', q, r) * xi[None, :, :, None]
    zeta = jnp.exp((L - 1.0 - idx)[:, None] * lg[None, :])
    r_new = jnp.exp(L * lg)[None, :, None, None] * r + jnp.einsum('bshd,bshv,sh->bhdv', k, v, zeta)
    return r_new, out


def chunk_causal_attention(q, k, v, q_pos, k_pos):
    B, Q, H, _ = q.shape
    Dv = v.shape[-1]
    scale = q.shape[-1] ** -0.5
    k_chunk = k_pos // CHUNK

    def attend(qb, qpb):
        s = jnp.einsum('bqhd,bkhd->bhqk', qb, k).astype(jnp.float32) * scale
        mask = k_chunk[None, :] <= (qpb // CHUNK)[:, None]
        s = jnp.where(mask[None, None], s, -1e30)
        p = jax.nn.softmax(s, axis=-1).astype(v.dtype)
        return jnp.einsum('bhqk,bkhd->bqhd', p, v)

    if Q <= Q_BLOCK:
        return attend(q, q_pos)
    nb = Q // Q_BLOCK
    qb = jnp.moveaxis(q.reshape(B, nb, Q_BLOCK, H, q.shape[-1]), 1, 0)
    pb = q_pos.reshape(nb, Q_BLOCK)
    out = lax.map(lambda a: attend(a[0], a[1]), (qb, pb))
    return jnp.moveaxis(out, 0, 1).reshape(B, Q, H, Dv)


def token_mixers(h, pos0, ckv_past, kr_past, c0, n0, m0, r0, w_in, b_in, g_mlstm, w_up_m,
                 g_qa, w_uq, g_kva, w_ukv, w_up_a, g_ret, w_up_r, w_o):
    f32 = jnp.float32
    B, S, _ = h.shape
    pos = pos0 + jnp.arange(S, dtype=jnp.int32)
    z = h @ w_in + b_in
    split_at = np.cumsum(IN_SPLITS)[:-1].tolist()
    (mq, mk, mv, mo, mi, mf, a_dq, a_dkv, a_kr, rq, rk, rv, rg,
     gate_m, gate_a, gate_r) = jnp.split(z, split_at, axis=-1)

    q = mq.reshape(B, S, MLSTM_H, MLSTM_DK).astype(f32)
    k = mk.reshape(B, S, MLSTM_H, MLSTM_DK).astype(f32) * (MLSTM_DK ** -0.5)
    v = mv.reshape(B, S, MLSTM_H, MLSTM_DV).astype(f32)
    ig = mi.astype(f32)
    lf = jax.nn.log_sigmoid(mf.astype(f32))
    (c1, n1, m1), hm = run_chunks(mlstm_step, (c0.astype(f32), n0.astype(f32), m0.astype(f32)),
                                  (q, k, v, ig, lf))
    ym = head_norm(hm, g_mlstm, False) * jax.nn.sigmoid(mo.astype(f32))
    u_m = ym.astype(h.dtype) @ w_up_m

    qa = (rmsnorm(a_dq, g_qa) @ w_uq).reshape(B, S, MLA_H, MLA_NOPE + MLA_ROPE)
    qa = jnp.concatenate([qa[..., :MLA_NOPE], rope(qa[..., MLA_NOPE:], pos)], axis=-1)
    ckv = rmsnorm(a_dkv, g_kva)
    kr = rope(a_kr, pos)
    ckv_all = jnp.concatenate([ckv_past.astype(h.dtype), ckv], axis=1)
    kr_all = jnp.concatenate([kr_past.astype(h.dtype), kr], axis=1)
    K = ckv_all.shape[1]
    kv = (ckv_all @ w_ukv).reshape(B, K, MLA_H, MLA_NOPE + MLA_V)
    ka = jnp.concatenate([kv[..., :MLA_NOPE],
                          jnp.broadcast_to(kr_all[:, :, None, :], (B, K, MLA_H, MLA_ROPE))], axis=-1)
    ya = chunk_causal_attention(qa, ka, kv[..., MLA_NOPE:], pos, jnp.arange(K, dtype=jnp.int32))
    u_a = ya.reshape(B, S, MLA_H * MLA_V) @ w_up_a

    q = rope(rq.reshape(B, S, RET_H, RET_DK), pos).astype(f32)
    k = rope(rk.reshape(B, S, RET_H, RET_DK), pos).astype(f32) * (RET_DK ** -0.5)
    v = rv.reshape(B, S, RET_H, RET_DV).astype(f32)
    r1, yr = run_chunks(ret_step, r0.astype(f32), (q, k, v))
    yr = head_norm(yr, g_ret, True) * jax.nn.silu(rg.astype(f32))
    u_r = yr.astype(h.dtype) @ w_up_r

    merged = (jax.nn.sigmoid(gate_m) * u_m + jax.nn.sigmoid(gate_a) * u_a
              + jax.nn.sigmoid(gate_r) * u_r)
    return merged @ w_o, (ckv, kr, c1, n1, m1, r1)


def swiglu(h, w_gu, w_down):
    a, g = jnp.split(h @ w_gu, 2, axis=-1)
    return (jax.nn.silu(g) * a) @ w_down


def run_trunk(x, pos0, past_ckv, past_kr, c0, n0, m0, r0, params):
    (g_mix_pre, w_in, b_in, g_mlstm, w_up_m, g_qa, w_uq, g_kva, w_ukv, w_up_a, g_ret, w_up_r,
     w_o, g_mix_post, g_ffn_pre, w_gu, w_down, g_ffn_post) = params
    outs = [[] for _ in range(6)]
    for l in range(DEPTH):
        h = rmsnorm(x, g_mix_pre[l])
        mix, st = token_mixers(h, pos0, past_ckv[l], past_kr[l], c0[l], n0[l], m0[l], r0[l],
                               w_in[l], b_in[l], g_mlstm[l], w_up_m[l], g_qa[l], w_uq[l],
                               g_kva[l], w_ukv[l], w_up_a[l], g_ret[l], w_up_r[l], w_o[l])
        x = x + rmsnorm(mix, g_mix_post[l])
        h = rmsnorm(x, g_ffn_pre[l])
        x = x + rmsnorm(swiglu(h, w_gu[l], w_down[l]), g_ffn_post[l])
        for lst, s in zip(outs, st):
            lst.append(s.astype(x.dtype))
    return x, [jnp.stack(lst) for lst in outs]


def setup_inputs(seed: int = 0) -> dict:
    key = jax.random.key(seed)
    ks = jax.random.split(key, 32)
    f32 = jnp.float32

    def nrm(k, shape, scale):
        return jax.random.normal(k, shape, f32) * scale

    def gain(k, shape):
        return 1.0 + 0.01 * jax.random.normal(k, shape, f32)

    f_off = sum(IN_SPLITS[:5])
    b_in = nrm(ks[10], (DEPTH, D_IN), 0.01)
    b_in = b_in.at[:, f_off:f_off + MLSTM_H].add(jnp.linspace(3.0, 6.0, MLSTM_H))
    return {
        'x_prompt': nrm(ks[0], (BATCH, SEQ, D_MODEL), 1.0),
        'x_sample': nrm(ks[1], (DEC_BATCH, DEC_SEQ, D_MODEL), 1.0),
        'cache_mla_ckv': nrm(ks[2], (DEPTH, DEC_BATCH, PAST_LEN, MLA_KV_LORA), 1.0),
        'cache_mla_krope': nrm(ks[3], (DEPTH, DEC_BATCH, PAST_LEN, MLA_ROPE), 1.0),
        'state_mlstm_c': nrm(ks[4], (DEPTH, DEC_BATCH, MLSTM_H, MLSTM_DV, MLSTM_DK), 1.0),
        'state_mlstm_n': nrm(ks[5], (DEPTH, DEC_BATCH, MLSTM_H, MLSTM_DK), 1.0),
        'state_mlstm_m': nrm(ks[6], (DEPTH, DEC_BATCH, MLSTM_H), 0.5),
        'state_ret': nrm(ks[7], (DEPTH, DEC_BATCH, RET_H, RET_DK, RET_DV), 1.0),
        'g_mix_pre': gain(ks[8], (DEPTH, D_MODEL)),
        'w_in': nrm(ks[9], (DEPTH, D_MODEL, D_IN), D_MODEL ** -0.5),
        'b_in': b_in,
        'g_mlstm': gain(ks[11], (DEPTH, MLSTM_H * MLSTM_DV)),
        'w_up_m': nrm(ks[12], (DEPTH, MLSTM_H * MLSTM_DV, D_MODEL), (MLSTM_H * MLSTM_DV) ** -0.5),
        'g_qa': gain(ks[13], (DEPTH, MLA_Q_LORA)),
        'w_uq': nrm(ks[14], (DEPTH, MLA_Q_LORA, MLA_H * (MLA_NOPE + MLA_ROPE)), MLA_Q_LORA ** -0.5),
        'g_kva': gain(ks[15], (DEPTH, MLA_KV_LORA)),
        'w_ukv': nrm(ks[16], (DEPTH, MLA_KV_LORA, MLA_H * (MLA_NOPE + MLA_V)), MLA_KV_LORA ** -0.5),
        'w_up_a': nrm(ks[17], (DEPTH, MLA_H * MLA_V, D_MODEL), (MLA_H * MLA_V) ** -0.5),
        'g_ret': gain(ks[18], (DEPTH, RET_H * RET_DV)),
        'w_up_r': nrm(ks[19], (DEPTH, RET_H * RET_DV, D_MODEL), (RET_H * RET_DV) ** -0.5),
        'w_o': nrm(ks[20], (DEPTH, D_MODEL, D_MODEL), D_MODEL ** -0.5),
        'g_mix_post': gain(ks[21], (DEPTH, D_MODEL)),
        'g_ffn_pre': gain(ks[22], (DEPTH, D_MODEL)),
        'w_gu': nrm(ks[23], (DEPTH, D_MODEL, 2 * D_FF), D_MODEL ** -0.5),
        'w_down': nrm(ks[24], (DEPTH, D_FF, D_MODEL), D_FF ** -0.5),
        'g_ffn_post': gain(ks[25], (DEPTH, D_MODEL)),
    }


def reference(x_prompt, x_sample, cache_mla_ckv, cache_mla_krope, state_mlstm_c, state_mlstm_n,
              state_mlstm_m, state_ret, g_mix_pre, w_in, b_in, g_mlstm, w_up_m, g_qa, w_uq, g_kva,
              w_ukv, w_up_a, g_ret, w_up_r, w_o, g_mix_post, g_ffn_pre, w_gu, w_down, g_ffn_post):
    params = (g_mix_pre, w_in, b_in, g_mlstm, w_up_m, g_qa, w_uq, g_kva, w_ukv, w_up_a, g_ret,
              w_up_r, w_o, g_mix_post, g_ffn_pre, w_gu, w_down, g_ffn_post)
    dt = x_prompt.dtype
    B = x_prompt.shape[0]
    y_prompt, (p_ckv, p_kr, p_c, p_n, p_m, p_r) = run_trunk(
        x_prompt, 0,
        jnp.zeros((DEPTH, B, 0, MLA_KV_LORA), dt), jnp.zeros((DEPTH, B, 0, MLA_ROPE), dt),
        jnp.zeros((DEPTH, B, MLSTM_H, MLSTM_DV, MLSTM_DK), dt), jnp.zeros((DEPTH, B, MLSTM_H, MLSTM_DK), dt),
        jnp.zeros((DEPTH, B, MLSTM_H), dt), jnp.zeros((DEPTH, B, RET_H, RET_DK, RET_DV), dt),
        params)
    y_sample, (s_ckv, s_kr, s_c, s_n, s_m, s_r) = run_trunk(
        x_sample, cache_mla_ckv.shape[2], cache_mla_ckv, cache_mla_krope,
        state_mlstm_c, state_mlstm_n, state_mlstm_m, state_ret, params)
    return (y_prompt, y_sample, p_ckv, p_kr, p_c, p_n, p_m, p_r,
            s_ckv, s_kr, s_c, s_n, s_m, s_r)
```

```python
import numpy as np
import concourse.bass as bass
import concourse.mybir as mybir
from concourse.bass_utils import run_bass_kernel_spmd

F32 = mybir.dt.float32
BF16 = mybir.dt.bfloat16
AF = mybir.ActivationFunctionType
OP = mybir.AluOpType
AX = mybir.AxisListType

D = 2048
DIN = 13384
DFF = 5632
NL = 2
PAST = 1024
SS = 64
EPS = 1e-6
NEG = -1.0e30
C_MQ, C_MK, C_MV, C_MO, C_MI, C_ADQ, C_ADKV, C_AKR, C_RQ, C_RK, C_RV, C_RG, C_GM, C_GA, C_GR = (
    0, 512, 1024, 2048, 3072, 3080, 3592, 4104, 4168, 4680, 5192, 6216, 7240, 9288, 11336)
SLOT_ELEMS = 4352
NWG = 700
NSLOT = 4
PIPE_DEPTH = 2
ARENA_E = 27456


class Buf:
    __slots__ = ("name", "lw", "rd", "dsem", "dcnt", "lw_dma")

    REG = []

    def __init__(self, name):
        Buf.REG.append(self)
        self.name = name
        self.lw = None
        self.rd = {}
        self.dsem = None
        self.dcnt = 0
        self.lw_dma = False


class Tracker:
    def __init__(self, nc):
        self.nc = nc
        self.eng = {"pe": nc.tensor, "act": nc.scalar, "dve": nc.vector, "pool": nc.gpsimd, "sp": nc.sync}
        self.sem = {k: nc.alloc_semaphore("e_" + k) for k in ("pe", "act", "dve", "pool")}
        self.cnt = {k: 0 for k in self.sem}
        self.waited = {k: {} for k in self.eng}
        self.nsem = 4

    def _need(self, q, reads, writes, is_dma):
        need = {}

        def add(sv):
            if sv is None:
                return
            s, v = sv
            k = id(s)
            if k not in need or need[k][1] < v:
                need[k] = (s, v)

        for b in reads:
            add(b.lw)
        for b in writes:
            if not (is_dma and b.lw_dma and not b.rd):
                add(b.lw)
            for sv in b.rd.values():
                add(sv)
        E = self.eng[q]
        wd = self.waited[q]
        for k, (s, v) in need.items():
            if q == "pe" and s is self.sem["pe"]:
                continue
            if wd.get(k, 0) >= v:
                continue
            E.wait_ge(s, v)
            wd[k] = v

    def op(self, q, fn, reads=(), writes=()):
        self._need(q, reads, writes, False)
        ins = fn(self.eng[q])
        self.cnt[q] += 1
        s = self.sem[q]
        ins.then_inc(s, 1)
        sv = (s, self.cnt[q])
        for b in reads:
            b.rd[id(s)] = sv
        for b in writes:
            b.lw = sv
            b.rd = {}
            b.lw_dma = False
        return ins

    def dma(self, q, out, in_, sb, reads=(), writes=(), **kw):
        self._need(q, reads, writes, True)
        if sb.dsem is None:
            sb.dsem = self.nc.alloc_semaphore("d_" + sb.name)
            self.nsem += 1
        ins = self.eng[q].dma_start(out=out, in_=in_, **kw)
        sb.dcnt += 16
        ins.then_inc(sb.dsem, 16)
        sv = (sb.dsem, sb.dcnt)
        for b in reads:
            b.rd[id(sb.dsem)] = sv
        for b in writes:
            keep = b.lw_dma and not b.rd
            b.lw = sv
            if not keep:
                b.rd = {}
            b.lw_dma = True
        return ins

    def wait_all(self, q, bufs):
        self._need(q, bufs, bufs, False)


class Seq:
    pass


def CK(fn, key):
    fn.ckey = key
    return fn


class Builder:
    def __init__(self, S_P, do_sample=True, nl=NL):
        self.S_P = S_P
        self.do_sample = do_sample
        self.nl = nl
        Buf.REG = []
        nc = self.nc = bass.Bass("TRN2", target_bir_lowering=False)
        self.T = Tracker(nc)
        self.bufs = {}
        self.rr = 0
        self._decl()
        self._alloc()

    def din(self, name, shape, dt=F32):
        return self.nc.dram_tensor(name, list(shape), dt, kind="ExternalInput").ap()

    def dout(self, name, shape):
        return self.nc.dram_tensor(name, list(shape), F32, kind="ExternalOutput").ap()

    def _decl(self):
        S_P = self.S_P
        nc = self.nc
        self.xp = self.din("xp", [S_P, D])
        self.xs = self.din("xs", [SS, D])
        self.cckv = self.din("cckv", [NL, PAST, 512])
        self.ckr = self.din("ckr", [NL, PAST, 64])
        self.sc = self.din("sc", [NL, 4, 256, 128])
        self.sn = self.din("sn", [NL, 4, 128])
        self.sm = self.din("sm", [NL, 4])
        self.sr = self.din("sr", [NL, 4, 128, 256])
        self.g_mix_pre = self.din("g_mix_pre", [NL, D])
        self.w_in = self.din("w_in", [NL, D, DIN])
        self.b_in = self.din("b_in", [NL, DIN])
        self.g_mlstm = self.din("g_mlstm", [NL, 1024])
        self.w_up_m = self.din("w_up_m", [NL, 1024, D])
        self.g_qa = self.din("g_qa", [NL, 512])
        self.w_uq = self.din("w_uq", [NL, 512, 1536])
        self.g_kva = self.din("g_kva", [NL, 512])
        self.w_ukv = self.din("w_ukv", [NL, 512, 2048])
        self.w_up_a = self.din("w_up_a", [NL, 1024, D])
        self.g_ret = self.din("g_ret", [NL, 1024])
        self.w_up_r = self.din("w_up_r", [NL, 1024, D])
        self.w_o = self.din("w_o", [NL, D, D])
        self.g_mix_post = self.din("g_mix_post", [NL, D])
        self.g_ffn_pre = self.din("g_ffn_pre", [NL, D])
        self.w_gu = self.din("w_gu", [NL, D, 2 * DFF])
        self.w_down = self.din("w_down", [NL, DFF, D])
        self.g_ffn_post = self.din("g_ffn_post", [NL, D])
        NP = S_P + SS
        self.t_cos128 = self.din("t_cos128", [NP, 64])
        self.t_sin128 = self.din("t_sin128", [NP, 64])
        self.t_cos64 = self.din("t_cos64", [NP, 32])
        self.t_sin64 = self.din("t_sin64", [NP, 32])
        self.t_c32 = self.din("t_c32", [128, 1300])
        self.yp = self.dout("yp", [S_P, D])
        self.ys = self.dout("ys", [SS, D])
        self.o_pckv = self.dout("o_pckv", [NL, S_P, 512])
        self.o_pkr = self.dout("o_pkr", [NL, S_P, 64])
        self.o_pc = self.dout("o_pc", [NL, 4, 256, 128])
        self.o_pn = self.dout("o_pn", [NL, 4, 128])
        self.o_pm = self.dout("o_pm", [NL, 4])
        self.o_pr = self.dout("o_pr", [NL, 4, 128, 256])
        self.o_sckv = self.dout("o_sckv", [NL, SS, 512])
        self.o_skr = self.dout("o_skr", [NL, SS, 64])
        self.o_sc = self.dout("o_sc", [NL, 4, 256, 128])
        self.o_sn = self.dout("o_sn", [NL, 4, 128])
        self.o_sm = self.dout("o_sm", [NL, 4])
        self.o_sr = self.dout("o_sr", [NL, 4, 128, 256])
        self.bhl = nc.dram_tensor("bhl", [2, NL * DIN], BF16).ap()
        self.wscr_t = [nc.dram_tensor("wscr%d" % i, [100, 128, SLOT_ELEMS], BF16).ap() for i in range((NWG + 99) // 100)]
        self.wcache = {}
        nktp = S_P // 128
        self.kv = {}
        for nm, nkt in (("p", nktp), ("s", 9)):
            self.kv[nm] = (
                nc.dram_tensor("kn_" + nm, [NL, nkt, 128, 1024], BF16).ap(),
                nc.dram_tensor("kr_" + nm, [NL, nkt, 128, 128], BF16).ap(),
                nc.dram_tensor("vv_" + nm, [NL, nkt, 128, 8 * 129], BF16).ap(),
            )

    def sb(self, name, shape, dt=F32):
        t = self.nc.alloc_sbuf_tensor(name, list(shape), dt)
        self.bufs[name] = Buf(name)
        return t

    def B(self, *names):
        return [self.bufs[n] for n in names]

    def _alloc(self):
        nc = self.nc
        sb = self.sb
        self.arena = nc.alloc_sbuf_tensor("arena", [128, ARENA_E], BF16)
        self._ao = 0

        def carve(name, shape, dt, mkbuf=True):
            n = 1
            for d_ in shape[1:]:
                n *= d_
            nb = n * (4 if dt == F32 else 2)
            o = self._ao
            self._ao += (nb + 31) // 32 * 32
            assert self._ao <= ARENA_E * 2, (name, self._ao)
            v = self.arena[:, o // 2:o // 2 + nb // 2]
            if dt == F32:
                v = v.bitcast(F32)
            if len(shape) == 3:
                v = v.rearrange("p (a b) -> p a b", a=shape[1])
            if mkbuf:
                self.bufs[name] = Buf(name)
            return v
        self.carve = carve
        self.c32 = sb("c32", [128, 1300])
        self.identb = sb("identb", [128, 128], BF16)
        self.ones2 = sb("ones2", [2, 128], BF16)
        self.onesf = sb("onesf", [128, 128])
        self.epsc = sb("epsc", [128, 1])
        self.x = sb("x", [128, D])
        self.gbuf = sb("gbuf", [128, D])
        self.gsm = sb("gsm", [128, 3, 1024])
        self.tmpA = sb("tmpA", [128, D])
        self.hb = sb("hb", [128, D], BF16)
        self.hT = sb("hT", [128, 16, 128], BF16)
        self.wslot = [sb("ws%d" % i, [128, SLOT_ELEMS], BF16) for i in range(NSLOT)]
        self.wbias = [nc.alloc_sbuf_tensor("wb%d" % i, [2, 520], BF16) for i in range(NSLOT)]
        self.wsi = 0
        self.rt128 = sb("rt128", [128, 2, 64])
        self.rt64 = sb("rt64", [128, 2, 32])
        self.qmT = None
        self.kmT = None
        self.km_tm = None
        self.vm = sb("vm", [128, 4, 257], BF16)
        self.gs_m = carve("gs_m", [128, 4, 256], F32)
        self.igf = sb("igf", [128, 8])
        self.hqT = None
        self.ckvT = None
        self.ckv_o = sb("ckv_o", [128, 512])
        self.kr_o = sb("kr_o", [128, 64])
        self.krd = sb("krd", [128, 128], BF16)
        self.krT = sb("krT", [128, 128], BF16)
        self.rqT = None
        self.rkT = None
        self.rk_tm = None
        self.vr = None
        self.gs_r = carve("gs_r", [128, 4, 256], F32)
        self.QnT = carve("QnT", [128, 8, 128], BF16)
        self.QrT = carve("QrT", [128, 4, 128], BF16)
        self.KnT = carve("KnT", [128, 8, 128], BF16)
        self.Vst = sb("Vst", [128, 8, 129], BF16)
        self.kvs = []
        for i in range(2):
            k = carve("kvk%d" % i, [128, 8, 128], BF16, False)
            r = carve("kvr%d" % i, [128, 128], BF16, False)
            v = carve("kvv%d" % i, [128, 8, 129], BF16, False)
            self.bufs["kvs%d" % i] = Buf("kvs%d" % i)
            self.kvs.append((k, r, v, self.bufs["kvs%d" % i]))
        self.kvi = 0
        self.PT = carve("PT", [128, 8, 128], BF16)
        self.PT2 = sb("PT2", [128, 8, 128], BF16)
        self.ytm = carve("ytm", [128, 1024], BF16)
        self.sm4 = {n: sb("s4_" + n, [128, 4]) for n in
                    ("b", "a", "cm", "mx", "nmx", "ps", "emt", "den", "rden", "ss", "sc", "ws", "mean", "t1", "t2")}
        self.sm8 = sb("sm8", [128, 8])
        self.bc8 = sb("bc8", [128, 8])
        self.carry = sb("carry", [128, 4])
        self.diag = carve("diag", [128, 4, 128], F32)
        self.Am = carve("Am", [128, 4, 128], F32)
        self.dmT = carve("dmT", [128, 4, 128], F32)
        self.wT = None
        self.hsum = carve("hsum", [128, 4, 257], F32)
        self.sqf = sb("sq", [128, 4, 257])
        self.inter = self.sqf
        self.bufs["inter"] = self.bufs["sq"]
        self.sq = self.sqf[:, :, 0:256]
        self.kws = None
        self.cn = [sb("cn%d" % l, [128, 4, 257]) for l in range(NL)]
        self.cnb = [sb("cnb%d" % l, [128, 4, 257], BF16) for l in range(NL)]
        self.mbc = [sb("mbc%d" % l, [128, 4]) for l in range(NL)]
        self.rs = [sb("rs%d" % l, [128, 4, 256]) for l in range(NL)]
        self.rb = [sb("rb%d" % l, [128, 4, 256], BF16) for l in range(NL)]
        self.stg = sb("stg", [128, 2, 128])
        self.sg = sb("sg", [128, 512])
        self.tmpB = sb("tmpB", [128, 512])

        for nm_, shp_ in (("qmT", [128, 4, 128]), ("kmT", [128, 4, 128]), ("km_tm", [128, 4, 128]), ("hqT", [128, 4, 128]), ("ckvT", [128, 4, 128]),
                          ("rqT", [128, 4, 128]), ("rkT", [128, 4, 128]), ("rk_tm", [128, 4, 128]), ("vr", [128, 4, 256]), ("wT", [128, 4, 128]), ("kws", [128, 4, 128])):
            setattr(self, nm_, carve(nm_, shp_, BF16))
        self.yT = {b: carve("yT_" + b, [128, 8, 128], BF16) for b in "mar"}
        ao_ = self._ao
        self._ao = 35840
        self.actT = carve("actT", [128, 44, 128], BF16)
        assert self._ao <= ao_ - 6144, (self._ao, ao_)
        self._ao = ao_
        self.hT2 = sb("hT2", [128, 16, 128], BF16)
        self.yT2 = {b: sb("yT2_" + b, [128, 8, 128], BF16) for b in "mar"}
        self.tiles_ = [
            dict(hT=(self.hT2, self.bufs["hT2"]), yT={b: (self.yT2[b], self.bufs["yT2_" + b]) for b in "mar"}),
            dict(hT=(self.hT, self.bufs["hT"]), yT={b: (self.yT[b], self.bufs["yT_" + b]) for b in "mar"}),
        ]
        self.bprep = self.tmpA[0:16, 0:1673]
        self.bprep2 = self.gbuf[0:16, 0:1673]
        self.bph = self.hb[0:16, 0:1673]
        self.bpl = self.actT[:, :, :].rearrange("p a b -> p (a b)")[0:16, 0:1673]
        for a_, b_ in (("bprep", "tmpA"), ("bprep2", "gbuf"), ("bph", "hb"), ("bpl", "actT")):
            self.bufs[a_] = self.bufs[b_]
        print("SBUF bytes remaining", nc.sbuf_bytes_remaining)
        ao = self._ao
        self._ao = 0
        self.hT_B = carve("hT_B", [128, 16, 128], BF16)
        self.act_tm_B = carve("act_tm_B", [128, 3072], BF16)
        self.actT_B = carve("actT_B", [128, 44, 128], BF16)
        self.tmpA_B = carve("tmpA_B", [128, D], F32)
        self.act_tm = carve("act_tm", [128, 3072], BF16)
        assert self._ao <= ao, (self._ao, ao)
        self.x2 = sb("x2", [128, D])
        self.xb = [(self.x, self.bufs["x"]), (self.x2, self.bufs["x2"])]
        print("SBUF bytes remaining (final)", nc.sbuf_bytes_remaining)
        self.ps = []
        for i in range(8):
            t = nc.alloc_psum_tensor("ps%d" % i, [128, 512], F32)
            self.bufs["ps%d" % i] = Buf("ps%d" % i)
            self.ps.append((t, self.bufs["ps%d" % i]))
        self.psi = 0
        self.attn_active = False
        self.held = set()
        self.pending = []
        self.defer_on = False

    def bank(self):
        nb = 5 if self.attn_active else 8
        while (self.psi % nb) in self.held:
            self.psi += 1
        t, b = self.ps[self.psi % nb]
        self.psi += 1
        return t, b

    def bank_i(self):
        nb = 5 if self.attn_active else 8
        while (self.psi % nb) in self.held:
            self.psi += 1
        i = self.psi % nb
        self.psi += 1
        return i

    def flush(self):
        while self.pending:
            self.pending.pop(0)()

    def set_x(self, i):
        self.x, self.bufs["x"] = self.xb[i]

    def set_tile(self, i):
        tl = self.tiles_[i]
        self.hT, self.bufs["hT"] = tl["hT"]
        self.yT = {b: tl["yT"][b][0] for b in "mar"}
        for b in "mar":
            self.bufs["yT_" + b] = tl["yT"][b][1]

    def barrier(self):
        allb = list(Buf.REG)
        for q in ("pe", "act", "dve", "sp"):
            self.T.wait_all(q, allb)

    def ev(self):
        self.rr += 1
        return "act" if self.rr % 2 else "dve"

    def copy(self, out, in_, reads, writes, q=None):
        q = q or self.ev()
        if q == "act":
            self.T.op("act", lambda e: e.activation(out, in_, AF.Identity), reads, writes)
        else:
            self.T.op(q, lambda e: e.tensor_copy(out, in_), reads, writes)

    def transposes(self, src, src_buf, n, TS, dst, dst_buf, dst_k0=0):
        T = self.T
        j = 0
        while j < n:
            g = min(4, n - j)
            pt, pb = self.bank()
            pv = pt[:, :].bitcast(BF16)
            for i in range(g):
                T.op("pe", lambda e, i=i: e.transpose(pv[:, i * 128:i * 128 + TS], src[0:TS, (j + i) * 128:(j + i + 1) * 128],
                                                      self.identb[0:TS, 0:TS]), [src_buf, self.bufs["identb"]], [pb])
            o = dst[:, dst_k0 + j:dst_k0 + j + g, 0:TS]
            i_ = pv[:, 0:g * 128].rearrange("p (g t) -> p g t", g=g)[:, :, 0:TS]
            self.copy(o, i_, [pb], [dst_buf])
            j += g

    def load_w(self, w2d, k0, nkc, cols):
        i = self.wsi % NSLOT
        self.wsi += 1
        slot, sbuf, bt = self.wslot[i], self.bufs["ws%d" % i], self.wbias[i]
        src = w2d[k0 * 128:(k0 + nkc) * 128]
        if callable(cols):
            ck = cols.ckey
            src = cols(src).rearrange("(k p) a b -> p k a b", p=128)
        else:
            ck = tuple(cols)
            src = src[:, cols[0]:cols[0] + cols[1]].rearrange("(k p) n -> p k n", p=128)
        ncols = 1
        for s_ in src.shape[2:]:
            ncols *= s_
        n = nkc * ncols
        view = slot[:, 0:n].rearrange("p (k n) -> p k n", k=nkc)
        key = (w2d.name, str(w2d.offset), k0, nkc, ck)
        if key in self.wcache:
            gi, gb = self.wcache[key]
            self.T.dma("pool", slot[:, 0:n], self.wscr_t[gi // 100][gi % 100, :, 0:n], sbuf, reads=[gb], writes=[sbuf])
            return view, sbuf, bt
        gi = len(self.wcache)
        gb = Buf("wg%d" % gi)
        self.wcache[key] = (gi, gb)
        sview = self.wscr_t[gi // 100][gi % 100, :, 0:n].rearrange("p (k n) -> p k n", k=nkc)
        dstv = view
        if len(src.shape) == 4:
            dstv = view.rearrange("p k (a b) -> p k a b", a=src.shape[2])
            sview = sview.rearrange("p k (a b) -> p k a b", a=src.shape[2])
            for k in range(nkc):
                self.T.dma("pool", dstv[:, k], src[:, k], sbuf, reads=[], writes=[sbuf])
            for k in range(nkc):
                self.T.dma("pool", sview[:, k], src[:, k], sbuf, reads=[], writes=[gb, sbuf])
        else:
            self.T.dma("pool", dstv, src, sbuf, reads=[], writes=[sbuf])
            self.T.dma("pool", sview, src, sbuf, reads=[], writes=[gb, sbuf])
        return view, sbuf, bt

    def gemm(self, xT, xbuf, nkc, TS, w2d, cols, subs, bias_off=None, ksplit=None):
        T = self.T
        multi = isinstance(xT, list)
        xs = xT if multi else [(xT, xbuf)]
        ncols = cols[1] if not callable(cols) else sum(s[1] for s in subs)
        kmax = SLOT_ELEMS // ncols
        chunks = []
        k = 0
        while k < nkc:
            c = min(kmax, nkc - k)
            chunks.append((k, c))
            k += c
        bidx = [[self.bank_i() for _ in subs] for _ in xs]
        banks = [[self.ps[i] for i in row] for row in bidx]
        bt0 = None
        for ci, (k0, kn) in enumerate(chunks):
            view, wbuf, bt = self.load_w(w2d, k0, kn, cols)
            if ci == 0 and bias_off is not None:
                T.dma("pool", bt[0:2, 0:ncols], self.bhl[:, bias_off:bias_off + ncols], wbuf, reads=[self.bufs["bhl"]], writes=[wbuf])
            for xi, (xT_, xb_) in enumerate(xs):
                for (off, n, _), (pt, pb) in zip(subs, banks[xi]):
                    first = ci == 0
                    if first and bias_off is not None:
                        T.op("pe", lambda e, pt=pt, off=off, n=n: e.matmul(pt[0:TS, 0:n], self.ones2[0:2, 0:TS], bt[0:2, off:off + n], start=True, stop=False),
                             [wbuf, self.bufs["ones2"]], [pb])
                        first = False
                    for kk in range(kn):
                        last = (ci == len(chunks) - 1) and kk == kn - 1
                        T.op("pe", lambda e, kk=kk, first=first, last=last, pt=pt, off=off, n=n, xT_=xT_: e.matmul(
                            pt[0:TS, 0:n], xT_[:, k0 + kk, 0:TS], view[:, kk, off:off + n], start=first, stop=last),
                            [wbuf, xb_], [pb])
                        first = False
        mine = set(i for row in bidx for i in row)

        def run_handlers():
            for xi in range(len(xs)):
                for (off, n, h), (pt, pb) in zip(subs, banks[xi]):
                    if multi:
                        h(pt[0:TS, 0:n], pb, xi)
                    else:
                        h(pt[0:TS, 0:n], pb)
            self.held -= mine
        if self.defer_on:
            self.held |= mine
            self.pending.append(run_handlers)
            while len(self.pending) > PIPE_DEPTH:
                self.pending.pop(0)()
        else:
            self.flush()
            run_handlers()

    def rms_scale(self, src, src_buf, TS, Dn, rstd):
        T = self.T
        jb = self.bufs["sq"]
        junk = self.sqf[:, :, :].rearrange("p a b -> p (a b)")
        T.op("dve", lambda e: e.memset(self.sm4["t2"][:, :], 0.0), [], [self.bufs["s4_t2"]])
        n = 0
        while n < Dn:
            c = min(1024, Dn - n)
            T.op("act", lambda e, n=n, c=c: e.activation(junk[0:TS, 0:c], src[0:TS, n:n + c], AF.Square,
                                                          accum_out=self.sm4["t2"][0:TS, (n // 1024):(n // 1024) + 1]),
                 [src_buf], [jb, self.bufs["s4_t2"]])
            n += c
        nch = (Dn + 1023) // 1024
        if nch > 1:
            T.op("dve", lambda e: e.tensor_reduce(rstd, self.sm4["t2"][0:TS, 0:nch], AX.X, OP.add),
                 [self.bufs["s4_t2"]], [self.bufs["s4_t1"]])
        else:
            T.op("dve", lambda e: e.tensor_copy(rstd, self.sm4["t2"][0:TS, 0:1]), [self.bufs["s4_t2"]], [self.bufs["s4_t1"]])
        T.op("act", lambda e: e.activation(rstd, rstd, AF.Sqrt, bias=self.epsc[0:TS, 0:1], scale=1.0 / Dn), [self.bufs["s4_t1"], self.bufs["epsc"]], [self.bufs["s4_t1"]])
        T.op("dve", lambda e: e.reciprocal(rstd, rstd), [self.bufs["s4_t1"]], [self.bufs["s4_t1"]])

    def load_g(self, g_ap):
        self.T.dma("sp", self.gbuf[:, :], g_ap.partition_broadcast(128), self.bufs["gbuf"], [], [self.bufs["gbuf"]])

    def norm_to_hT(self, l, TS, g_ap, dst=None):
        T = self.T
        dT, dTb = dst if dst is not None else (self.hT, self.bufs["hT"])
        rstd = self.sm4["t1"][0:TS, 0:1]
        self.load_g(g_ap)
        self.rms_scale(self.x, self.bufs["x"], TS, D, rstd)
        T.op("dve", lambda e: e.scalar_tensor_tensor(self.hb[0:TS, :], self.x[0:TS, :], rstd, self.gbuf[0:TS, :], OP.mult, OP.mult),
             self.B("x", "s4_t1", "gbuf"), self.B("hb"))
        self.transposes(self.hb, self.bufs["hb"], 16, TS, dT, dTb)

    def resid_add(self, l, TS, g_ap, src=None):
        T = self.T
        tA, tAb = src if src is not None else (self.tmpA, self.bufs["tmpA"])
        x, xB = self.x, self.bufs["x"]
        rstd = self.sm4["t1"][0:TS, 0:1]
        self.load_g(g_ap)
        self.rms_scale(tA, tAb, TS, D, rstd)
        T.op("dve", lambda e: e.scalar_tensor_tensor(tA[0:TS, :], tA[0:TS, :], rstd, self.gbuf[0:TS, :], OP.mult, OP.mult),
             [tAb] + self.B("s4_t1", "gbuf"), [tAb])
        T.op("dve", lambda e: e.tensor_tensor(x[0:TS, :], x[0:TS, :], tA[0:TS, :], OP.add), [xB, tAb], [xB])

    def rope(self, src, src_buf, TS, H, d, tab, tab_buf, out, out_buf, scale=None):
        T = self.T
        hd = d // 2
        x1, x2 = src[0:TS, :, 0:hd], src[0:TS, :, hd:d]
        cos = tab[0:TS, 0, :].unsqueeze(1).broadcast_to([TS, H, hd])
        sin = tab[0:TS, 1, :].unsqueeze(1).broadcast_to([TS, H, hd])
        t1 = self.sqf[:, :, :].rearrange("p a b -> p (a b)")[0:TS, 0:H * hd].rearrange("p (h d) -> p h d", h=H)
        t2 = self.sqf[:, :, :].rearrange("p a b -> p (a b)")[0:TS, 512:512 + H * hd].rearrange("p (h d) -> p h d", h=H)
        jb = self.bufs["sq"]
        T.op("dve", lambda e: e.tensor_tensor(t1, x1, cos, OP.mult), [src_buf, tab_buf], [jb])
        T.op("dve", lambda e: e.tensor_tensor(t2, x2, sin, OP.mult), [src_buf, tab_buf], [jb])
        T.op("dve", lambda e: e.tensor_tensor(out[0:TS, :, 0:hd], t1, t2, OP.subtract), [jb], [out_buf])
        T.op("dve", lambda e: e.tensor_tensor(t1, x1, sin, OP.mult), [src_buf, tab_buf], [jb])
        T.op("dve", lambda e: e.tensor_tensor(t2, x2, cos, OP.mult), [src_buf, tab_buf], [jb])
        T.op("dve", lambda e: e.tensor_tensor(out[0:TS, :, hd:d], t1, t2, OP.add), [jb], [out_buf])
        if scale is not None:
            T.op("dve", lambda e: e.tensor_scalar(out[0:TS], out[0:TS], scale, None, OP.mult), [out_buf], [out_buf])

    def prologue(self):
        T = self.T
        nc = self.nc
        self.bufs["bhl"] = Buf("bhl")
        T.dma("sp", self.c32[:, :], self.t_c32, self.bufs["c32"], [], [self.bufs["c32"]])
        C = self.c32
        self.ident = C[:, 0:128]
        self.tri = C[:, 128:256]
        self.negmask = C[:, 256:384]
        self.negmaskT = C[:, 384:512]
        self.DT = C[:, 640:1152].rearrange("p (h l) -> p h l", h=4)
        self.misc = C[:, 1152:1172]
        T.op("dve", lambda e: e.tensor_copy(self.identb[:, :], self.ident), self.B("c32"), self.B("identb"))
        T.op("dve", lambda e: e.memset(self.ones2[:, :], 1.0), [], self.B("ones2"))
        T.op("dve", lambda e: e.memset(self.onesf[:, :], 1.0), [], self.B("onesf"))
        T.op("dve", lambda e: e.memset(self.epsc[:, :], EPS), [], self.B("epsc"))
        T.op("dve", lambda e: e.memset(self.vm[:, :, :], 1.0), [], self.B("vm"))
        T.op("dve", lambda e: e.memset(self.Vst[:, :, :], 1.0), [], self.B("Vst"))
        bflat = self.b_in.rearrange("l n -> (l n)").rearrange("(a b) -> a b", a=16)
        T.dma("sp", self.bprep, bflat, self.bufs["bprep"], [], self.B("bprep"))
        T.op("dve", lambda e: e.tensor_copy(self.bph, self.bprep), self.B("bprep"), self.B("bph"))
        T.op("dve", lambda e: e.tensor_copy(self.bprep2, self.bph), self.B("bph"), self.B("bprep2"))
        T.op("dve", lambda e: e.tensor_tensor(self.bprep2, self.bprep, self.bprep2, OP.subtract), self.B("bprep", "bprep2"), self.B("bprep2"))
        T.op("dve", lambda e: e.tensor_copy(self.bpl, self.bprep2), self.B("bprep2"), self.B("bpl"))
        T.dma("sp", self.bhl[0].rearrange("(a b) -> a b", a=16), self.bph, self.bufs["bph"], self.B("bph"), [self.bufs["bhl"]])
        T.dma("sp", self.bhl[1].rearrange("(a b) -> a b", a=16), self.bpl, self.bufs["bph"], self.B("bpl"), [self.bufs["bhl"]])
        for nm in ("p", "s"):
            for j, t in enumerate(("kn", "kr", "vv")):
                self.bufs[t + "_" + nm] = Buf(t + "_" + nm)

    def load_layer_consts(self, l):
        T = self.T
        g = self.gsm
        b = self.bufs["gsm"]
        T.dma("sp", g[:, 0, :], self.g_mlstm[l].partition_broadcast(128), b, [], [b])
        T.dma("sp", g[:, 1, :], self.g_ret[l].partition_broadcast(128), b, [], [b])
        T.dma("sp", g[:, 2, 0:512], self.g_qa[l].partition_broadcast(128), b, [], [b])
        T.dma("sp", g[:, 2, 512:1024], self.g_kva[l].partition_broadcast(128), b, [], [b])

    def _stage_in(self, seq, t, l, TS):
        T = self.T
        w = self.w_in[l]
        bo = l * DIN
        r0 = seq.tab0 + t * 128
        T.dma("sp", self.rt128[0:TS, 0, :], self.t_cos128[r0:r0 + TS, :], self.bufs["rt128"], [], self.B("rt128"))
        T.dma("sp", self.rt128[0:TS, 1, :], self.t_sin128[r0:r0 + TS, :], self.bufs["rt128"], [], self.B("rt128"))
        T.dma("sp", self.rt64[0:TS, 0, :], self.t_cos64[r0:r0 + TS, :], self.bufs["rt64"], [], self.B("rt64"))
        T.dma("sp", self.rt64[0:TS, 1, :], self.t_sin64[r0:r0 + TS, :], self.bufs["rt64"], [], self.B("rt64"))
        hT, hTb = self.hT, self.bufs["hT"]
        tA = self.tmpA
        tAb = self.bufs["tmpA"]

        def G(c0, subs, n=None):
            n = n or sum(s[1] for s in subs)
            self.gemm(hT, hTb, 16, TS, w, (c0, n), subs, bias_off=bo + c0)

        def h_mq(ps, pb):
            self.copy(self.hb[0:TS, 0:512], ps, [pb], self.B("hb"))
            self.transposes(self.hb, self.bufs["hb"], 4, TS, self.qmT, self.bufs["qmT"])
        G(C_MQ, [(0, 512, h_mq)])

        def h_mk(ps, pb):
            T.op("act", lambda e: e.activation(self.km_tm[0:TS, :, :].rearrange("p a b -> p (a b)"), ps, AF.Identity, scale=128 ** -0.5), [pb], self.B("km_tm"))
            self.transposes(self.km_tm[:, :, :].rearrange("p a b -> p (a b)"), self.bufs["km_tm"], 4, TS, self.kmT, self.bufs["kmT"])
        G(C_MK, [(0, 512, h_mk)])
        for j in range(2):
            def h_mv(ps, pb, j=j):
                self.copy(self.vm[0:TS, 2 * j:2 * j + 2, 0:256], ps.rearrange("p (h v) -> p h v", h=2), [pb], self.B("vm"))
            G(C_MV + 512 * j, [(0, 512, h_mv)])
        for j in range(2):
            def h_mo(ps, pb, j=j):
                o = self.gs_m[0:TS, 2 * j:2 * j + 2, :].rearrange("p a b -> p (a b)")
                T.op("act", lambda e: e.activation(o, ps, AF.Sigmoid), [pb], self.B("gs_m"))
                T.op("dve", lambda e: e.tensor_tensor(o, o, self.gsm[0:TS, 0, 512 * j:512 * j + 512], OP.mult), self.B("gs_m", "gsm"), self.B("gs_m"))
            G(C_MO + 512 * j, [(0, 512, h_mo)])

        def h_if(ps, pb):
            T.op("dve", lambda e: e.tensor_copy(self.igf[0:TS, 0:4], ps[:, 0:4]), [pb], self.B("igf"))
            T.op("act", lambda e: e.activation(self.igf[0:TS, 4:8], ps[:, 4:8], AF.Exp, scale=-1.0), [pb], self.B("igf"))
            T.op("act", lambda e: e.activation(self.igf[0:TS, 4:8], self.igf[0:TS, 4:8], AF.Ln, bias=1.0), self.B("igf"), self.B("igf"))
            T.op("dve", lambda e: e.tensor_scalar(self.igf[0:TS, 4:8], self.igf[0:TS, 4:8], -1.0, None, OP.mult), self.B("igf"), self.B("igf"))

        def h_adq(ps, pb):
            self.copy(tA[0:TS, 0:512], ps, [pb], [tAb])
            rstd = self.sm4["t1"][0:TS, 0:1]
            self.rms_scale(tA, tAb, TS, 512, rstd)
            T.op("dve", lambda e: e.scalar_tensor_tensor(self.hb[0:TS, 0:512], tA[0:TS, 0:512], rstd, self.gsm[0:TS, 2, 0:512], OP.mult, OP.mult),
                 [tAb] + self.B("s4_t1", "gsm"), self.B("hb"))
            self.transposes(self.hb, self.bufs["hb"], 4, TS, self.hqT, self.bufs["hqT"])
        G(C_MI, [(0, 8, h_if), (8, 512, h_adq)])

        def h_adkv(ps, pb):
            self.copy(tA[0:TS, 0:512], ps, [pb], [tAb])
            rstd = self.sm4["t1"][0:TS, 0:1]
            self.rms_scale(tA, tAb, TS, 512, rstd)
            T.op("dve", lambda e: e.scalar_tensor_tensor(self.ckv_o[0:TS, :], tA[0:TS, 0:512], rstd, self.gsm[0:TS, 2, 512:1024], OP.mult, OP.mult),
                 [tAb] + self.B("s4_t1", "gsm"), self.B("ckv_o"))
            T.dma("sp", seq.o_ckv[l, t * 128:t * 128 + TS, :], self.ckv_o[0:TS, :], self.bufs["ckv_o"], self.B("ckv_o"), [])
            T.op("act", lambda e: e.activation(self.hb[0:TS, 0:512], self.ckv_o[0:TS, :], AF.Identity), self.B("ckv_o"), self.B("hb"))
            self.transposes(self.hb, self.bufs["hb"], 4, TS, self.ckvT, self.bufs["ckvT"])
        G(C_ADKV, [(0, 512, h_adkv)])

        def h_akr(ps, pb):
            self.copy(tA[0:TS, 0:64], ps, [pb], [tAb])
            self.rope(tA[:, 0:64].rearrange("p (h d) -> p h d", h=1), tAb, TS, 1, 64, self.rt64, self.bufs["rt64"],
                      self.kr_o[:, :].rearrange("p (h d) -> p h d", h=1), self.bufs["kr_o"])
            T.dma("sp", seq.o_kr[l, t * 128:t * 128 + TS, :], self.kr_o[0:TS, :], self.bufs["kr_o"], self.B("kr_o"), [])
            T.op("act", lambda e: e.activation(self.krd[0:TS, 0:64], self.kr_o[0:TS, :], AF.Identity), self.B("kr_o"), self.B("krd"))
            T.op("act", lambda e: e.activation(self.krd[0:TS, 64:128], self.kr_o[0:TS, :], AF.Identity), self.B("kr_o"), self.B("krd"))
            self.transposes(self.krd, self.bufs["krd"], 1, TS, self.krT[:, :].rearrange("p (o t) -> p o t", o=1), self.bufs["krT"])
        G(C_AKR, [(0, 64, h_akr)])

        def h_rq(ps, pb):
            self.copy(tA[0:TS, 0:512], ps, [pb], [tAb])
            self.rope(tA[:, 0:512].rearrange("p (h d) -> p h d", h=4), tAb, TS, 4, 128, self.rt128, self.bufs["rt128"],
                      self.hb[:, 0:512].rearrange("p (h d) -> p h d", h=4), self.bufs["hb"])
            self.transposes(self.hb, self.bufs["hb"], 4, TS, self.rqT, self.bufs["rqT"])
        G(C_RQ, [(0, 512, h_rq)])

        def h_rk(ps, pb):
            self.copy(tA[0:TS, 0:512], ps, [pb], [tAb])
            self.rope(tA[:, 0:512].rearrange("p (h d) -> p h d", h=4), tAb, TS, 4, 128, self.rt128, self.bufs["rt128"],
                      self.rk_tm, self.bufs["rk_tm"], scale=128 ** -0.5)
            self.transposes(self.rk_tm[:, :, :].rearrange("p a b -> p (a b)"), self.bufs["rk_tm"], 4, TS, self.rkT, self.bufs["rkT"])
        G(C_RK, [(0, 512, h_rk)])
        for j in range(2):
            def h_rv(ps, pb, j=j):
                self.copy(self.vr[0:TS, 2 * j:2 * j + 2, :].rearrange("p a b -> p (a b)"), ps, [pb], self.B("vr"))
            G(C_RV + 512 * j, [(0, 512, h_rv)])
        for j in range(2):
            def h_rg(ps, pb, j=j):
                o = self.gs_r[0:TS, 2 * j:2 * j + 2, :].rearrange("p a b -> p (a b)")
                T.op("act", lambda e: e.activation(o, ps, AF.Silu), [pb], self.B("gs_r"))
                T.op("dve", lambda e: e.tensor_tensor(o, o, self.gsm[0:TS, 1, 512 * j:512 * j + 512], OP.mult), self.B("gs_r", "gsm"), self.B("gs_r"))
            G(C_RG + 512 * j, [(0, 512, h_rg)])

    def stage_in(self, seq, t, l, TS):
        self.defer_on = True
        self._stage_in(seq, t, l, TS)
        self.flush()
        self.defer_on = False

    def bcast_rows(self, col, col_buf, L):
        T = self.T
        idb = self.ident[0:L, 0:L].unsqueeze(1).broadcast_to([L, 4, L])
        cb = col[0:L, 0:4].unsqueeze(2).broadcast_to([L, 4, L])
        T.op("dve", lambda e: e.tensor_tensor(self.diag[0:L, :, 0:L], idb, cb, OP.mult), [col_buf] + self.B("c32"), self.B("diag"))
        pt, pb = self.bank()
        for h in range(4):
            T.op("pe", lambda e, h=h: e.matmul(pt[0:L, h * L:(h + 1) * L], self.onesf[0:L, 0:L], self.diag[0:L, h, 0:L], start=True, stop=True),
                 self.B("diag", "onesf"), [pb])
        return pt[0:L, 0:4 * L].rearrange("p (h l) -> p h l", h=4), pb

    def mlstm(self, l, L):
        T = self.T
        s4 = self.sm4
        sb4 = {k: self.bufs["s4_" + k] for k in s4}
        ig, lf = self.igf[0:L, 0:4], self.igf[0:L, 4:8]
        cn, cnb, mbc = self.cn[l], self.cnb[l], self.mbc[l]
        cnB, cnbB, mbcB = self.bufs["cn%d" % l], self.bufs["cnb%d" % l], self.bufs["mbc%d" % l]
        b, a, cm, mx, nmx = (s4[k][0:L, :] for k in ("b", "a", "cm", "mx", "nmx"))
        pt, pb = self.bank()
        T.op("pe", lambda e: e.matmul(pt[0:L, 0:4], self.tri[0:L, 0:L], lf, start=True, stop=True), self.B("c32", "igf"), [pb])
        T.op("dve", lambda e: e.tensor_copy(b, pt[0:L, 0:4]), [pb], [sb4["b"]])
        T.op("dve", lambda e: e.tensor_tensor(a, ig, b, OP.subtract), [sb4["b"]] + self.B("igf"), [sb4["a"]])
        Abc, Ab = self.bcast_rows(s4["a"], sb4["a"], L)
        nm = self.negmask[0:L, 0:L].unsqueeze(1).broadcast_to([L, 4, L])
        T.op("dve", lambda e: e.tensor_tensor(self.Am[0:L, :, 0:L], Abc, nm, OP.add), [Ab] + self.B("c32"), self.B("Am"))
        T.op("dve", lambda e: e.tensor_reduce(cm, self.Am[0:L, :, 0:L], AX.X, OP.max), self.B("Am"), [sb4["cm"]])
        T.op("dve", lambda e: e.tensor_tensor(mx, cm, mbc[0:L, :], OP.max), [sb4["cm"], mbcB], [sb4["mx"]])
        T.op("dve", lambda e: e.tensor_scalar(nmx, mx, -1.0, None, OP.mult), [sb4["mx"]], [sb4["nmx"]])
        mt = self.sm8[0:L, 4:8]
        T.op("dve", lambda e: e.tensor_copy(self.sm8[0:L, 0:4], b), [sb4["b"]], self.B("sm8"))
        T.op("dve", lambda e: e.tensor_tensor(mt, b, mx, OP.add), [sb4["b"], sb4["mx"]], self.B("sm8"))
        ps_, emt = s4["ps"][0:L, :], s4["emt"][0:L, :]
        T.op("dve", lambda e: e.tensor_tensor(ps_, mbc[0:L, :], mx, OP.subtract), [mbcB, sb4["mx"]], [sb4["ps"]])
        T.op("act", lambda e: e.activation(ps_, ps_, AF.Exp), [sb4["ps"]], [sb4["ps"]])
        T.op("act", lambda e: e.activation(emt, mt, AF.Exp, scale=-1.0), self.B("sm8"), [sb4["emt"]])
        Bbc, Bb = self.bcast_rows(s4["nmx"], sb4["nmx"], L)
        nmT = self.negmaskT[0:L, 0:L].unsqueeze(1).broadcast_to([L, 4, L])
        T.op("dve", lambda e: e.tensor_tensor(self.Am[0:L, :, 0:L], Bbc, nmT, OP.add), [Bb] + self.B("c32"), self.B("Am"))
        for h in range(4):
            T.op("act", lambda e, h=h: e.activation(self.dmT[0:L, h, 0:L], self.Am[0:L, h, 0:L], AF.Exp, bias=s4["a"][0:L, h:h + 1]),
                 self.B("Am") + [sb4["a"]], self.B("dmT"))
        pt, pb = self.bank()
        for h in range(4):
            T.op("pe", lambda e, h=h: e.matmul(pt[0:L, h * L:(h + 1) * L], self.kmT[:, h, 0:L], self.qmT[:, h, 0:L], start=True, stop=True),
                 self.B("kmT", "qmT"), [pb])
        T.op("dve", lambda e: e.tensor_tensor(self.wT[0:L, :, 0:L], pt[0:L, 0:4 * L].rearrange("p (h l) -> p h l", h=4), self.dmT[0:L, :, 0:L], OP.mult),
             [pb] + self.B("dmT"), self.B("wT"))
        for h in range(4):
            p1, b1 = self.bank()
            p2, b2 = self.bank()
            T.op("pe", lambda e, h=h: e.matmul(p1[0:L, 0:257], self.wT[0:L, h, 0:L], self.vm[0:L, h, :], start=True, stop=True), self.B("wT", "vm"), [b1])
            T.op("pe", lambda e, h=h: e.matmul(p2[0:L, 0:257], self.qmT[:, h, 0:L], cnb[:, h, :], start=True, stop=True), [self.bufs["qmT"], cnbB], [b2])
            T.op("act", lambda e, h=h: e.activation(self.inter[0:L, h, :], p2[0:L, 0:257], AF.Identity, scale=s4["ps"][0:L, h:h + 1]), [b2, sb4["ps"]], self.B("inter"))
            T.op("dve", lambda e, h=h: e.tensor_tensor(self.hsum[0:L, h, :], p1[0:L, 0:257], self.inter[0:L, h, :], OP.add), [b1] + self.B("inter"), self.B("hsum"))
        den, rden, ss, sc = (s4[k][0:L, :] for k in ("den", "rden", "ss", "sc"))
        T.op("act", lambda e: e.activation(den, self.hsum[0:L, :, 256], AF.Abs), self.B("hsum"), [sb4["den"]])
        T.op("dve", lambda e: e.tensor_tensor(den, den, emt, OP.max), [sb4["den"], sb4["emt"]], [sb4["den"]])
        T.op("dve", lambda e: e.reciprocal(rden, den), [sb4["den"]], [sb4["rden"]])
        hv = self.hsum[0:L, :, 0:256]
        T.op("dve", lambda e: e.tensor_tensor(self.sq[0:L, :, :], hv, hv, OP.mult), self.B("hsum"), self.B("sq"))
        T.op("dve", lambda e: e.tensor_reduce(ss, self.sq[0:L, :, :], AX.X, OP.add), self.B("sq"), [sb4["ss"]])
        T.op("dve", lambda e: e.tensor_tensor(ss, ss, rden, OP.mult), [sb4["ss"], sb4["rden"]], [sb4["ss"]])
        T.op("dve", lambda e: e.tensor_tensor(ss, ss, rden, OP.mult), [sb4["ss"], sb4["rden"]], [sb4["ss"]])
        T.op("act", lambda e: e.activation(ss, ss, AF.Sqrt, bias=self.epsc[0:L, 0:1], scale=1.0 / 256), [sb4["ss"], self.bufs["epsc"]], [sb4["ss"]])
        T.op("dve", lambda e: e.reciprocal(ss, ss), [sb4["ss"]], [sb4["ss"]])
        T.op("dve", lambda e: e.tensor_tensor(sc, ss, rden, OP.mult), [sb4["ss"], sb4["rden"]], [sb4["sc"]])
        T.op("dve", lambda e: e.tensor_tensor(self.sq[0:L, :, :], hv, s4["sc"][0:L, :].unsqueeze(2).broadcast_to([L, 4, 256]), OP.mult),
             self.B("hsum") + [sb4["sc"]], self.B("sq"))
        T.op("dve", lambda e: e.tensor_tensor(self.ytm[0:L, :].rearrange("p (h v) -> p h v", h=4), self.sq[0:L, :, :], self.gs_m[0:L, :, :], OP.mult),
             self.B("sq", "gs_m"), self.B("ytm"))
        self.transposes(self.ytm, self.bufs["ytm"], 8, L, self.yT["m"], self.bufs["yT_m"])
        sel = self.c32[0:L, 512:640] if L == 128 else self.c32[0:L, 1172:1300]
        pt, pb = self.bank()
        T.op("pe", lambda e: e.matmul(pt[:, 0:8], sel, self.sm8[0:L, :], start=True, stop=True), self.B("c32", "sm8"), [pb])
        T.op("dve", lambda e: e.tensor_copy(self.bc8[:, :], pt[:, 0:8]), [pb], self.B("bc8"))
        blast, mnew = self.bc8[:, 0:4], self.bc8[:, 4:8]
        t1 = s4["t1"]
        T.op("dve", lambda e: e.tensor_tensor(t1[:, :], blast, mnew, OP.subtract), self.B("bc8"), [sb4["t1"]])
        ws = s4["ws"][0:L, :]
        T.op("dve", lambda e: e.tensor_tensor(ws, a, t1[0:L, :], OP.add), [sb4["a"], sb4["t1"]], [sb4["ws"]])
        T.op("act", lambda e: e.activation(ws, ws, AF.Exp), [sb4["ws"]], [sb4["ws"]])
        T.op("dve", lambda e: e.tensor_tensor(self.carry[:, :], t1[:, :], mbc[:, :], OP.add), [sb4["t1"], mbcB], self.B("carry"))
        T.op("act", lambda e: e.activation(self.carry[:, :], self.carry[:, :], AF.Exp), self.B("carry"), self.B("carry"))
        T.op("dve", lambda e: e.tensor_tensor(self.kws[0:L, :, :], self.km_tm[0:L, :, :], s4["ws"][0:L, :].unsqueeze(2).broadcast_to([L, 4, 128]), OP.mult),
             self.B("km_tm") + [sb4["ws"]], self.B("kws"))
        for h in range(4):
            p1, b1 = self.bank()
            T.op("pe", lambda e, h=h: e.matmul(p1[:, 0:257], self.kws[0:L, h, :], self.vm[0:L, h, :], start=True, stop=True), self.B("kws", "vm"), [b1])
            T.op("dve", lambda e, h=h: e.scalar_tensor_tensor(cn[:, h, :], cn[:, h, :], self.carry[:, h:h + 1], p1[:, 0:257], OP.mult, OP.add),
                 [cnB, b1] + self.B("carry"), [cnB])
        T.op("act", lambda e: e.activation(cnb[:, :, :], cn[:, :, :], AF.Identity), [cnB], [cnbB])
        T.op("dve", lambda e: e.tensor_copy(mbc[:, :], mnew), self.B("bc8"), [mbcB])

    def retention(self, l, L):
        T = self.T
        s4 = self.sm4
        sb4 = {k: self.bufs["s4_" + k] for k in s4}
        rs, rb = self.rs[l], self.rb[l]
        rsB, rbB = self.bufs["rs%d" % l], self.bufs["rb%d" % l]
        pt, pb = self.bank()
        for h in range(4):
            T.op("pe", lambda e, h=h: e.matmul(pt[0:L, h * L:(h + 1) * L], self.rkT[:, h, 0:L], self.rqT[:, h, 0:L], start=True, stop=True),
                 self.B("rkT", "rqT"), [pb])
        T.op("dve", lambda e: e.tensor_tensor(self.wT[0:L, :, 0:L], pt[0:L, 0:4 * L].rearrange("p (h l) -> p h l", h=4), self.DT[0:L, :, 0:L], OP.mult),
             [pb] + self.B("c32"), self.B("wT"))
        o = self.hsum[0:L, :, 0:256]
        for h in range(4):
            p1, b1 = self.bank()
            p2, b2 = self.bank()
            T.op("pe", lambda e, h=h: e.matmul(p1[0:L, 0:256], self.wT[0:L, h, 0:L], self.vr[0:L, h, :], start=True, stop=True), self.B("wT", "vr"), [b1])
            T.op("pe", lambda e, h=h: e.matmul(p2[0:L, 0:256], self.rqT[:, h, 0:L], rb[:, h, :], start=True, stop=True), [self.bufs["rqT"], rbB], [b2])
            T.op("act", lambda e, h=h: e.activation(self.inter[0:L, h, 0:256], p2[0:L, 0:256], AF.Identity, scale=self.misc[0:L, h:h + 1]), [b2] + self.B("c32"), self.B("inter"))
            T.op("dve", lambda e, h=h: e.tensor_tensor(self.hsum[0:L, h, 0:256], p1[0:L, 0:256], self.inter[0:L, h, 0:256], OP.add), [b1] + self.B("inter"), self.B("hsum"))
        mean, ss = s4["mean"][0:L, :], s4["ss"][0:L, :]
        T.op("dve", lambda e: e.tensor_reduce(mean, o, AX.X, OP.add), self.B("hsum"), [sb4["mean"]])
        T.op("dve", lambda e: e.tensor_scalar(mean, mean, 1.0 / 256, None, OP.mult), [sb4["mean"]], [sb4["mean"]])
        T.op("dve", lambda e: e.tensor_tensor(o, o, s4["mean"][0:L, :].unsqueeze(2).broadcast_to([L, 4, 256]), OP.subtract), self.B("hsum") + [sb4["mean"]], self.B("hsum"))
        T.op("dve", lambda e: e.tensor_tensor(self.sq[0:L, :, :], o, o, OP.mult), self.B("hsum"), self.B("sq"))
        T.op("dve", lambda e: e.tensor_reduce(ss, self.sq[0:L, :, :], AX.X, OP.add), self.B("sq"), [sb4["ss"]])
        T.op("act", lambda e: e.activation(ss, ss, AF.Sqrt, bias=self.epsc[0:L, 0:1], scale=1.0 / 256), [sb4["ss"], self.bufs["epsc"]], [sb4["ss"]])
        T.op("dve", lambda e: e.reciprocal(ss, ss), [sb4["ss"]], [sb4["ss"]])
        T.op("dve", lambda e: e.tensor_tensor(self.sq[0:L, :, :], o, s4["ss"][0:L, :].unsqueeze(2).broadcast_to([L, 4, 256]), OP.mult),
             self.B("hsum") + [sb4["ss"]], self.B("sq"))
        T.op("dve", lambda e: e.tensor_tensor(self.ytm[0:L, :].rearrange("p (h v) -> p h v", h=4), self.sq[0:L, :, :], self.gs_r[0:L, :, :], OP.mult),
             self.B("sq", "gs_r"), self.B("ytm"))
        self.transposes(self.ytm, self.bufs["ytm"], 8, L, self.yT["r"], self.bufs["yT_r"])
        zoff = 4 if L == 128 else 8
        doff = 12 if L == 128 else 16
        zeta = self.misc[0:L, zoff:zoff + 4].unsqueeze(2).broadcast_to([L, 4, 128])
        T.op("dve", lambda e: e.tensor_tensor(self.kws[0:L, :, :], self.rk_tm[0:L, :, :], zeta, OP.mult), self.B("rk_tm", "c32"), self.B("kws"))
        for h in range(4):
            p1, b1 = self.bank()
            T.op("pe", lambda e, h=h: e.matmul(p1[:, 0:256], self.kws[0:L, h, :], self.vr[0:L, h, :], start=True, stop=True), self.B("kws", "vr"), [b1])
            T.op("dve", lambda e, h=h: e.scalar_tensor_tensor(rs[:, h, :], rs[:, h, :], self.misc[:, doff + h:doff + h + 1], p1[:, 0:256], OP.mult, OP.add),
                 [rsB, b1] + self.B("c32"), [rsB])
        T.op("act", lambda e: e.activation(rb[:, :, :], rs[:, :, :], AF.Identity), [rsB], [rbB])

    def kv_up(self, l, TS, seqname, kt):
        T = self.T
        kn_d, kr_d, vv_d = self.kv[seqname]
        bkn, bkr, bvv = (self.bufs[n + "_" + seqname] for n in ("kn", "kr", "vv"))
        w = self.w_ukv[l]
        for half in range(2):
            view, wbuf, _ = self.load_w(w, 0, 4, CK(lambda s, half=half: s.rearrange("r (h c) -> r h c", h=8)[:, 4 * half:4 * half + 4, 0:128], ("n", half)))
            pt, pb = self.bank()
            for hh in range(4):
                for kc in range(4):
                    T.op("pe", lambda e, hh=hh, kc=kc: e.matmul(pt[:, hh * 128:hh * 128 + TS], view[:, kc, hh * 128:(hh + 1) * 128], self.ckvT[:, kc, 0:TS],
                                                                start=(kc == 0), stop=(kc == 3)), [wbuf] + self.B("ckvT"), [pb])
            self.copy(self.KnT[:, 4 * half:4 * half + 4, 0:TS], pt[:, :].rearrange("p (h t) -> p h t", h=4)[:, :, 0:TS], [pb], self.B("KnT"))
        for half in range(2):
            def hv(ps, pb, half=half):
                self.copy(self.Vst[0:TS, 4 * half:4 * half + 4, 0:128], ps.rearrange("p (h v) -> p h v", h=4), [pb], self.B("Vst"))
            self.gemm(self.ckvT, self.bufs["ckvT"], 4, TS, w,
                      CK(lambda s, half=half: s.rearrange("r (h c) -> r h c", h=8)[:, 4 * half:4 * half + 4, 128:256], ("v", half)), [(0, 512, hv)])
        T.dma("sp", kn_d[l, kt].rearrange("p (h t) -> p h t", h=8)[:, :, 0:TS], self.KnT[:, :, 0:TS], self.bufs["KnT"], self.B("KnT"), [bkn])
        T.dma("sp", kr_d[l, kt][:, 0:TS], self.krT[:, 0:TS], self.bufs["krT"], self.B("krT"), [bkr])
        T.dma("sp", vv_d[l, kt][0:TS, :], self.Vst[0:TS, :, :].rearrange("p h v -> p (h v)"), self.bufs["Vst"], self.B("Vst"), [bvv])

    def mla(self, seq, t, l, TS):
        T = self.T
        w = self.w_uq[l]
        for half in range(2):
            view, wbuf, _ = self.load_w(w, 0, 4, CK(lambda s, half=half: s.rearrange("r (h c) -> r h c", h=8)[:, 4 * half:4 * half + 4, 0:128], ("n", half)))
            pt, pb = self.bank()
            for hh in range(4):
                for kc in range(4):
                    T.op("pe", lambda e, hh=hh, kc=kc: e.matmul(pt[:, hh * 128:hh * 128 + TS], view[:, kc, hh * 128:(hh + 1) * 128], self.hqT[:, kc, 0:TS],
                                                                start=(kc == 0), stop=(kc == 3)), [wbuf] + self.B("hqT"), [pb])
            self.copy(self.QnT[:, 4 * half:4 * half + 4, 0:TS], pt[:, :].rearrange("p (h t) -> p h t", h=4)[:, :, 0:TS], [pb], self.B("QnT"))
        tA, tAb = self.tmpA, self.bufs["tmpA"]

        def hqr(ps, pb):
            self.copy(tA[0:TS, 0:512], ps, [pb], [tAb])
            self.rope(tA[:, 0:512].rearrange("p (h d) -> p h d", h=8), tAb, TS, 8, 64, self.rt64, self.bufs["rt64"],
                      self.hb[:, 0:512].rearrange("p (h d) -> p h d", h=8), self.bufs["hb"])
            self.transposes(self.hb, self.bufs["hb"], 4, TS, self.QrT, self.bufs["QrT"])
        self.gemm(self.hqT, self.bufs["hqT"], 4, TS, w, CK(lambda s: s.rearrange("r (h c) -> r h c", h=8)[:, :, 128:192], ("r",)), [(0, 512, hqr)])
        kt_own = seq.kt0 + t
        kn_d, kr_d, vv_d = self.kv[seq.name]
        bkn, bkr, bvv = (self.bufs[n + "_" + seq.name] for n in ("kn", "kr", "vv"))

        def load_kv(kt, KS):
            ks, rs_, vs, sbuf = self.kvs[self.kvi % 2]
            self.kvi += 1
            T.dma("sp", ks[:, :, 0:KS], kn_d[l, kt].rearrange("p (h t) -> p h t", h=8)[:, :, 0:KS], sbuf, [bkn], [sbuf])
            T.dma("sp", rs_[:, 0:KS], kr_d[l, kt][:, 0:KS], sbuf, [bkr], [sbuf])
            T.dma("sp", vs[0:KS, :, :].rearrange("p h v -> p (h v)"), vv_d[l, kt][0:KS, :], sbuf, [bvv], [sbuf])
            return ks, rs_, vs, sbuf
        pre = {}
        for kt in range(min(2, kt_own)):
            pre[kt] = load_kv(kt, 128)
        self.kv_up(l, TS, seq.name, kt_own)
        kn_d, kr_d, vv_d = self.kv[seq.name]
        bkn, bkr, bvv = (self.bufs[n + "_" + seq.name] for n in ("kn", "kr", "vv"))
        nkt = kt_own + 1
        scale = 192 ** -0.5
        self.attn_active = True
        accs = [self.ps[5], self.ps[6], self.ps[7]]
        hb_of = lambda h: (accs[h // 3], (h % 3) * 129)
        PTs = [(self.PT, self.bufs["PT"]), (self.PT2, self.bufs["PT2"])]

        def pv(kt, KS, PTt, PTb, vs, sbuf):
            for h in range(8):
                (at, ab), co = hb_of(h)
                T.op("pe", lambda e, h=h, at=at, co=co: e.matmul(at[0:TS, co:co + 129], PTt[0:KS, h, 0:TS], vs[0:KS, h, :], start=(kt == 0 and h % 3 == 0), stop=(kt == nkt - 1)),
                     [sbuf, PTb], [ab])
        prev = None
        for kt in range(nkt):
            KS = TS if kt == kt_own else 128
            ks, rs_, vs, sbuf = pre[kt] if kt in pre else load_kv(kt, KS)
            PTt, PTb = PTs[kt % 2]
            for half in range(2):
                pt, pb = self.bank()
                for hh in range(4):
                    h = 4 * half + hh
                    po = (h % 2) * 64
                    T.op("pe", lambda e, h=h, hh=hh: e.matmul(pt[0:KS, hh * 128:hh * 128 + TS], ks[:, h, 0:KS], self.QnT[:, h, 0:TS], start=True, stop=False),
                         [sbuf] + self.B("QnT"), [pb])
                    T.op("pe", lambda e, h=h, hh=hh, po=po: e.matmul(pt[0:KS, hh * 128:hh * 128 + TS], rs_[po:po + 64, 0:KS], self.QrT[po:po + 64, h // 2, 0:TS],
                                                                       start=False, stop=True), [sbuf] + self.B("QrT"), [pb])
                T.op("act", lambda e, half=half: e.activation(PTt[0:KS, 4 * half:4 * half + 4, 0:TS],
                                                               pt[0:KS, :].rearrange("p (h t) -> p h t", h=4)[:, :, 0:TS], AF.Exp, scale=scale),
                     [pb], [PTb])
            if seq.causal and kt == kt_own and TS == 128:
                T.op("dve", lambda e: e.memset(PTt[64:128, :, 0:64], 0.0), [], [PTb])
            if prev is not None:
                pv(*prev)
            prev = (kt, KS, PTt, PTb, vs, sbuf)
        pv(*prev)
        for i, (at, ab) in enumerate(accs):
            nh = 3 if i < 2 else 2
            v3 = at[0:TS, 0:nh * 129].rearrange("p (h v) -> p h v", h=nh)
            rd = self.sm8[0:TS, 0:nh]
            T.op("dve", lambda e, v3=v3, rd=rd: e.reciprocal(rd, v3[:, :, 128]), [ab], self.B("sm8"))
            T.op("dve", lambda e, v3=v3, rd=rd, i=i, nh=nh: e.tensor_tensor(
                self.ytm[0:TS, i * 384:i * 384 + nh * 128].rearrange("p (h v) -> p h v", h=nh), v3[:, :, 0:128],
                rd.unsqueeze(2).broadcast_to([TS, nh, 128]), OP.mult), [ab] + self.B("sm8"), self.B("ytm"))
        self.transposes(self.ytm, self.bufs["ytm"], 8, TS, self.yT["a"], self.bufs["yT_a"])
        self.attn_active = False

    def merge_out(self, l, TS):
        T = self.T
        tA, tAb = self.tmpA, self.bufs["tmpA"]
        mg = self.gbuf
        for cb in range(4):
            for bi, (br, goff, wup) in enumerate((("m", C_GM, self.w_up_m), ("a", C_GA, self.w_up_a), ("r", C_GR, self.w_up_r))):
                hold = {}

                def hg(ps, pb):
                    T.op("act", lambda e: e.activation(self.sg[0:TS, :], ps, AF.Sigmoid), [pb], self.B("sg"))
                self.gemm(self.hT, self.bufs["hT"], 16, TS, self.w_in[l], (goff + cb * 512, 512), [(0, 512, hg)], bias_off=l * DIN + goff + cb * 512)

                def hu(ps, pb, bi=bi, cb=cb):
                    dst = tA[0:TS, cb * 512:(cb + 1) * 512]
                    if bi == 0:
                        T.op("dve", lambda e: e.tensor_tensor(dst, ps, self.sg[0:TS, :], OP.mult), [pb] + self.B("sg"), [tAb])
                    else:
                        T.op("dve", lambda e: e.tensor_tensor(self.tmpB[0:TS, :], ps, self.sg[0:TS, :], OP.mult), [pb] + self.B("sg"), self.B("tmpB"))
                        T.op("dve", lambda e: e.tensor_tensor(dst, dst, self.tmpB[0:TS, :], OP.add), [tAb] + self.B("tmpB"), [tAb])
                self.gemm(self.yT[br], self.bufs["yT_" + br], 8, TS, wup[l], (cb * 512, 512), [(0, 512, hu)])
        T.op("act", lambda e: e.activation(self.hb[0:TS, :], tA[0:TS, :], AF.Identity), [tAb], self.B("hb"))
        self.transposes(self.hb, self.bufs["hb"], 16, TS, self.hT, self.bufs["hT"])
        for cb in range(4):
            def ho(ps, pb, cb=cb):
                self.copy(tA[0:TS, cb * 512:(cb + 1) * 512], ps, [pb], [tAb])
            self.gemm(self.hT, self.bufs["hT"], 16, TS, self.w_o[l], (cb * 512, 512), [(0, 512, ho)])
        self.resid_add(l, TS, self.g_mix_post[l])

    def ffn(self, l, TS):
        T = self.T
        tA, tAb = self.tmpA, self.bufs["tmpA"]
        self.norm_to_hT(l, TS, self.g_ffn_pre[l])
        for j in range(11):
            def hg(ps, pb):
                T.op("act", lambda e: e.activation(self.sg[0:TS, :], ps, AF.Silu), [pb], self.B("sg"))
            self.gemm(self.hT, self.bufs["hT"], 16, TS, self.w_gu[l], (DFF + j * 512, 512), [(0, 512, hg)])

            def ha(ps, pb, j=j):
                jj = j if j < 6 else j - 6
                T.op("dve", lambda e: e.tensor_tensor(self.act_tm[0:TS, jj * 512:(jj + 1) * 512], ps, self.sg[0:TS, :], OP.mult), [pb] + self.B("sg"), self.B("act_tm"))
            self.gemm(self.hT, self.bufs["hT"], 16, TS, self.w_gu[l], (j * 512, 512), [(0, 512, ha)])
            if j == 5:
                self.transposes(self.act_tm, self.bufs["act_tm"], 24, TS, self.actT, self.bufs["actT"])
            if j == 10:
                self.transposes(self.act_tm, self.bufs["act_tm"], 20, TS, self.actT, self.bufs["actT"], dst_k0=24)
        for cb in range(4):
            def ho(ps, pb, cb=cb):
                self.copy(tA[0:TS, cb * 512:(cb + 1) * 512], ps, [pb], [tAb])
            self.gemm(self.actT, self.bufs["actT"], 44, TS, self.w_down[l], (cb * 512, 512), [(0, 512, ho)])
        self.resid_add(l, TS, self.g_ffn_post[l])

    def merge2(self, l, TS):
        T = self.T
        Bf = self.bufs
        sets = []
        for i, (tA, tAb, sg, sgb) in enumerate(((self.tmpA, Bf["tmpA"], self.sg, Bf["sg"]), (self.tmpA_B, Bf["tmpA_B"], self.tmpB, Bf["tmpB"]))):
            tl = self.tiles_[i]
            sets.append(dict(hT=tl["hT"][0], hTb=tl["hT"][1], yT=tl["yT"], tA=tA, tAb=tAb, sg=sg, sgb=sgb))
        tmp = self.sqf[:, :, :].rearrange("p a b -> p (a b)")[:, 0:512]
        tmpb = Bf["sq"]
        xs = [(st["hT"], st["hTb"]) for st in sets]
        self.defer_on = True
        for cb in range(4):
            for bi, (br, goff, wup) in enumerate((("m", C_GM, self.w_up_m), ("a", C_GA, self.w_up_a), ("r", C_GR, self.w_up_r))):
                def hg(ps, pb, xi):
                    st = sets[xi]
                    T.op("act", lambda e: e.activation(st["sg"][0:TS, :], ps, AF.Sigmoid), [pb], [st["sgb"]])
                self.gemm(xs, None, 16, TS, self.w_in[l], (goff + cb * 512, 512), [(0, 512, hg)], bias_off=l * DIN + goff + cb * 512)

                def hu(ps, pb, xi, bi=bi, cb=cb):
                    st = sets[xi]
                    dst = st["tA"][0:TS, cb * 512:(cb + 1) * 512]
                    if bi == 0:
                        T.op("dve", lambda e: e.tensor_tensor(dst, ps, st["sg"][0:TS, :], OP.mult), [pb, st["sgb"]], [st["tAb"]])
                    else:
                        T.op("dve", lambda e: e.tensor_tensor(tmp[0:TS, :], ps, st["sg"][0:TS, :], OP.mult), [pb, st["sgb"]], [tmpb])
                        T.op("dve", lambda e: e.tensor_tensor(dst, dst, tmp[0:TS, :], OP.add), [st["tAb"], tmpb], [st["tAb"]])
                self.gemm([(st["yT"][br][0], st["yT"][br][1]) for st in sets], None, 8, TS, wup[l], (cb * 512, 512), [(0, 512, hu)])
        self.flush()
        for st in sets:
            T.op("act", lambda e, st=st: e.activation(self.hb[0:TS, :], st["tA"][0:TS, :], AF.Identity), [st["tAb"]], self.B("hb"))
            self.transposes(self.hb, Bf["hb"], 16, TS, st["hT"], st["hTb"])
        for cb in range(4):
            def ho(ps, pb, xi, cb=cb):
                st = sets[xi]
                self.copy(st["tA"][0:TS, cb * 512:(cb + 1) * 512], ps, [pb], [st["tAb"]])
            self.gemm(xs, None, 16, TS, self.w_o[l], (cb * 512, 512), [(0, 512, ho)])
        self.flush()
        self.defer_on = False
        for i in range(2):
            self.set_x(i)
            self.resid_add(l, TS, self.g_mix_post[l], src=(sets[i]["tA"], sets[i]["tAb"]))

    def ffn2(self, l, TS):
        T = self.T
        Bf = self.bufs
        sets = [
            dict(hT=self.hT, hTb=Bf["hT"], act=self.act_tm, actb=Bf["act_tm"], actT=self.actT, actTb=Bf["actT"], tA=self.tmpA, tAb=Bf["tmpA"], sg=self.sg, sgb=Bf["sg"]),
            dict(hT=self.hT_B, hTb=Bf["hT_B"], act=self.act_tm_B, actb=Bf["act_tm_B"], actT=self.actT_B, actTb=Bf["actT_B"], tA=self.tmpA_B, tAb=Bf["tmpA_B"], sg=self.tmpB, sgb=Bf["tmpB"]),
        ]
        for i in range(2):
            self.set_x(i)
            self.norm_to_hT(l, TS, self.g_ffn_pre[l], dst=(sets[i]["hT"], sets[i]["hTb"]))
        xs = [(st["hT"], st["hTb"]) for st in sets]
        self.defer_on = True
        for j in range(11):
            def hg(ps, pb, xi):
                st = sets[xi]
                T.op("act", lambda e: e.activation(st["sg"][0:TS, :], ps, AF.Silu), [pb], [st["sgb"]])
            self.gemm(xs, None, 16, TS, self.w_gu[l], (DFF + j * 512, 512), [(0, 512, hg)])

            def ha(ps, pb, xi, j=j):
                st = sets[xi]
                jj = j if j < 6 else j - 6
                T.op("dve", lambda e: e.tensor_tensor(st["act"][0:TS, jj * 512:(jj + 1) * 512], ps, st["sg"][0:TS, :], OP.mult), [pb, st["sgb"]], [st["actb"]])
            self.gemm(xs, None, 16, TS, self.w_gu[l], (j * 512, 512), [(0, 512, ha)])
            if j == 5:
                self.flush()
                for st in sets:
                    self.transposes(st["act"], st["actb"], 24, TS, st["actT"], st["actTb"])
            if j == 10:
                self.flush()
                for st in sets:
                    self.transposes(st["act"], st["actb"], 20, TS, st["actT"], st["actTb"], dst_k0=24)
        xs2 = [(st["actT"], st["actTb"]) for st in sets]
        for cb in range(4):
            def ho(ps, pb, xi, cb=cb):
                st = sets[xi]
                self.copy(st["tA"][0:TS, cb * 512:(cb + 1) * 512], ps, [pb], [st["tAb"]])
            self.gemm(xs2, None, 44, TS, self.w_down[l], (cb * 512, 512), [(0, 512, ho)])
        self.flush()
        self.defer_on = False
        for i in range(2):
            self.set_x(i)
            self.resid_add(l, TS, self.g_ffn_post[l], src=(sets[i]["tA"], sets[i]["tAb"]))

    def init_states(self, seq):
        T = self.T
        for l in range(self.nl):
            cnB, cnbB, mbcB = self.bufs["cn%d" % l], self.bufs["cnb%d" % l], self.bufs["mbc%d" % l]
            rsB, rbB = self.bufs["rs%d" % l], self.bufs["rb%d" % l]
            if seq.name == "p":
                T.op("dve", lambda e, l=l: e.memset(self.cn[l][:, :, :], 0.0), [], [cnB])
                T.op("dve", lambda e, l=l: e.memset(self.cnb[l][:, :, :], 0.0), [], [cnbB])
                T.op("dve", lambda e, l=l: e.memset(self.mbc[l][:, :], 0.0), [], [mbcB])
                T.op("dve", lambda e, l=l: e.memset(self.rs[l][:, :, :], 0.0), [], [rsB])
                T.op("dve", lambda e, l=l: e.memset(self.rb[l][:, :, :], 0.0), [], [rbB])
            else:
                for h in range(4):
                    T.dma("sp", self.stg[:, :, :], self.sc[l, h].rearrange("(a p) k -> p a k", p=128), self.bufs["stg"], [], self.B("stg"))
                    pt, pb = self.bank()
                    for a_ in range(2):
                        T.op("pe", lambda e, a_=a_: e.transpose(pt[:, a_ * 128:(a_ + 1) * 128], self.stg[:, a_, :], self.ident), self.B("stg", "c32"), [pb])
                    T.op("dve", lambda e, l=l, h=h: e.tensor_copy(self.cn[l][:, h, 0:256], pt[:, 0:256]), [pb], [cnB])
                T.dma("sp", self.cn[l][:, :, 256], self.sn[l].rearrange("h k -> k h"), cnB, [], [cnB], allow_slow_non_contiguous=True)
                T.dma("sp", self.mbc[l][:, :], self.sm[l].partition_broadcast(128), mbcB, [], [mbcB])
                T.dma("sp", self.rs[l][:, :, :], self.sr[l].rearrange("h k v -> k h v"), rsB, [], [rsB])
                T.op("act", lambda e, l=l: e.activation(self.cnb[l][:, :, :], self.cn[l][:, :, :], AF.Identity), [cnB], [cnbB])
                T.op("act", lambda e, l=l: e.activation(self.rb[l][:, :, :], self.rs[l][:, :, :], AF.Identity), [rsB], [rbB])

    def store_states(self, seq):
        T = self.T
        oc, on, om, orr = seq.o_states
        for l in range(self.nl):
            cnB, mbcB, rsB = self.bufs["cn%d" % l], self.bufs["mbc%d" % l], self.bufs["rs%d" % l]
            for h in range(4):
                pt, pb = self.bank()
                for a_ in range(2):
                    T.op("pe", lambda e, a_=a_, l=l, h=h: e.transpose(pt[:, a_ * 128:(a_ + 1) * 128], self.cn[l][:, h, a_ * 128:(a_ + 1) * 128], self.ident), [cnB] + self.B("c32"), [pb])
                T.op("dve", lambda e: e.tensor_copy(self.stg[:, :, :].rearrange("p a k -> p (a k)"), pt[:, 0:256]), [pb], self.B("stg"))
                T.dma("sp", oc[l, h].rearrange("(a p) k -> p a k", p=128), self.stg[:, :, :], self.bufs["stg"], self.B("stg"), [])
            T.dma("sp", on[l].rearrange("h k -> k h"), self.cn[l][:, :, 256], cnB, [cnB], [], allow_slow_non_contiguous=True)
            T.dma("sp", om[l:l + 1, :], self.mbc[l][0:1, :], mbcB, [mbcB], [])
            T.dma("sp", orr[l].rearrange("h k v -> k h v"), self.rs[l][:, :, :], rsB, [rsB], [])

    def past_kv(self, seq):
        T = self.T
        for l in range(self.nl):
            for kt in range(8):
                T.dma("sp", self.ckv_o[:, :], self.cckv[l, kt * 128:(kt + 1) * 128, :], self.bufs["ckv_o"], [], self.B("ckv_o"))
                T.op("act", lambda e: e.activation(self.hb[:, 0:512], self.ckv_o[:, :], AF.Identity), self.B("ckv_o"), self.B("hb"))
                self.transposes(self.hb, self.bufs["hb"], 4, 128, self.ckvT, self.bufs["ckvT"])
                T.dma("sp", self.kr_o[:, :], self.ckr[l, kt * 128:(kt + 1) * 128, :], self.bufs["kr_o"], [], self.B("kr_o"))
                T.op("act", lambda e: e.activation(self.krd[:, 0:64], self.kr_o[:, :], AF.Identity), self.B("kr_o"), self.B("krd"))
                T.op("act", lambda e: e.activation(self.krd[:, 64:128], self.kr_o[:, :], AF.Identity), self.B("kr_o"), self.B("krd"))
                self.transposes(self.krd, self.bufs["krd"], 1, 128, self.krT[:, :].rearrange("p (o t) -> p o t", o=1), self.bufs["krT"])
                self.kv_up(l, 128, "s", kt)

    def mixer_core(self, seq, t, l, TS):
        self.norm_to_hT(l, TS, self.g_mix_pre[l])
        self.stage_in(seq, t, l, TS)
        self.mlstm(l, TS)
        self.retention(l, TS)
        self.mla(seq, t, l, TS)

    def run_seq(self, seq):
        T = self.T
        self.init_states(seq)
        self.set_tile(1)
        if seq.name == "s":
            self.past_kv(seq)
        TS = seq.TS
        t = 0
        while t < seq.ntiles:
            pair = (TS == 128 and t + 1 < seq.ntiles)
            tiles = [t, t + 1] if pair else [t]
            for i, tt in enumerate(tiles):
                self.set_x(i)
                T.dma("sp", self.x[0:TS, :], seq.x_in[tt * 128:tt * 128 + TS, :], self.bufs["x"], [], self.B("x"))
            for l in range(self.nl):
                self.load_layer_consts(l)
                if pair:
                    for i, tt in enumerate(tiles):
                        self.set_x(i)
                        self.set_tile(i)
                        self.mixer_core(seq, tt, l, TS)
                    self.set_tile(1)
                    self.barrier()
                    self.merge2(l, TS)
                    self.barrier()
                    self.ffn2(l, TS)
                    self.barrier()
                else:
                    self.set_x(0)
                    self.set_tile(1)
                    self.mixer_core(seq, t, l, TS)
                    self.merge_out(l, TS)
                    self.ffn(l, TS)
            for i, tt in enumerate(tiles):
                self.set_x(i)
                T.dma("sp", seq.y_out[tt * 128:tt * 128 + TS, :], self.x[0:TS, :], self.bufs["x"], self.B("x"), [])
            t += len(tiles)
        self.set_x(0)
        self.set_tile(1)
        self.store_states(seq)

    def build(self):
        self.prologue()
        seqs = []
        p = Seq()
        p.name, p.ntiles, p.TS, p.tab0, p.kt0, p.causal = "p", self.S_P // 128, 128, 0, 0, True
        p.x_in, p.y_out, p.o_ckv, p.o_kr = self.xp, self.yp, self.o_pckv, self.o_pkr
        p.o_states = (self.o_pc, self.o_pn, self.o_pm, self.o_pr)
        seqs.append(p)
        if self.do_sample:
            s = Seq()
            s.name, s.ntiles, s.TS, s.tab0, s.kt0, s.causal = "s", 1, 64, self.S_P, 8, False
            s.x_in, s.y_out, s.o_ckv, s.o_kr = self.xs, self.ys, self.o_sckv, self.o_skr
            s.o_states = (self.o_sc, self.o_sn, self.o_sm, self.o_sr)
            seqs.append(s)
        for s in seqs:
            self.run_seq(s)
        self.T.wait_all("sp", list(Buf.REG))
        return self.nc


def const_tables(S_P):
    pos = np.concatenate([np.arange(S_P), PAST + np.arange(SS)]).astype(np.float32)

    def tabs(d):
        inv = (1.0 / (10000.0 ** (np.arange(0, d, 2, dtype=np.float32) / np.float32(d)))).astype(np.float32)
        ang = pos[:, None] * inv[None, :]
        return np.cos(ang).astype(np.float32), np.sin(ang).astype(np.float32)
    c128, s128 = tabs(128)
    c64, s64 = tabs(64)
    i = np.arange(128)
    ident = np.eye(128, dtype=np.float32)
    tri = (i[:, None] <= i[None, :]).astype(np.float32)
    negmask = np.where(i[None, :] <= i[:, None], 0.0, NEG).astype(np.float32)
    negmaskT = negmask.T.copy()
    sel = np.zeros((128, 128), np.float32)
    sel[127, :] = 1.0
    sel[63, :] = 1.0
    lg = np.log1p(-np.exp2(-5.0 - np.arange(4, dtype=np.float64)))
    diff = (i[None, :] - i[:, None]).astype(np.float64)
    DT = np.stack([np.where(diff >= 0, np.exp(np.maximum(diff, 0) * lg[h]), 0.0) for h in range(4)], 1)
    xi = np.exp((i[:, None] + 1.0) * lg[None, :])
    z128 = np.exp((127.0 - i)[:, None] * lg[None, :])
    z64 = np.exp((63.0 - i)[:, None] * lg[None, :])
    d128 = np.broadcast_to(np.exp(128 * lg)[None, :], (128, 4))
    d64 = np.broadcast_to(np.exp(64 * lg)[None, :], (128, 4))
    sel128 = np.zeros((128, 128), np.float32)
    sel128[127, :] = 1.0
    sel64 = np.zeros((128, 128), np.float32)
    sel64[63, :] = 1.0
    c32 = np.concatenate([ident, tri, negmask, negmaskT, sel128, DT.reshape(128, 512), xi, z128, z64, d128, d64, sel64], 1).astype(np.float32)
    return c128, s128, c64, s64, c32


_CACHE = {}


def get_nc(S_P, do_sample=True):
    key = (S_P, do_sample)
    if key not in _CACHE:
        b = Builder(S_P, do_sample)
        _CACHE[key] = b.build()
    return _CACHE[key]


W_NAMES = ["g_mix_pre", "w_in", "b_in", "g_mlstm", "w_up_m", "g_qa", "w_uq", "g_kva", "w_ukv", "w_up_a", "g_ret",
           "w_up_r", "w_o", "g_mix_post", "g_ffn_pre", "w_gu", "w_down", "g_ffn_post"]


def kernel(**inp):
    x_prompt = np.asarray(inp["x_prompt"], np.float32)
    Bn, S_P, _ = x_prompt.shape
    nc = get_nc(S_P)
    c128, s128, c64, s64, c32 = const_tables(S_P)
    shared = {k: np.ascontiguousarray(np.asarray(inp[k], np.float32)) for k in W_NAMES}
    shared.update(t_cos128=c128, t_sin128=s128, t_cos64=c64, t_sin64=s64, t_c32=c32)
    in_maps = []
    for b in range(Bn):
        m = dict(shared)
        m["xp"] = np.ascontiguousarray(x_prompt[b])
        m["xs"] = np.ascontiguousarray(np.asarray(inp["x_sample"], np.float32)[b])
        m["cckv"] = np.ascontiguousarray(np.asarray(inp["cache_mla_ckv"], np.float32)[:, b])
        m["ckr"] = np.ascontiguousarray(np.asarray(inp["cache_mla_krope"], np.float32)[:, b])
        m["sc"] = np.ascontiguousarray(np.asarray(inp["state_mlstm_c"], np.float32)[:, b])
        m["sn"] = np.ascontiguousarray(np.asarray(inp["state_mlstm_n"], np.float32)[:, b])
        m["sm"] = np.ascontiguousarray(np.asarray(inp["state_mlstm_m"], np.float32)[:, b])
        m["sr"] = np.ascontiguousarray(np.asarray(inp["state_ret"], np.float32)[:, b])
        in_maps.append(m)
    res = run_bass_kernel_spmd(nc, in_maps, core_ids=list(range(Bn)))
    R = res.results

    def st(k, axis):
        return np.stack([np.asarray(r[k], np.float32) for r in R], axis)
    return (st("yp", 0), st("ys", 0), st("o_pckv", 1), st("o_pkr", 1), st("o_pc", 1), st("o_pn", 1), st("o_pm", 1), st("o_pr", 1),
            st("o_sckv", 1), st("o_skr", 1), st("o_sc", 1), st("o_sn", 1), st("o_sm", 1), st("o_sr", 1))
```

```python
import numpy as np
import concourse.bass as bass
import concourse.mybir as mybir
from concourse.bass_utils import run_bass_kernel_spmd

F32 = mybir.dt.float32
BF16 = mybir.dt.bfloat16
AF = mybir.ActivationFunctionType
OP = mybir.AluOpType
AX = mybir.AxisListType

D = 2048
DIN = 13384
DFF = 5632
NL = 2
PAST = 1024
SS = 64
EPS = 1e-6
NEG = -1.0e30
C_MQ, C_MK, C_MV, C_MO, C_MI, C_ADQ, C_ADKV, C_AKR, C_RQ, C_RK, C_RV, C_RG, C_GM, C_GA, C_GR = (
    0, 512, 1024, 2048, 3072, 3080, 3592, 4104, 4168, 4680, 5192, 6216, 7240, 9288, 11336)
SLOT_ELEMS = 4352
NWG = 700
NSLOT = 4
PIPE_DEPTH = 1
ARENA_E = 29664


class Buf:
    __slots__ = ("name", "lw", "rd", "dsem", "dcnt", "lw_dma")

    REG = []

    def __init__(self, name):
        Buf.REG.append(self)
        self.name = name
        self.lw = None
        self.rd = {}
        self.dsem = None
        self.dcnt = 0
        self.lw_dma = False


class Tracker:
    def __init__(self, nc):
        self.nc = nc
        self.eng = {"pe": nc.tensor, "act": nc.scalar, "dve": nc.vector, "pool": nc.gpsimd, "sp": nc.sync}
        self.sem = {k: nc.alloc_semaphore("e_" + k) for k in ("pe", "act", "dve", "pool")}
        self.cnt = {k: 0 for k in self.sem}
        self.waited = {k: {} for k in self.eng}
        self.nsem = 4

    def _need(self, q, reads, writes, is_dma):
        need = {}

        def add(sv):
            if sv is None:
                return
            s, v = sv
            k = id(s)
            if k not in need or need[k][1] < v:
                need[k] = (s, v)

        for b in reads:
            add(b.lw)
        for b in writes:
            if not (is_dma and b.lw_dma and not b.rd):
                add(b.lw)
            for sv in b.rd.values():
                add(sv)
        E = self.eng[q]
        wd = self.waited[q]
        for k, (s, v) in need.items():
            if q == "pe" and s is self.sem["pe"]:
                continue
            if wd.get(k, 0) >= v:
                continue
            E.wait_ge(s, v)
            wd[k] = v

    def op(self, q, fn, reads=(), writes=()):
        self._need(q, reads, writes, False)
        ins = fn(self.eng[q])
        self.cnt[q] += 1
        s = self.sem[q]
        ins.then_inc(s, 1)
        sv = (s, self.cnt[q])
        for b in reads:
            b.rd[id(s)] = sv
        for b in writes:
            b.lw = sv
            b.rd = {}
            b.lw_dma = False
        return ins

    def dma(self, q, out, in_, sb, reads=(), writes=(), **kw):
        self._need(q, reads, writes, True)
        if sb.dsem is None:
            sb.dsem = self.nc.alloc_semaphore("d_" + sb.name)
            self.nsem += 1
        ins = self.eng[q].dma_start(out=out, in_=in_, **kw)
        sb.dcnt += 16
        ins.then_inc(sb.dsem, 16)
        sv = (sb.dsem, sb.dcnt)
        for b in reads:
            b.rd[id(sb.dsem)] = sv
        for b in writes:
            keep = b.lw_dma and not b.rd
            b.lw = sv
            if not keep:
                b.rd = {}
            b.lw_dma = True
        return ins

    def wait_all(self, q, bufs):
        self._need(q, bufs, bufs, False)


class Seq:
    pass


def CK(fn, key):
    fn.ckey = key
    return fn


class Builder:
    def __init__(self, S_P, do_sample=True, nl=NL):
        self.S_P = S_P
        self.do_sample = do_sample
        self.nl = nl
        Buf.REG = []
        nc = self.nc = bass.Bass("TRN2", target_bir_lowering=False)
        self.T = Tracker(nc)
        self.bufs = {}
        self.rr = 0
        self._decl()
        self._alloc()

    def din(self, name, shape, dt=F32):
        return self.nc.dram_tensor(name, list(shape), dt, kind="ExternalInput").ap()

    def dout(self, name, shape):
        return self.nc.dram_tensor(name, list(shape), F32, kind="ExternalOutput").ap()

    def _decl(self):
        S_P = self.S_P
        nc = self.nc
        self.xp = self.din("xp", [S_P, D])
        self.xs = self.din("xs", [SS, D])
        self.cckv = self.din("cckv", [NL, PAST, 512])
        self.ckr = self.din("ckr", [NL, PAST, 64])
        self.sc = self.din("sc", [NL, 4, 256, 128])
        self.sn = self.din("sn", [NL, 4, 128])
        self.sm = self.din("sm", [NL, 4])
        self.sr = self.din("sr", [NL, 4, 128, 256])
        self.g_mix_pre = self.din("g_mix_pre", [NL, D])
        self.w_in = self.din("w_in", [NL, D, DIN])
        self.b_in = self.din("b_in", [NL, DIN])
        self.g_mlstm = self.din("g_mlstm", [NL, 1024])
        self.w_up_m = self.din("w_up_m", [NL, 1024, D])
        self.g_qa = self.din("g_qa", [NL, 512])
        self.w_uq = self.din("w_uq", [NL, 512, 1536])
        self.g_kva = self.din("g_kva", [NL, 512])
        self.w_ukv = self.din("w_ukv", [NL, 512, 2048])
        self.w_up_a = self.din("w_up_a", [NL, 1024, D])
        self.g_ret = self.din("g_ret", [NL, 1024])
        self.w_up_r = self.din("w_up_r", [NL, 1024, D])
        self.w_o = self.din("w_o", [NL, D, D])
        self.g_mix_post = self.din("g_mix_post", [NL, D])
        self.g_ffn_pre = self.din("g_ffn_pre", [NL, D])
        self.w_gu = self.din("w_gu", [NL, D, 2 * DFF])
        self.w_down = self.din("w_down", [NL, DFF, D])
        self.g_ffn_post = self.din("g_ffn_post", [NL, D])
        NP = S_P + SS
        self.t_cos128 = self.din("t_cos128", [NP, 64])
        self.t_sin128 = self.din("t_sin128", [NP, 64])
        self.t_cos64 = self.din("t_cos64", [NP, 32])
        self.t_sin64 = self.din("t_sin64", [NP, 32])
        self.t_c32 = self.din("t_c32", [128, 1300])
        self.yp = self.dout("yp", [S_P, D])
        self.ys = self.dout("ys", [SS, D])
        self.o_pckv = self.dout("o_pckv", [NL, S_P, 512])
        self.o_pkr = self.dout("o_pkr", [NL, S_P, 64])
        self.o_pc = self.dout("o_pc", [NL, 4, 256, 128])
        self.o_pn = self.dout("o_pn", [NL, 4, 128])
        self.o_pm = self.dout("o_pm", [NL, 4])
        self.o_pr = self.dout("o_pr", [NL, 4, 128, 256])
        self.o_sckv = self.dout("o_sckv", [NL, SS, 512])
        self.o_skr = self.dout("o_skr", [NL, SS, 64])
        self.o_sc = self.dout("o_sc", [NL, 4, 256, 128])
        self.o_sn = self.dout("o_sn", [NL, 4, 128])
        self.o_sm = self.dout("o_sm", [NL, 4])
        self.o_sr = self.dout("o_sr", [NL, 4, 128, 256])
        self.bhl = nc.dram_tensor("bhl", [2, NL * DIN], BF16).ap()
        self.wscr_t = [nc.dram_tensor("wscr%d" % i, [100, 128, SLOT_ELEMS], BF16).ap() for i in range((NWG + 99) // 100)]
        self.wcache = {}
        nktp = S_P // 128
        self.kv = {}
        for nm, nkt in (("p", nktp), ("s", 9)):
            self.kv[nm] = (
                nc.dram_tensor("kn_" + nm, [NL, nkt, 128, 1024], BF16).ap(),
                nc.dram_tensor("kr_" + nm, [NL, nkt, 128, 128], BF16).ap(),
                nc.dram_tensor("vv_" + nm, [NL, nkt, 128, 8 * 129], BF16).ap(),
            )

    def sb(self, name, shape, dt=F32):
        t = self.nc.alloc_sbuf_tensor(name, list(shape), dt)
        self.bufs[name] = Buf(name)
        return t

    def B(self, *names):
        return [self.bufs[n] for n in names]

    def _alloc(self):
        nc = self.nc
        sb = self.sb
        self.arena = nc.alloc_sbuf_tensor("arena", [128, ARENA_E], BF16)
        self._ao = 0

        def carve(name, shape, dt, mkbuf=True):
            n = 1
            for d_ in shape[1:]:
                n *= d_
            nb = n * (4 if dt == F32 else 2)
            o = self._ao
            self._ao += (nb + 31) // 32 * 32
            assert self._ao <= ARENA_E * 2, (name, self._ao)
            v = self.arena[:, o // 2:o // 2 + nb // 2]
            if dt == F32:
                v = v.bitcast(F32)
            if len(shape) == 3:
                v = v.rearrange("p (a b) -> p a b", a=shape[1])
            if mkbuf:
                self.bufs[name] = Buf(name)
            return v
        self.carve = carve
        self.c32 = sb("c32", [128, 1300])
        self.identb = sb("identb", [128, 128], BF16)
        self.ones2 = sb("ones2", [2, 128], BF16)
        self.onesf = sb("onesf", [128, 128])
        self.epsc = sb("epsc", [128, 1])
        self.x = sb("x", [128, D])
        self.gbuf = sb("gbuf", [128, D])
        self.gsm = sb("gsm", [128, 3, 1024])
        self.tmpA = sb("tmpA", [128, D])
        self.hb = sb("hb", [128, D], BF16)
        self.hT = sb("hT", [128, 16, 128], BF16)
        self.wslot = [sb("ws%d" % i, [128, SLOT_ELEMS], BF16) for i in range(NSLOT)]
        self.wbias = [nc.alloc_sbuf_tensor("wb%d" % i, [2, 520], BF16) for i in range(NSLOT)]
        self.wsi = 0
        self.rt128 = sb("rt128", [128, 2, 64])
        self.rt64 = sb("rt64", [128, 2, 32])
        self.qmT = None
        self.kmT = None
        self.km_tm = None
        self.vm = sb("vm", [128, 4, 257], BF16)
        self.gs_m = carve("gs_m", [128, 4, 256], F32)
        self.igf = sb("igf", [128, 8])
        self.hqT = None
        self.ckvT = None
        self.ckv_o = sb("ckv_o", [128, 512])
        self.kr_o = sb("kr_o", [128, 64])
        self.krd = sb("krd", [128, 128], BF16)
        self.krT = sb("krT", [128, 128], BF16)
        self.rqT = None
        self.rkT = None
        self.rk_tm = None
        self.vr = None
        self.gs_r = carve("gs_r", [128, 4, 256], F32)
        self.QnT = carve("QnT", [128, 8, 128], BF16)
        self.QrT = carve("QrT", [128, 4, 128], BF16)
        self.KnT = carve("KnT", [128, 8, 128], BF16)
        self.Vst = sb("Vst", [128, 8, 129], BF16)
        self.kvs = []
        for i in range(3):
            k = carve("kvk%d" % i, [128, 8, 128], BF16, False)
            r = carve("kvr%d" % i, [128, 128], BF16, False)
            v = carve("kvv%d" % i, [128, 8, 129], BF16, False)
            self.bufs["kvs%d" % i] = Buf("kvs%d" % i)
            self.kvs.append((k, r, v, self.bufs["kvs%d" % i]))
        self.kvi = 0
        self.PT = carve("PT", [128, 8, 128], BF16)
        self.PT2 = sb("PT2", [128, 8, 128], BF16)
        self.ytm = carve("ytm", [128, 1024], BF16)
        self.sm4 = {n: sb("s4_" + n, [128, 4]) for n in
                    ("b", "a", "cm", "mx", "nmx", "ps", "emt", "den", "rden", "ss", "sc", "ws", "mean", "t1", "t2")}
        self.sm8 = sb("sm8", [128, 8])
        self.bc8 = sb("bc8", [128, 8])
        self.carry = sb("carry", [128, 4])
        self.diag = carve("diag", [128, 4, 128], F32)
        self.Am = carve("Am", [128, 4, 128], F32)
        self.dmT = carve("dmT", [128, 4, 128], F32)
        self.wT = None
        self.hsum = carve("hsum", [128, 4, 257], F32)
        self.sqf = sb("sq", [128, 4, 257])
        self.inter = self.sqf
        self.bufs["inter"] = self.bufs["sq"]
        self.sq = self.sqf[:, :, 0:256]
        self.kws = None
        self.cn = [sb("cn%d" % l, [128, 4, 257]) for l in range(NL)]
        self.cnb = [sb("cnb%d" % l, [128, 4, 257], BF16) for l in range(NL)]
        self.mbc = [sb("mbc%d" % l, [128, 4]) for l in range(NL)]
        self.rs = [sb("rs%d" % l, [128, 4, 256]) for l in range(NL)]
        self.rb = [sb("rb%d" % l, [128, 4, 256], BF16) for l in range(NL)]
        self.stg = sb("stg", [128, 2, 128])
        self.sg = sb("sg", [128, 512])
        self.tmpB = sb("tmpB", [128, 512])

        for nm_, shp_ in (("qmT", [128, 4, 128]), ("kmT", [128, 4, 128]), ("km_tm", [128, 4, 128]), ("hqT", [128, 4, 128]), ("ckvT", [128, 4, 128]),
                          ("rqT", [128, 4, 128]), ("rkT", [128, 4, 128]), ("rk_tm", [128, 4, 128]), ("vr", [128, 4, 256]), ("wT", [128, 4, 128]), ("kws", [128, 4, 128])):
            setattr(self, nm_, carve(nm_, shp_, BF16))
        self.yT = {b: carve("yT_" + b, [128, 8, 128], BF16) for b in "mar"}
        ao_ = self._ao
        self._ao = 35840
        self.actT = carve("actT", [128, 44, 128], BF16)
        assert self._ao <= ao_ - 6144, (self._ao, ao_)
        self._ao = ao_
        self.hT2 = sb("hT2", [128, 16, 128], BF16)
        self.yT2 = {b: sb("yT2_" + b, [128, 8, 128], BF16) for b in "mar"}
        self.tiles_ = [
            dict(hT=(self.hT2, self.bufs["hT2"]), yT={b: (self.yT2[b], self.bufs["yT2_" + b]) for b in "mar"}),
            dict(hT=(self.hT, self.bufs["hT"]), yT={b: (self.yT[b], self.bufs["yT_" + b]) for b in "mar"}),
        ]
        self.bprep = self.tmpA[0:16, 0:1673]
        self.bprep2 = self.gbuf[0:16, 0:1673]
        self.bph = self.hb[0:16, 0:1673]
        self.bpl = self.actT[:, :, :].rearrange("p a b -> p (a b)")[0:16, 0:1673]
        for a_, b_ in (("bprep", "tmpA"), ("bprep2", "gbuf"), ("bph", "hb"), ("bpl", "actT")):
            self.bufs[a_] = self.bufs[b_]
        print("SBUF bytes remaining", nc.sbuf_bytes_remaining)
        ao = self._ao
        self._ao = 0
        self.hT_B = carve("hT_B", [128, 16, 128], BF16)
        self.act_tm_B = carve("act_tm_B", [128, 3072], BF16)
        self.actT_B = carve("actT_B", [128, 44, 128], BF16)
        self.tmpA_B = carve("tmpA_B", [128, D], F32)
        self.act_tm = carve("act_tm", [128, 3072], BF16)
        assert self._ao <= ao, (self._ao, ao)
        self.x2 = sb("x2", [128, D])
        self.xb = [(self.x, self.bufs["x"]), (self.x2, self.bufs["x2"])]
        print("SBUF bytes remaining (final)", nc.sbuf_bytes_remaining)
        self.ps = []
        for i in range(8):
            t = nc.alloc_psum_tensor("ps%d" % i, [128, 512], F32)
            self.bufs["ps%d" % i] = Buf("ps%d" % i)
            self.ps.append((t, self.bufs["ps%d" % i]))
        self.psi = 0
        self.attn_active = False
        self.held = set()
        self.pending = []
        self.defer_on = False

    def bank(self):
        nb = 5 if self.attn_active else 8
        while (self.psi % nb) in self.held:
            self.psi += 1
        t, b = self.ps[self.psi % nb]
        self.psi += 1
        return t, b

    def bank_i(self):
        nb = 5 if self.attn_active else 8
        while (self.psi % nb) in self.held:
            self.psi += 1
        i = self.psi % nb
        self.psi += 1
        return i

    def flush(self):
        while self.pending:
            self.pending.pop(0)()

    def set_x(self, i):
        self.x, self.bufs["x"] = self.xb[i]

    def set_tile(self, i):
        tl = self.tiles_[i]
        self.hT, self.bufs["hT"] = tl["hT"]
        self.yT = {b: tl["yT"][b][0] for b in "mar"}
        for b in "mar":
            self.bufs["yT_" + b] = tl["yT"][b][1]

    def barrier(self):
        allb = list(Buf.REG)
        for q in ("pe", "act", "dve", "sp"):
            self.T.wait_all(q, allb)

    def ev(self):
        self.rr += 1
        return "act" if self.rr % 2 else "dve"

    def copy(self, out, in_, reads, writes, q=None):
        q = q or self.ev()
        if q == "act":
            self.T.op("act", lambda e: e.activation(out, in_, AF.Identity), reads, writes)
        else:
            self.T.op(q, lambda e: e.tensor_copy(out, in_), reads, writes)

    def transposes(self, src, src_buf, n, TS, dst, dst_buf, dst_k0=0):
        T = self.T
        j = 0
        while j < n:
            g = min(4, n - j)
            pt, pb = self.bank()
            pv = pt[:, :].bitcast(BF16)
            for i in range(g):
                T.op("pe", lambda e, i=i: e.transpose(pv[:, i * 128:i * 128 + TS], src[0:TS, (j + i) * 128:(j + i + 1) * 128],
                                                      self.identb[0:TS, 0:TS]), [src_buf, self.bufs["identb"]], [pb])
            o = dst[:, dst_k0 + j:dst_k0 + j + g, 0:TS]
            i_ = pv[:, 0:g * 128].rearrange("p (g t) -> p g t", g=g)[:, :, 0:TS]
            self.copy(o, i_, [pb], [dst_buf])
            j += g

    def load_w(self, w2d, k0, nkc, cols):
        i = self.wsi % NSLOT
        self.wsi += 1
        slot, sbuf, bt = self.wslot[i], self.bufs["ws%d" % i], self.wbias[i]
        src = w2d[k0 * 128:(k0 + nkc) * 128]
        if callable(cols):
            ck = cols.ckey
            src = cols(src).rearrange("(k p) a b -> p k a b", p=128)
        else:
            ck = tuple(cols)
            src = src[:, cols[0]:cols[0] + cols[1]].rearrange("(k p) n -> p k n", p=128)
        ncols = 1
        for s_ in src.shape[2:]:
            ncols *= s_
        n = nkc * ncols
        view = slot[:, 0:n].rearrange("p (k n) -> p k n", k=nkc)
        key = (w2d.name, str(w2d.offset), k0, nkc, ck)
        if key in self.wcache:
            gi, gb = self.wcache[key]
            self.T.dma("pool", slot[:, 0:n], self.wscr_t[gi // 100][gi % 100, :, 0:n], sbuf, reads=[gb], writes=[sbuf])
            return view, sbuf, bt
        gi = len(self.wcache)
        gb = Buf("wg%d" % gi)
        self.wcache[key] = (gi, gb)
        sview = self.wscr_t[gi // 100][gi % 100, :, 0:n].rearrange("p (k n) -> p k n", k=nkc)
        dstv = view
        if len(src.shape) == 4:
            dstv = view.rearrange("p k (a b) -> p k a b", a=src.shape[2])
            sview = sview.rearrange("p k (a b) -> p k a b", a=src.shape[2])
            for k in range(nkc):
                self.T.dma("pool", dstv[:, k], src[:, k], sbuf, reads=[], writes=[sbuf])
            for k in range(nkc):
                self.T.dma("pool", sview[:, k], src[:, k], sbuf, reads=[], writes=[gb, sbuf])
        else:
            self.T.dma("pool", dstv, src, sbuf, reads=[], writes=[sbuf])
            self.T.dma("pool", sview, src, sbuf, reads=[], writes=[gb, sbuf])
        return view, sbuf, bt

    def gemm(self, xT, xbuf, nkc, TS, w2d, cols, subs, bias_off=None, ksplit=None):
        T = self.T
        multi = isinstance(xT, list)
        xs = xT if multi else [(xT, xbuf)]
        ncols = cols[1] if not callable(cols) else sum(s[1] for s in subs)
        kmax = SLOT_ELEMS // ncols
        chunks = []
        k = 0
        while k < nkc:
            c = min(kmax, nkc - k)
            chunks.append((k, c))
            k += c
        bidx = [[self.bank_i() for _ in subs] for _ in xs]
        banks = [[self.ps[i] for i in row] for row in bidx]
        bt0 = None
        for ci, (k0, kn) in enumerate(chunks):
            view, wbuf, bt = self.load_w(w2d, k0, kn, cols)
            if ci == 0 and bias_off is not None:
                T.dma("pool", bt[0:2, 0:ncols], self.bhl[:, bias_off:bias_off + ncols], wbuf, reads=[self.bufs["bhl"]], writes=[wbuf])
            for xi, (xT_, xb_) in enumerate(xs):
                for (off, n, _), (pt, pb) in zip(subs, banks[xi]):
                    first = ci == 0
                    if first and bias_off is not None:
                        T.op("pe", lambda e, pt=pt, off=off, n=n: e.matmul(pt[0:TS, 0:n], self.ones2[0:2, 0:TS], bt[0:2, off:off + n], start=True, stop=False),
                             [wbuf, self.bufs["ones2"]], [pb])
                        first = False
                    for kk in range(kn):
                        last = (ci == len(chunks) - 1) and kk == kn - 1
                        T.op("pe", lambda e, kk=kk, first=first, last=last, pt=pt, off=off, n=n, xT_=xT_: e.matmul(
                            pt[0:TS, 0:n], xT_[:, k0 + kk, 0:TS], view[:, kk, off:off + n], start=first, stop=last),
                            [wbuf, xb_], [pb])
                        first = False
        mine = set(i for row in bidx for i in row)

        def run_handlers():
            for xi in range(len(xs)):
                for (off, n, h), (pt, pb) in zip(subs, banks[xi]):
                    if multi:
                        h(pt[0:TS, 0:n], pb, xi)
                    else:
                        h(pt[0:TS, 0:n], pb)
            self.held -= mine
        if self.defer_on:
            self.held |= mine
            self.pending.append(run_handlers)
            while len(self.pending) > PIPE_DEPTH:
                self.pending.pop(0)()
        else:
            self.flush()
            run_handlers()

    def rms_scale(self, src, src_buf, TS, Dn, rstd):
        T = self.T
        jb = self.bufs["sq"]
        junk = self.sqf[:, :, :].rearrange("p a b -> p (a b)")
        T.op("dve", lambda e: e.memset(self.sm4["t2"][:, :], 0.0), [], [self.bufs["s4_t2"]])
        n = 0
        while n < Dn:
            c = min(1024, Dn - n)
            T.op("act", lambda e, n=n, c=c: e.activation(junk[0:TS, 0:c], src[0:TS, n:n + c], AF.Square,
                                                          accum_out=self.sm4["t2"][0:TS, (n // 1024):(n // 1024) + 1]),
                 [src_buf], [jb, self.bufs["s4_t2"]])
            n += c
        nch = (Dn + 1023) // 1024
        if nch > 1:
            T.op("dve", lambda e: e.tensor_reduce(rstd, self.sm4["t2"][0:TS, 0:nch], AX.X, OP.add),
                 [self.bufs["s4_t2"]], [self.bufs["s4_t1"]])
        else:
            T.op("dve", lambda e: e.tensor_copy(rstd, self.sm4["t2"][0:TS, 0:1]), [self.bufs["s4_t2"]], [self.bufs["s4_t1"]])
        T.op("act", lambda e: e.activation(rstd, rstd, AF.Sqrt, bias=self.epsc[0:TS, 0:1], scale=1.0 / Dn), [self.bufs["s4_t1"], self.bufs["epsc"]], [self.bufs["s4_t1"]])
        T.op("dve", lambda e: e.reciprocal(rstd, rstd), [self.bufs["s4_t1"]], [self.bufs["s4_t1"]])

    def load_g(self, g_ap):
        self.T.dma("sp", self.gbuf[:, :], g_ap.partition_broadcast(128), self.bufs["gbuf"], [], [self.bufs["gbuf"]])

    def norm_to_hT(self, l, TS, g_ap, dst=None):
        T = self.T
        dT, dTb = dst if dst is not None else (self.hT, self.bufs["hT"])
        rstd = self.sm4["t1"][0:TS, 0:1]
        self.load_g(g_ap)
        self.rms_scale(self.x, self.bufs["x"], TS, D, rstd)
        T.op("dve", lambda e: e.scalar_tensor_tensor(self.hb[0:TS, :], self.x[0:TS, :], rstd, self.gbuf[0:TS, :], OP.mult, OP.mult),
             self.B("x", "s4_t1", "gbuf"), self.B("hb"))
        self.transposes(self.hb, self.bufs["hb"], 16, TS, dT, dTb)

    def resid_add(self, l, TS, g_ap, src=None):
        T = self.T
        tA, tAb = src if src is not None else (self.tmpA, self.bufs["tmpA"])
        x, xB = self.x, self.bufs["x"]
        rstd = self.sm4["t1"][0:TS, 0:1]
        self.load_g(g_ap)
        self.rms_scale(tA, tAb, TS, D, rstd)
        T.op("dve", lambda e: e.scalar_tensor_tensor(tA[0:TS, :], tA[0:TS, :], rstd, self.gbuf[0:TS, :], OP.mult, OP.mult),
             [tAb] + self.B("s4_t1", "gbuf"), [tAb])
        T.op("dve", lambda e: e.tensor_tensor(x[0:TS, :], x[0:TS, :], tA[0:TS, :], OP.add), [xB, tAb], [xB])

    def rope(self, src, src_buf, TS, H, d, tab, tab_buf, out, out_buf, scale=None):
        T = self.T
        hd = d // 2
        x1, x2 = src[0:TS, :, 0:hd], src[0:TS, :, hd:d]
        cos = tab[0:TS, 0, :].unsqueeze(1).broadcast_to([TS, H, hd])
        sin = tab[0:TS, 1, :].unsqueeze(1).broadcast_to([TS, H, hd])
        t1 = self.sqf[:, :, :].rearrange("p a b -> p (a b)")[0:TS, 0:H * hd].rearrange("p (h d) -> p h d", h=H)
        t2 = self.sqf[:, :, :].rearrange("p a b -> p (a b)")[0:TS, 512:512 + H * hd].rearrange("p (h d) -> p h d", h=H)
        jb = self.bufs["sq"]
        T.op("dve", lambda e: e.tensor_tensor(t1, x1, cos, OP.mult), [src_buf, tab_buf], [jb])
        T.op("dve", lambda e: e.tensor_tensor(t2, x2, sin, OP.mult), [src_buf, tab_buf], [jb])
        T.op("dve", lambda e: e.tensor_tensor(out[0:TS, :, 0:hd], t1, t2, OP.subtract), [jb], [out_buf])
        T.op("dve", lambda e: e.tensor_tensor(t1, x1, sin, OP.mult), [src_buf, tab_buf], [jb])
        T.op("dve", lambda e: e.tensor_tensor(t2, x2, cos, OP.mult), [src_buf, tab_buf], [jb])
        T.op("dve", lambda e: e.tensor_tensor(out[0:TS, :, hd:d], t1, t2, OP.add), [jb], [out_buf])
        if scale is not None:
            T.op("dve", lambda e: e.tensor_scalar(out[0:TS], out[0:TS], scale, None, OP.mult), [out_buf], [out_buf])

    def prologue(self):
        T = self.T
        nc = self.nc
        self.bufs["bhl"] = Buf("bhl")
        T.dma("sp", self.c32[:, :], self.t_c32, self.bufs["c32"], [], [self.bufs["c32"]])
        C = self.c32
        self.ident = C[:, 0:128]
        self.tri = C[:, 128:256]
        self.negmask = C[:, 256:384]
        self.negmaskT = C[:, 384:512]
        self.DT = C[:, 640:1152].rearrange("p (h l) -> p h l", h=4)
        self.misc = C[:, 1152:1172]
        T.op("dve", lambda e: e.tensor_copy(self.identb[:, :], self.ident), self.B("c32"), self.B("identb"))
        T.op("dve", lambda e: e.memset(self.ones2[:, :], 1.0), [], self.B("ones2"))
        T.op("dve", lambda e: e.memset(self.onesf[:, :], 1.0), [], self.B("onesf"))
        T.op("dve", lambda e: e.memset(self.epsc[:, :], EPS), [], self.B("epsc"))
        T.op("dve", lambda e: e.memset(self.vm[:, :, :], 1.0), [], self.B("vm"))
        T.op("dve", lambda e: e.memset(self.Vst[:, :, :], 1.0), [], self.B("Vst"))
        bflat = self.b_in.rearrange("l n -> (l n)").rearrange("(a b) -> a b", a=16)
        T.dma("sp", self.bprep, bflat, self.bufs["bprep"], [], self.B("bprep"))
        T.op("dve", lambda e: e.tensor_copy(self.bph, self.bprep), self.B("bprep"), self.B("bph"))
        T.op("dve", lambda e: e.tensor_copy(self.bprep2, self.bph), self.B("bph"), self.B("bprep2"))
        T.op("dve", lambda e: e.tensor_tensor(self.bprep2, self.bprep, self.bprep2, OP.subtract), self.B("bprep", "bprep2"), self.B("bprep2"))
        T.op("dve", lambda e: e.tensor_copy(self.bpl, self.bprep2), self.B("bprep2"), self.B("bpl"))
        T.dma("sp", self.bhl[0].rearrange("(a b) -> a b", a=16), self.bph, self.bufs["bph"], self.B("bph"), [self.bufs["bhl"]])
        T.dma("sp", self.bhl[1].rearrange("(a b) -> a b", a=16), self.bpl, self.bufs["bph"], self.B("bpl"), [self.bufs["bhl"]])
        for nm in ("p", "s"):
            for j, t in enumerate(("kn", "kr", "vv")):
                self.bufs[t + "_" + nm] = Buf(t + "_" + nm)

    def load_layer_consts(self, l):
        T = self.T
        g = self.gsm
        b = self.bufs["gsm"]
        T.dma("sp", g[:, 0, :], self.g_mlstm[l].partition_broadcast(128), b, [], [b])
        T.dma("sp", g[:, 1, :], self.g_ret[l].partition_broadcast(128), b, [], [b])
        T.dma("sp", g[:, 2, 0:512], self.g_qa[l].partition_broadcast(128), b, [], [b])
        T.dma("sp", g[:, 2, 512:1024], self.g_kva[l].partition_broadcast(128), b, [], [b])

    def _stage_in(self, seq, t, l, TS):
        T = self.T
        w = self.w_in[l]
        bo = l * DIN
        r0 = seq.tab0 + t * 128
        T.dma("sp", self.rt128[0:TS, 0, :], self.t_cos128[r0:r0 + TS, :], self.bufs["rt128"], [], self.B("rt128"))
        T.dma("sp", self.rt128[0:TS, 1, :], self.t_sin128[r0:r0 + TS, :], self.bufs["rt128"], [], self.B("rt128"))
        T.dma("sp", self.rt64[0:TS, 0, :], self.t_cos64[r0:r0 + TS, :], self.bufs["rt64"], [], self.B("rt64"))
        T.dma("sp", self.rt64[0:TS, 1, :], self.t_sin64[r0:r0 + TS, :], self.bufs["rt64"], [], self.B("rt64"))
        hT, hTb = self.hT, self.bufs["hT"]
        tA = self.tmpA
        tAb = self.bufs["tmpA"]

        def G(c0, subs, n=None):
            n = n or sum(s[1] for s in subs)
            self.gemm(hT, hTb, 16, TS, w, (c0, n), subs, bias_off=bo + c0)

        def h_mq(ps, pb):
            self.copy(self.hb[0:TS, 0:512], ps, [pb], self.B("hb"))
            self.transposes(self.hb, self.bufs["hb"], 4, TS, self.qmT, self.bufs["qmT"])
        G(C_MQ, [(0, 512, h_mq)])

        def h_mk(ps, pb):
            T.op("act", lambda e: e.activation(self.km_tm[0:TS, :, :].rearrange("p a b -> p (a b)"), ps, AF.Identity, scale=128 ** -0.5), [pb], self.B("km_tm"))
            self.transposes(self.km_tm[:, :, :].rearrange("p a b -> p (a b)"), self.bufs["km_tm"], 4, TS, self.kmT, self.bufs["kmT"])
        G(C_MK, [(0, 512, h_mk)])
        for j in range(2):
            def h_mv(ps, pb, j=j):
                self.copy(self.vm[0:TS, 2 * j:2 * j + 2, 0:256], ps.rearrange("p (h v) -> p h v", h=2), [pb], self.B("vm"))
            G(C_MV + 512 * j, [(0, 512, h_mv)])
        for j in range(2):
            def h_mo(ps, pb, j=j):
                o = self.gs_m[0:TS, 2 * j:2 * j + 2, :].rearrange("p a b -> p (a b)")
                T.op("act", lambda e: e.activation(o, ps, AF.Sigmoid), [pb], self.B("gs_m"))
                T.op("dve", lambda e: e.tensor_tensor(o, o, self.gsm[0:TS, 0, 512 * j:512 * j + 512], OP.mult), self.B("gs_m", "gsm"), self.B("gs_m"))
            G(C_MO + 512 * j, [(0, 512, h_mo)])

        def h_if(ps, pb):
            T.op("dve", lambda e: e.tensor_copy(self.igf[0:TS, 0:4], ps[:, 0:4]), [pb], self.B("igf"))
            T.op("act", lambda e: e.activation(self.igf[0:TS, 4:8], ps[:, 4:8], AF.Exp, scale=-1.0), [pb], self.B("igf"))
            T.op("act", lambda e: e.activation(self.igf[0:TS, 4:8], self.igf[0:TS, 4:8], AF.Ln, bias=1.0), self.B("igf"), self.B("igf"))
            T.op("dve", lambda e: e.tensor_scalar(self.igf[0:TS, 4:8], self.igf[0:TS, 4:8], -1.0, None, OP.mult), self.B("igf"), self.B("igf"))

        def h_adq(ps, pb):
            self.copy(tA[0:TS, 0:512], ps, [pb], [tAb])
            rstd = self.sm4["t1"][0:TS, 0:1]
            self.rms_scale(tA, tAb, TS, 512, rstd)
            T.op("dve", lambda e: e.scalar_tensor_tensor(self.hb[0:TS, 0:512], tA[0:TS, 0:512], rstd, self.gsm[0:TS, 2, 0:512], OP.mult, OP.mult),
                 [tAb] + self.B("s4_t1", "gsm"), self.B("hb"))
            self.transposes(self.hb, self.bufs["hb"], 4, TS, self.hqT, self.bufs["hqT"])
        G(C_MI, [(0, 8, h_if), (8, 512, h_adq)])

        def h_adkv(ps, pb):
            self.copy(tA[0:TS, 0:512], ps, [pb], [tAb])
            rstd = self.sm4["t1"][0:TS, 0:1]
            self.rms_scale(tA, tAb, TS, 512, rstd)
            T.op("dve", lambda e: e.scalar_tensor_tensor(self.ckv_o[0:TS, :], tA[0:TS, 0:512], rstd, self.gsm[0:TS, 2, 512:1024], OP.mult, OP.mult),
                 [tAb] + self.B("s4_t1", "gsm"), self.B("ckv_o"))
            T.dma("sp", seq.o_ckv[l, t * 128:t * 128 + TS, :], self.ckv_o[0:TS, :], self.bufs["ckv_o"], self.B("ckv_o"), [])
            T.op("act", lambda e: e.activation(self.hb[0:TS, 0:512], self.ckv_o[0:TS, :], AF.Identity), self.B("ckv_o"), self.B("hb"))
            self.transposes(self.hb, self.bufs["hb"], 4, TS, self.ckvT, self.bufs["ckvT"])
        G(C_ADKV, [(0, 512, h_adkv)])

        def h_akr(ps, pb):
            self.copy(tA[0:TS, 0:64], ps, [pb], [tAb])
            self.rope(tA[:, 0:64].rearrange("p (h d) -> p h d", h=1), tAb, TS, 1, 64, self.rt64, self.bufs["rt64"],
                      self.kr_o[:, :].rearrange("p (h d) -> p h d", h=1), self.bufs["kr_o"])
            T.dma("sp", seq.o_kr[l, t * 128:t * 128 + TS, :], self.kr_o[0:TS, :], self.bufs["kr_o"], self.B("kr_o"), [])
            T.op("act", lambda e: e.activation(self.krd[0:TS, 0:64], self.kr_o[0:TS, :], AF.Identity), self.B("kr_o"), self.B("krd"))
            T.op("act", lambda e: e.activation(self.krd[0:TS, 64:128], self.kr_o[0:TS, :], AF.Identity), self.B("kr_o"), self.B("krd"))
            self.transposes(self.krd, self.bufs["krd"], 1, TS, self.krT[:, :].rearrange("p (o t) -> p o t", o=1), self.bufs["krT"])
        G(C_AKR, [(0, 64, h_akr)])

        def h_rq(ps, pb):
            self.copy(tA[0:TS, 0:512], ps, [pb], [tAb])
            self.rope(tA[:, 0:512].rearrange("p (h d) -> p h d", h=4), tAb, TS, 4, 128, self.rt128, self.bufs["rt128"],
                      self.hb[:, 0:512].rearrange("p (h d) -> p h d", h=4), self.bufs["hb"])
            self.transposes(self.hb, self.bufs["hb"], 4, TS, self.rqT, self.bufs["rqT"])
        G(C_RQ, [(0, 512, h_rq)])

        def h_rk(ps, pb):
            self.copy(tA[0:TS, 0:512], ps, [pb], [tAb])
            self.rope(tA[:, 0:512].rearrange("p (h d) -> p h d", h=4), tAb, TS, 4, 128, self.rt128, self.bufs["rt128"],
                      self.rk_tm, self.bufs["rk_tm"], scale=128 ** -0.5)
            self.transposes(self.rk_tm[:, :, :].rearrange("p a b -> p (a b)"), self.bufs["rk_tm"], 4, TS, self.rkT, self.bufs["rkT"])
        G(C_RK, [(0, 512, h_rk)])
        for j in range(2):
            def h_rv(ps, pb, j=j):
                self.copy(self.vr[0:TS, 2 * j:2 * j + 2, :].rearrange("p a b -> p (a b)"), ps, [pb], self.B("vr"))
            G(C_RV + 512 * j, [(0, 512, h_rv)])
        for j in range(2):
            def h_rg(ps, pb, j=j):
                o = self.gs_r[0:TS, 2 * j:2 * j + 2, :].rearrange("p a b -> p (a b)")
                T.op("act", lambda e: e.activation(o, ps, AF.Silu), [pb], self.B("gs_r"))
                T.op("dve", lambda e: e.tensor_tensor(o, o, self.gsm[0:TS, 1, 512 * j:512 * j + 512], OP.mult), self.B("gs_r", "gsm"), self.B("gs_r"))
            G(C_RG + 512 * j, [(0, 512, h_rg)])

    def stage_in(self, seq, t, l, TS):
        self.defer_on = True
        self._stage_in(seq, t, l, TS)
        self.flush()
        self.defer_on = False

    def bcast_rows(self, col, col_buf, L):
        T = self.T
        idb = self.ident[0:L, 0:L].unsqueeze(1).broadcast_to([L, 4, L])
        cb = col[0:L, 0:4].unsqueeze(2).broadcast_to([L, 4, L])
        T.op("dve", lambda e: e.tensor_tensor(self.diag[0:L, :, 0:L], idb, cb, OP.mult), [col_buf] + self.B("c32"), self.B("diag"))
        pt, pb = self.bank()
        for h in range(4):
            T.op("pe", lambda e, h=h: e.matmul(pt[0:L, h * L:(h + 1) * L], self.onesf[0:L, 0:L], self.diag[0:L, h, 0:L], start=True, stop=True),
                 self.B("diag", "onesf"), [pb])
        return pt[0:L, 0:4 * L].rearrange("p (h l) -> p h l", h=4), pb

    def mlstm(self, l, L):
        T = self.T
        s4 = self.sm4
        sb4 = {k: self.bufs["s4_" + k] for k in s4}
        ig, lf = self.igf[0:L, 0:4], self.igf[0:L, 4:8]
        cn, cnb, mbc = self.cn[l], self.cnb[l], self.mbc[l]
        cnB, cnbB, mbcB = self.bufs["cn%d" % l], self.bufs["cnb%d" % l], self.bufs["mbc%d" % l]
        b, a, cm, mx, nmx = (s4[k][0:L, :] for k in ("b", "a", "cm", "mx", "nmx"))
        pt, pb = self.bank()
        T.op("pe", lambda e: e.matmul(pt[0:L, 0:4], self.tri[0:L, 0:L], lf, start=True, stop=True), self.B("c32", "igf"), [pb])
        T.op("dve", lambda e: e.tensor_copy(b, pt[0:L, 0:4]), [pb], [sb4["b"]])
        T.op("dve", lambda e: e.tensor_tensor(a, ig, b, OP.subtract), [sb4["b"]] + self.B("igf"), [sb4["a"]])
        Abc, Ab = self.bcast_rows(s4["a"], sb4["a"], L)
        nm = self.negmask[0:L, 0:L].unsqueeze(1).broadcast_to([L, 4, L])
        T.op("dve", lambda e: e.tensor_tensor(self.Am[0:L, :, 0:L], Abc, nm, OP.add), [Ab] + self.B("c32"), self.B("Am"))
        T.op("dve", lambda e: e.tensor_reduce(cm, self.Am[0:L, :, 0:L], AX.X, OP.max), self.B("Am"), [sb4["cm"]])
        T.op("dve", lambda e: e.tensor_tensor(mx, cm, mbc[0:L, :], OP.max), [sb4["cm"], mbcB], [sb4["mx"]])
        T.op("dve", lambda e: e.tensor_scalar(nmx, mx, -1.0, None, OP.mult), [sb4["mx"]], [sb4["nmx"]])
        mt = self.sm8[0:L, 4:8]
        T.op("dve", lambda e: e.tensor_copy(self.sm8[0:L, 0:4], b), [sb4["b"]], self.B("sm8"))
        T.op("dve", lambda e: e.tensor_tensor(mt, b, mx, OP.add), [sb4["b"], sb4["mx"]], self.B("sm8"))
        ps_, emt = s4["ps"][0:L, :], s4["emt"][0:L, :]
        T.op("dve", lambda e: e.tensor_tensor(ps_, mbc[0:L, :], mx, OP.subtract), [mbcB, sb4["mx"]], [sb4["ps"]])
        T.op("act", lambda e: e.activation(ps_, ps_, AF.Exp), [sb4["ps"]], [sb4["ps"]])
        T.op("act", lambda e: e.activation(emt, mt, AF.Exp, scale=-1.0), self.B("sm8"), [sb4["emt"]])
        Bbc, Bb = self.bcast_rows(s4["nmx"], sb4["nmx"], L)
        nmT = self.negmaskT[0:L, 0:L].unsqueeze(1).broadcast_to([L, 4, L])
        T.op("dve", lambda e: e.tensor_tensor(self.Am[0:L, :, 0:L], Bbc, nmT, OP.add), [Bb] + self.B("c32"), self.B("Am"))
        for h in range(4):
            T.op("act", lambda e, h=h: e.activation(self.dmT[0:L, h, 0:L], self.Am[0:L, h, 0:L], AF.Exp, bias=s4["a"][0:L, h:h + 1]),
                 self.B("Am") + [sb4["a"]], self.B("dmT"))
        pt, pb = self.bank()
        for h in range(4):
            T.op("pe", lambda e, h=h: e.matmul(pt[0:L, h * L:(h + 1) * L], self.kmT[:, h, 0:L], self.qmT[:, h, 0:L], start=True, stop=True),
                 self.B("kmT", "qmT"), [pb])
        T.op("dve", lambda e: e.tensor_tensor(self.wT[0:L, :, 0:L], pt[0:L, 0:4 * L].rearrange("p (h l) -> p h l", h=4), self.dmT[0:L, :, 0:L], OP.mult),
             [pb] + self.B("dmT"), self.B("wT"))
        for h in range(4):
            p1, b1 = self.bank()
            p2, b2 = self.bank()
            T.op("pe", lambda e, h=h: e.matmul(p1[0:L, 0:257], self.wT[0:L, h, 0:L], self.vm[0:L, h, :], start=True, stop=True), self.B("wT", "vm"), [b1])
            T.op("pe", lambda e, h=h: e.matmul(p2[0:L, 0:257], self.qmT[:, h, 0:L], cnb[:, h, :], start=True, stop=True), [self.bufs["qmT"], cnbB], [b2])
            T.op("act", lambda e, h=h: e.activation(self.inter[0:L, h, :], p2[0:L, 0:257], AF.Identity, scale=s4["ps"][0:L, h:h + 1]), [b2, sb4["ps"]], self.B("inter"))
            T.op("dve", lambda e, h=h: e.tensor_tensor(self.hsum[0:L, h, :], p1[0:L, 0:257], self.inter[0:L, h, :], OP.add), [b1] + self.B("inter"), self.B("hsum"))
        den, rden, ss, sc = (s4[k][0:L, :] for k in ("den", "rden", "ss", "sc"))
        T.op("act", lambda e: e.activation(den, self.hsum[0:L, :, 256], AF.Abs), self.B("hsum"), [sb4["den"]])
        T.op("dve", lambda e: e.tensor_tensor(den, den, emt, OP.max), [sb4["den"], sb4["emt"]], [sb4["den"]])
        T.op("dve", lambda e: e.reciprocal(rden, den), [sb4["den"]], [sb4["rden"]])
        hv = self.hsum[0:L, :, 0:256]
        T.op("dve", lambda e: e.tensor_tensor(self.sq[0:L, :, :], hv, hv, OP.mult), self.B("hsum"), self.B("sq"))
        T.op("dve", lambda e: e.tensor_reduce(ss, self.sq[0:L, :, :], AX.X, OP.add), self.B("sq"), [sb4["ss"]])
        T.op("dve", lambda e: e.tensor_tensor(ss, ss, rden, OP.mult), [sb4["ss"], sb4["rden"]], [sb4["ss"]])
        T.op("dve", lambda e: e.tensor_tensor(ss, ss, rden, OP.mult), [sb4["ss"], sb4["rden"]], [sb4["ss"]])
        T.op("act", lambda e: e.activation(ss, ss, AF.Sqrt, bias=self.epsc[0:L, 0:1], scale=1.0 / 256), [sb4["ss"], self.bufs["epsc"]], [sb4["ss"]])
        T.op("dve", lambda e: e.reciprocal(ss, ss), [sb4["ss"]], [sb4["ss"]])
        T.op("dve", lambda e: e.tensor_tensor(sc, ss, rden, OP.mult), [sb4["ss"], sb4["rden"]], [sb4["sc"]])
        T.op("dve", lambda e: e.tensor_tensor(self.sq[0:L, :, :], hv, s4["sc"][0:L, :].unsqueeze(2).broadcast_to([L, 4, 256]), OP.mult),
             self.B("hsum") + [sb4["sc"]], self.B("sq"))
        T.op("dve", lambda e: e.tensor_tensor(self.ytm[0:L, :].rearrange("p (h v) -> p h v", h=4), self.sq[0:L, :, :], self.gs_m[0:L, :, :], OP.mult),
             self.B("sq", "gs_m"), self.B("ytm"))
        self.transposes(self.ytm, self.bufs["ytm"], 8, L, self.yT["m"], self.bufs["yT_m"])
        sel = self.c32[0:L, 512:640] if L == 128 else self.c32[0:L, 1172:1300]
        pt, pb = self.bank()
        T.op("pe", lambda e: e.matmul(pt[:, 0:8], sel, self.sm8[0:L, :], start=True, stop=True), self.B("c32", "sm8"), [pb])
        T.op("dve", lambda e: e.tensor_copy(self.bc8[:, :], pt[:, 0:8]), [pb], self.B("bc8"))
        blast, mnew = self.bc8[:, 0:4], self.bc8[:, 4:8]
        t1 = s4["t1"]
        T.op("dve", lambda e: e.tensor_tensor(t1[:, :], blast, mnew, OP.subtract), self.B("bc8"), [sb4["t1"]])
        ws = s4["ws"][0:L, :]
        T.op("dve", lambda e: e.tensor_tensor(ws, a, t1[0:L, :], OP.add), [sb4["a"], sb4["t1"]], [sb4["ws"]])
        T.op("act", lambda e: e.activation(ws, ws, AF.Exp), [sb4["ws"]], [sb4["ws"]])
        T.op("dve", lambda e: e.tensor_tensor(self.carry[:, :], t1[:, :], mbc[:, :], OP.add), [sb4["t1"], mbcB], self.B("carry"))
        T.op("act", lambda e: e.activation(self.carry[:, :], self.carry[:, :], AF.Exp), self.B("carry"), self.B("carry"))
        T.op("dve", lambda e: e.tensor_tensor(self.kws[0:L, :, :], self.km_tm[0:L, :, :], s4["ws"][0:L, :].unsqueeze(2).broadcast_to([L, 4, 128]), OP.mult),
             self.B("km_tm") + [sb4["ws"]], self.B("kws"))
        for h in range(4):
            p1, b1 = self.bank()
            T.op("pe", lambda e, h=h: e.matmul(p1[:, 0:257], self.kws[0:L, h, :], self.vm[0:L, h, :], start=True, stop=True), self.B("kws", "vm"), [b1])
            T.op("dve", lambda e, h=h: e.scalar_tensor_tensor(cn[:, h, :], cn[:, h, :], self.carry[:, h:h + 1], p1[:, 0:257], OP.mult, OP.add),
                 [cnB, b1] + self.B("carry"), [cnB])
        T.op("act", lambda e: e.activation(cnb[:, :, :], cn[:, :, :], AF.Identity), [cnB], [cnbB])
        T.op("dve", lambda e: e.tensor_copy(mbc[:, :], mnew), self.B("bc8"), [mbcB])

    def retention(self, l, L):
        T = self.T
        s4 = self.sm4
        sb4 = {k: self.bufs["s4_" + k] for k in s4}
        rs, rb = self.rs[l], self.rb[l]
        rsB, rbB = self.bufs["rs%d" % l], self.bufs["rb%d" % l]
        pt, pb = self.bank()
        for h in range(4):
            T.op("pe", lambda e, h=h: e.matmul(pt[0:L, h * L:(h + 1) * L], self.rkT[:, h, 0:L], self.rqT[:, h, 0:L], start=True, stop=True),
                 self.B("rkT", "rqT"), [pb])
        T.op("dve", lambda e: e.tensor_tensor(self.wT[0:L, :, 0:L], pt[0:L, 0:4 * L].rearrange("p (h l) -> p h l", h=4), self.DT[0:L, :, 0:L], OP.mult),
             [pb] + self.B("c32"), self.B("wT"))
        o = self.hsum[0:L, :, 0:256]
        for h in range(4):
            p1, b1 = self.bank()
            p2, b2 = self.bank()
            T.op("pe", lambda e, h=h: e.matmul(p1[0:L, 0:256], self.wT[0:L, h, 0:L], self.vr[0:L, h, :], start=True, stop=True), self.B("wT", "vr"), [b1])
            T.op("pe", lambda e, h=h: e.matmul(p2[0:L, 0:256], self.rqT[:, h, 0:L], rb[:, h, :], start=True, stop=True), [self.bufs["rqT"], rbB], [b2])
            T.op("act", lambda e, h=h: e.activation(self.inter[0:L, h, 0:256], p2[0:L, 0:256], AF.Identity, scale=self.misc[0:L, h:h + 1]), [b2] + self.B("c32"), self.B("inter"))
            T.op("dve", lambda e, h=h: e.tensor_tensor(self.hsum[0:L, h, 0:256], p1[0:L, 0:256], self.inter[0:L, h, 0:256], OP.add), [b1] + self.B("inter"), self.B("hsum"))
        mean, ss = s4["mean"][0:L, :], s4["ss"][0:L, :]
        T.op("dve", lambda e: e.tensor_reduce(mean, o, AX.X, OP.add), self.B("hsum"), [sb4["mean"]])
        T.op("dve", lambda e: e.tensor_scalar(mean, mean, 1.0 / 256, None, OP.mult), [sb4["mean"]], [sb4["mean"]])
        T.op("dve", lambda e: e.tensor_tensor(o, o, s4["mean"][0:L, :].unsqueeze(2).broadcast_to([L, 4, 256]), OP.subtract), self.B("hsum") + [sb4["mean"]], self.B("hsum"))
        T.op("dve", lambda e: e.tensor_tensor(self.sq[0:L, :, :], o, o, OP.mult), self.B("hsum"), self.B("sq"))
        T.op("dve", lambda e: e.tensor_reduce(ss, self.sq[0:L, :, :], AX.X, OP.add), self.B("sq"), [sb4["ss"]])
        T.op("act", lambda e: e.activation(ss, ss, AF.Sqrt, bias=self.epsc[0:L, 0:1], scale=1.0 / 256), [sb4["ss"], self.bufs["epsc"]], [sb4["ss"]])
        T.op("dve", lambda e: e.reciprocal(ss, ss), [sb4["ss"]], [sb4["ss"]])
        T.op("dve", lambda e: e.tensor_tensor(self.sq[0:L, :, :], o, s4["ss"][0:L, :].unsqueeze(2).broadcast_to([L, 4, 256]), OP.mult),
             self.B("hsum") + [sb4["ss"]], self.B("sq"))
        T.op("dve", lambda e: e.tensor_tensor(self.ytm[0:L, :].rearrange("p (h v) -> p h v", h=4), self.sq[0:L, :, :], self.gs_r[0:L, :, :], OP.mult),
             self.B("sq", "gs_r"), self.B("ytm"))
        self.transposes(self.ytm, self.bufs["ytm"], 8, L, self.yT["r"], self.bufs["yT_r"])
        zoff = 4 if L == 128 else 8
        doff = 12 if L == 128 else 16
        zeta = self.misc[0:L, zoff:zoff + 4].unsqueeze(2).broadcast_to([L, 4, 128])
        T.op("dve", lambda e: e.tensor_tensor(self.kws[0:L, :, :], self.rk_tm[0:L, :, :], zeta, OP.mult), self.B("rk_tm", "c32"), self.B("kws"))
        for h in range(4):
            p1, b1 = self.bank()
            T.op("pe", lambda e, h=h: e.matmul(p1[:, 0:256], self.kws[0:L, h, :], self.vr[0:L, h, :], start=True, stop=True), self.B("kws", "vr"), [b1])
            T.op("dve", lambda e, h=h: e.scalar_tensor_tensor(rs[:, h, :], rs[:, h, :], self.misc[:, doff + h:doff + h + 1], p1[:, 0:256], OP.mult, OP.add),
                 [rsB, b1] + self.B("c32"), [rsB])
        T.op("act", lambda e: e.activation(rb[:, :, :], rs[:, :, :], AF.Identity), [rsB], [rbB])

    def kv_up(self, l, TS, seqname, kt):
        T = self.T
        kn_d, kr_d, vv_d = self.kv[seqname]
        bkn, bkr, bvv = (self.bufs[n + "_" + seqname] for n in ("kn", "kr", "vv"))
        w = self.w_ukv[l]
        for half in range(2):
            view, wbuf, _ = self.load_w(w, 0, 4, CK(lambda s, half=half: s.rearrange("r (h c) -> r h c", h=8)[:, 4 * half:4 * half + 4, 0:128], ("n", half)))
            pt, pb = self.bank()
            for hh in range(4):
                for kc in range(4):
                    T.op("pe", lambda e, hh=hh, kc=kc: e.matmul(pt[:, hh * 128:hh * 128 + TS], view[:, kc, hh * 128:(hh + 1) * 128], self.ckvT[:, kc, 0:TS],
                                                                start=(kc == 0), stop=(kc == 3)), [wbuf] + self.B("ckvT"), [pb])
            self.copy(self.KnT[:, 4 * half:4 * half + 4, 0:TS], pt[:, :].rearrange("p (h t) -> p h t", h=4)[:, :, 0:TS], [pb], self.B("KnT"))
        for half in range(2):
            def hv(ps, pb, half=half):
                self.copy(self.Vst[0:TS, 4 * half:4 * half + 4, 0:128], ps.rearrange("p (h v) -> p h v", h=4), [pb], self.B("Vst"))
            self.gemm(self.ckvT, self.bufs["ckvT"], 4, TS, w,
                      CK(lambda s, half=half: s.rearrange("r (h c) -> r h c", h=8)[:, 4 * half:4 * half + 4, 128:256], ("v", half)), [(0, 512, hv)])
        T.dma("sp", kn_d[l, kt].rearrange("p (h t) -> p h t", h=8)[:, :, 0:TS], self.KnT[:, :, 0:TS], self.bufs["KnT"], self.B("KnT"), [bkn])
        T.dma("sp", kr_d[l, kt][:, 0:TS], self.krT[:, 0:TS], self.bufs["krT"], self.B("krT"), [bkr])
        T.dma("sp", vv_d[l, kt][0:TS, :], self.Vst[0:TS, :, :].rearrange("p h v -> p (h v)"), self.bufs["Vst"], self.B("Vst"), [bvv])

    def mla(self, seq, t, l, TS):
        T = self.T
        w = self.w_uq[l]
        for half in range(2):
            view, wbuf, _ = self.load_w(w, 0, 4, CK(lambda s, half=half: s.rearrange("r (h c) -> r h c", h=8)[:, 4 * half:4 * half + 4, 0:128], ("n", half)))
            pt, pb = self.bank()
            for hh in range(4):
                for kc in range(4):
                    T.op("pe", lambda e, hh=hh, kc=kc: e.matmul(pt[:, hh * 128:hh * 128 + TS], view[:, kc, hh * 128:(hh + 1) * 128], self.hqT[:, kc, 0:TS],
                                                                start=(kc == 0), stop=(kc == 3)), [wbuf] + self.B("hqT"), [pb])
            self.copy(self.QnT[:, 4 * half:4 * half + 4, 0:TS], pt[:, :].rearrange("p (h t) -> p h t", h=4)[:, :, 0:TS], [pb], self.B("QnT"))
        tA, tAb = self.tmpA, self.bufs["tmpA"]

        def hqr(ps, pb):
            self.copy(tA[0:TS, 0:512], ps, [pb], [tAb])
            self.rope(tA[:, 0:512].rearrange("p (h d) -> p h d", h=8), tAb, TS, 8, 64, self.rt64, self.bufs["rt64"],
                      self.hb[:, 0:512].rearrange("p (h d) -> p h d", h=8), self.bufs["hb"])
            self.transposes(self.hb, self.bufs["hb"], 4, TS, self.QrT, self.bufs["QrT"])
        self.gemm(self.hqT, self.bufs["hqT"], 4, TS, w, CK(lambda s: s.rearrange("r (h c) -> r h c", h=8)[:, :, 128:192], ("r",)), [(0, 512, hqr)])
        kt_own = seq.kt0 + t
        kn_d, kr_d, vv_d = self.kv[seq.name]
        bkn, bkr, bvv = (self.bufs[n + "_" + seq.name] for n in ("kn", "kr", "vv"))

        def load_kv(kt, KS):
            ks, rs_, vs, sbuf = self.kvs[self.kvi % 3]
            self.kvi += 1
            T.dma("sp", ks[:, :, 0:KS], kn_d[l, kt].rearrange("p (h t) -> p h t", h=8)[:, :, 0:KS], sbuf, [bkn], [sbuf])
            T.dma("sp", rs_[:, 0:KS], kr_d[l, kt][:, 0:KS], sbuf, [bkr], [sbuf])
            T.dma("sp", vs[0:KS, :, :].rearrange("p h v -> p (h v)"), vv_d[l, kt][0:KS, :], sbuf, [bvv], [sbuf])
            return ks, rs_, vs, sbuf
        pre = {}
        for kt in range(min(2, kt_own)):
            pre[kt] = load_kv(kt, 128)
        self.kv_up(l, TS, seq.name, kt_own)
        kn_d, kr_d, vv_d = self.kv[seq.name]
        bkn, bkr, bvv = (self.bufs[n + "_" + seq.name] for n in ("kn", "kr", "vv"))
        nkt = kt_own + 1
        scale = 192 ** -0.5
        self.attn_active = True
        accs = [self.ps[5], self.ps[6], self.ps[7]]
        hb_of = lambda h: (accs[h // 3], (h % 3) * 129)
        PTs = [(self.PT, self.bufs["PT"]), (self.PT2, self.bufs["PT2"])]

        def pv(kt, KS, PTt, PTb, vs, sbuf):
            for h in range(8):
                (at, ab), co = hb_of(h)
                T.op("pe", lambda e, h=h, at=at, co=co: e.matmul(at[0:TS, co:co + 129], PTt[0:KS, h, 0:TS], vs[0:KS, h, :], start=(kt == 0 and h % 3 == 0), stop=(kt == nkt - 1)),
                     [sbuf, PTb], [ab])
        prev = None
        for kt in range(nkt):
            KS = TS if kt == kt_own else 128
            ks, rs_, vs, sbuf = pre[kt] if kt in pre else load_kv(kt, KS)
            PTt, PTb = PTs[kt % 2]
            for half in range(2):
                pt, pb = self.bank()
                for hh in range(4):
                    h = 4 * half + hh
                    po = (h % 2) * 64
                    T.op("pe", lambda e, h=h, hh=hh: e.matmul(pt[0:KS, hh * 128:hh * 128 + TS], ks[:, h, 0:KS], self.QnT[:, h, 0:TS], start=True, stop=False),
                         [sbuf] + self.B("QnT"), [pb])
                    T.op("pe", lambda e, h=h, hh=hh, po=po: e.matmul(pt[0:KS, hh * 128:hh * 128 + TS], rs_[po:po + 64, 0:KS], self.QrT[po:po + 64, h // 2, 0:TS],
                                                                       start=False, stop=True), [sbuf] + self.B("QrT"), [pb])
                T.op("act", lambda e, half=half: e.activation(PTt[0:KS, 4 * half:4 * half + 4, 0:TS],
                                                               pt[0:KS, :].rearrange("p (h t) -> p h t", h=4)[:, :, 0:TS], AF.Exp, scale=scale),
                     [pb], [PTb])
            if seq.causal and kt == kt_own and TS == 128:
                T.op("dve", lambda e: e.memset(PTt[64:128, :, 0:64], 0.0), [], [PTb])
            if prev is not None:
                pv(*prev)
            prev = (kt, KS, PTt, PTb, vs, sbuf)
        pv(*prev)
        for i, (at, ab) in enumerate(accs):
            nh = 3 if i < 2 else 2
            v3 = at[0:TS, 0:nh * 129].rearrange("p (h v) -> p h v", h=nh)
            rd = self.sm8[0:TS, 0:nh]
            T.op("dve", lambda e, v3=v3, rd=rd: e.reciprocal(rd, v3[:, :, 128]), [ab], self.B("sm8"))
            T.op("dve", lambda e, v3=v3, rd=rd, i=i, nh=nh: e.tensor_tensor(
                self.ytm[0:TS, i * 384:i * 384 + nh * 128].rearrange("p (h v) -> p h v", h=nh), v3[:, :, 0:128],
                rd.unsqueeze(2).broadcast_to([TS, nh, 128]), OP.mult), [ab] + self.B("sm8"), self.B("ytm"))
        self.transposes(self.ytm, self.bufs["ytm"], 8, TS, self.yT["a"], self.bufs["yT_a"])
        self.attn_active = False

    def merge_out(self, l, TS):
        T = self.T
        tA, tAb = self.tmpA, self.bufs["tmpA"]
        mg = self.gbuf
        for cb in range(4):
            for bi, (br, goff, wup) in enumerate((("m", C_GM, self.w_up_m), ("a", C_GA, self.w_up_a), ("r", C_GR, self.w_up_r))):
                hold = {}

                def hg(ps, pb):
                    T.op("act", lambda e: e.activation(self.sg[0:TS, :], ps, AF.Sigmoid), [pb], self.B("sg"))
                self.gemm(self.hT, self.bufs["hT"], 16, TS, self.w_in[l], (goff + cb * 512, 512), [(0, 512, hg)], bias_off=l * DIN + goff + cb * 512)

                def hu(ps, pb, bi=bi, cb=cb):
                    dst = tA[0:TS, cb * 512:(cb + 1) * 512]
                    if bi == 0:
                        T.op("dve", lambda e: e.tensor_tensor(dst, ps, self.sg[0:TS, :], OP.mult), [pb] + self.B("sg"), [tAb])
                    else:
                        T.op("dve", lambda e: e.tensor_tensor(self.tmpB[0:TS, :], ps, self.sg[0:TS, :], OP.mult), [pb] + self.B("sg"), self.B("tmpB"))
                        T.op("dve", lambda e: e.tensor_tensor(dst, dst, self.tmpB[0:TS, :], OP.add), [tAb] + self.B("tmpB"), [tAb])
                self.gemm(self.yT[br], self.bufs["yT_" + br], 8, TS, wup[l], (cb * 512, 512), [(0, 512, hu)])
        T.op("act", lambda e: e.activation(self.hb[0:TS, :], tA[0:TS, :], AF.Identity), [tAb], self.B("hb"))
        self.transposes(self.hb, self.bufs["hb"], 16, TS, self.hT, self.bufs["hT"])
        for cb in range(4):
            def ho(ps, pb, cb=cb):
                self.copy(tA[0:TS, cb * 512:(cb + 1) * 512], ps, [pb], [tAb])
            self.gemm(self.hT, self.bufs["hT"], 16, TS, self.w_o[l], (cb * 512, 512), [(0, 512, ho)])
        self.resid_add(l, TS, self.g_mix_post[l])

    def ffn(self, l, TS):
        T = self.T
        tA, tAb = self.tmpA, self.bufs["tmpA"]
        self.norm_to_hT(l, TS, self.g_ffn_pre[l])
        for j in range(11):
            def hg(ps, pb):
                T.op("act", lambda e: e.activation(self.sg[0:TS, :], ps, AF.Silu), [pb], self.B("sg"))
            self.gemm(self.hT, self.bufs["hT"], 16, TS, self.w_gu[l], (DFF + j * 512, 512), [(0, 512, hg)])

            def ha(ps, pb, j=j):
                jj = j if j < 6 else j - 6
                T.op("dve", lambda e: e.tensor_tensor(self.act_tm[0:TS, jj * 512:(jj + 1) * 512], ps, self.sg[0:TS, :], OP.mult), [pb] + self.B("sg"), self.B("act_tm"))
            self.gemm(self.hT, self.bufs["hT"], 16, TS, self.w_gu[l], (j * 512, 512), [(0, 512, ha)])
            if j == 5:
                self.transposes(self.act_tm, self.bufs["act_tm"], 24, TS, self.actT, self.bufs["actT"])
            if j == 10:
                self.transposes(self.act_tm, self.bufs["act_tm"], 20, TS, self.actT, self.bufs["actT"], dst_k0=24)
        for cb in range(4):
            def ho(ps, pb, cb=cb):
                self.copy(tA[0:TS, cb * 512:(cb + 1) * 512], ps, [pb], [tAb])
            self.gemm(self.actT, self.bufs["actT"], 44, TS, self.w_down[l], (cb * 512, 512), [(0, 512, ho)])
        self.resid_add(l, TS, self.g_ffn_post[l])

    def merge2(self, l, TS):
        T = self.T
        Bf = self.bufs
        sets = []
        for i, (tA, tAb, sg, sgb) in enumerate(((self.tmpA, Bf["tmpA"], self.sg, Bf["sg"]), (self.tmpA_B, Bf["tmpA_B"], self.tmpB, Bf["tmpB"]))):
            tl = self.tiles_[i]
            sets.append(dict(hT=tl["hT"][0], hTb=tl["hT"][1], yT=tl["yT"], tA=tA, tAb=tAb, sg=sg, sgb=sgb))
        tmp = self.sqf[:, :, :].rearrange("p a b -> p (a b)")[:, 0:512]
        tmpb = Bf["sq"]
        xs = [(st["hT"], st["hTb"]) for st in sets]
        self.defer_on = True
        for cb in range(4):
            for bi, (br, goff, wup) in enumerate((("m", C_GM, self.w_up_m), ("a", C_GA, self.w_up_a), ("r", C_GR, self.w_up_r))):
                def hg(ps, pb, xi):
                    st = sets[xi]
                    T.op("act", lambda e: e.activation(st["sg"][0:TS, :], ps, AF.Sigmoid), [pb], [st["sgb"]])
                self.gemm(xs, None, 16, TS, self.w_in[l], (goff + cb * 512, 512), [(0, 512, hg)], bias_off=l * DIN + goff + cb * 512)

                def hu(ps, pb, xi, bi=bi, cb=cb):
                    st = sets[xi]
                    dst = st["tA"][0:TS, cb * 512:(cb + 1) * 512]
                    if bi == 0:
                        T.op("dve", lambda e: e.tensor_tensor(dst, ps, st["sg"][0:TS, :], OP.mult), [pb, st["sgb"]], [st["tAb"]])
                    else:
                        T.op("dve", lambda e: e.tensor_tensor(tmp[0:TS, :], ps, st["sg"][0:TS, :], OP.mult), [pb, st["sgb"]], [tmpb])
                        T.op("dve", lambda e: e.tensor_tensor(dst, dst, tmp[0:TS, :], OP.add), [st["tAb"], tmpb], [st["tAb"]])
                self.gemm([(st["yT"][br][0], st["yT"][br][1]) for st in sets], None, 8, TS, wup[l], (cb * 512, 512), [(0, 512, hu)])
        self.flush()
        for st in sets:
            T.op("act", lambda e, st=st: e.activation(self.hb[0:TS, :], st["tA"][0:TS, :], AF.Identity), [st["tAb"]], self.B("hb"))
            self.transposes(self.hb, Bf["hb"], 16, TS, st["hT"], st["hTb"])
        for cb in range(4):
            def ho(ps, pb, xi, cb=cb):
                st = sets[xi]
                self.copy(st["tA"][0:TS, cb * 512:(cb + 1) * 512], ps, [pb], [st["tAb"]])
            self.gemm(xs, None, 16, TS, self.w_o[l], (cb * 512, 512), [(0, 512, ho)])
        self.flush()
        self.defer_on = False
        for i in range(2):
            self.set_x(i)
            self.resid_add(l, TS, self.g_mix_post[l], src=(sets[i]["tA"], sets[i]["tAb"]))

    def ffn2(self, l, TS):
        T = self.T
        Bf = self.bufs
        sets = [
            dict(hT=self.hT, hTb=Bf["hT"], act=self.act_tm, actb=Bf["act_tm"], actT=self.actT, actTb=Bf["actT"], tA=self.tmpA, tAb=Bf["tmpA"], sg=self.sg, sgb=Bf["sg"]),
            dict(hT=self.hT_B, hTb=Bf["hT_B"], act=self.act_tm_B, actb=Bf["act_tm_B"], actT=self.actT_B, actTb=Bf["actT_B"], tA=self.tmpA_B, tAb=Bf["tmpA_B"], sg=self.tmpB, sgb=Bf["tmpB"]),
        ]
        for i in range(2):
            self.set_x(i)
            self.norm_to_hT(l, TS, self.g_ffn_pre[l], dst=(sets[i]["hT"], sets[i]["hTb"]))
        xs = [(st["hT"], st["hTb"]) for st in sets]
        self.defer_on = True
        for j in range(11):
            def hg(ps, pb, xi):
                st = sets[xi]
                T.op("act", lambda e: e.activation(st["sg"][0:TS, :], ps, AF.Silu), [pb], [st["sgb"]])
            self.gemm(xs, None, 16, TS, self.w_gu[l], (DFF + j * 512, 512), [(0, 512, hg)])

            def ha(ps, pb, xi, j=j):
                st = sets[xi]
                jj = j if j < 6 else j - 6
                T.op("dve", lambda e: e.tensor_tensor(st["act"][0:TS, jj * 512:(jj + 1) * 512], ps, st["sg"][0:TS, :], OP.mult), [pb, st["sgb"]], [st["actb"]])
            self.gemm(xs, None, 16, TS, self.w_gu[l], (j * 512, 512), [(0, 512, ha)])
            if j == 5:
                self.flush()
                for st in sets:
                    self.transposes(st["act"], st["actb"], 24, TS, st["actT"], st["actTb"])
            if j == 10:
                self.flush()
                for st in sets:
                    self.transposes(st["act"], st["actb"], 20, TS, st["actT"], st["actTb"], dst_k0=24)
        xs2 = [(st["actT"], st["actTb"]) for st in sets]
        for cb in range(4):
            def ho(ps, pb, xi, cb=cb):
                st = sets[xi]
                self.copy(st["tA"][0:TS, cb * 512:(cb + 1) * 512], ps, [pb], [st["tAb"]])
            self.gemm(xs2, None, 44, TS, self.w_down[l], (cb * 512, 512), [(0, 512, ho)])
        self.flush()
        self.defer_on = False
        for i in range(2):
            self.set_x(i)
            self.resid_add(l, TS, self.g_ffn_post[l], src=(sets[i]["tA"], sets[i]["tAb"]))

    def init_states(self, seq):
        T = self.T
        for l in range(self.nl):
            cnB, cnbB, mbcB = self.bufs["cn%d" % l], self.bufs["cnb%d" % l], self.bufs["mbc%d" % l]
            rsB, rbB = self.bufs["rs%d" % l], self.bufs["rb%d" % l]
            if seq.name == "p":
                T.op("dve", lambda e, l=l: e.memset(self.cn[l][:, :, :], 0.0), [], [cnB])
                T.op("dve", lambda e, l=l: e.memset(self.cnb[l][:, :, :], 0.0), [], [cnbB])
                T.op("dve", lambda e, l=l: e.memset(self.mbc[l][:, :], 0.0), [], [mbcB])
                T.op("dve", lambda e, l=l: e.memset(self.rs[l][:, :, :], 0.0), [], [rsB])
                T.op("dve", lambda e, l=l: e.memset(self.rb[l][:, :, :], 0.0), [], [rbB])
            else:
                for h in range(4):
                    T.dma("sp", self.stg[:, :, :], self.sc[l, h].rearrange("(a p) k -> p a k", p=128), self.bufs["stg"], [], self.B("stg"))
                    pt, pb = self.bank()
                    for a_ in range(2):
                        T.op("pe", lambda e, a_=a_: e.transpose(pt[:, a_ * 128:(a_ + 1) * 128], self.stg[:, a_, :], self.ident), self.B("stg", "c32"), [pb])
                    T.op("dve", lambda e, l=l, h=h: e.tensor_copy(self.cn[l][:, h, 0:256], pt[:, 0:256]), [pb], [cnB])
                T.dma("sp", self.cn[l][:, :, 256], self.sn[l].rearrange("h k -> k h"), cnB, [], [cnB], allow_slow_non_contiguous=True)
                T.dma("sp", self.mbc[l][:, :], self.sm[l].partition_broadcast(128), mbcB, [], [mbcB])
                T.dma("sp", self.rs[l][:, :, :], self.sr[l].rearrange("h k v -> k h v"), rsB, [], [rsB])
                T.op("act", lambda e, l=l: e.activation(self.cnb[l][:, :, :], self.cn[l][:, :, :], AF.Identity), [cnB], [cnbB])
                T.op("act", lambda e, l=l: e.activation(self.rb[l][:, :, :], self.rs[l][:, :, :], AF.Identity), [rsB], [rbB])

    def store_states(self, seq):
        T = self.T
        oc, on, om, orr = seq.o_states
        for l in range(self.nl):
            cnB, mbcB, rsB = self.bufs["cn%d" % l], self.bufs["mbc%d" % l], self.bufs["rs%d" % l]
            for h in range(4):
                pt, pb = self.bank()
                for a_ in range(2):
                    T.op("pe", lambda e, a_=a_, l=l, h=h: e.transpose(pt[:, a_ * 128:(a_ + 1) * 128], self.cn[l][:, h, a_ * 128:(a_ + 1) * 128], self.ident), [cnB] + self.B("c32"), [pb])
                T.op("dve", lambda e: e.tensor_copy(self.stg[:, :, :].rearrange("p a k -> p (a k)"), pt[:, 0:256]), [pb], self.B("stg"))
                T.dma("sp", oc[l, h].rearrange("(a p) k -> p a k", p=128), self.stg[:, :, :], self.bufs["stg"], self.B("stg"), [])
            T.dma("sp", on[l].rearrange("h k -> k h"), self.cn[l][:, :, 256], cnB, [cnB], [], allow_slow_non_contiguous=True)
            T.dma("sp", om[l:l + 1, :], self.mbc[l][0:1, :], mbcB, [mbcB], [])
            T.dma("sp", orr[l].rearrange("h k v -> k h v"), self.rs[l][:, :, :], rsB, [rsB], [])

    def past_kv(self, seq):
        T = self.T
        for l in range(self.nl):
            for kt in range(8):
                T.dma("sp", self.ckv_o[:, :], self.cckv[l, kt * 128:(kt + 1) * 128, :], self.bufs["ckv_o"], [], self.B("ckv_o"))
                T.op("act", lambda e: e.activation(self.hb[:, 0:512], self.ckv_o[:, :], AF.Identity), self.B("ckv_o"), self.B("hb"))
                self.transposes(self.hb, self.bufs["hb"], 4, 128, self.ckvT, self.bufs["ckvT"])
                T.dma("sp", self.kr_o[:, :], self.ckr[l, kt * 128:(kt + 1) * 128, :], self.bufs["kr_o"], [], self.B("kr_o"))
                T.op("act", lambda e: e.activation(self.krd[:, 0:64], self.kr_o[:, :], AF.Identity), self.B("kr_o"), self.B("krd"))
                T.op("act", lambda e: e.activation(self.krd[:, 64:128], self.kr_o[:, :], AF.Identity), self.B("kr_o"), self.B("krd"))
                self.transposes(self.krd, self.bufs["krd"], 1, 128, self.krT[:, :].rearrange("p (o t) -> p o t", o=1), self.bufs["krT"])
                self.kv_up(l, 128, "s", kt)

    def mixer_core(self, seq, t, l, TS):
        self.norm_to_hT(l, TS, self.g_mix_pre[l])
        self.stage_in(seq, t, l, TS)
        self.mlstm(l, TS)
        self.retention(l, TS)
        self.mla(seq, t, l, TS)

    def run_seq(self, seq):
        T = self.T
        self.init_states(seq)
        self.set_tile(1)
        if seq.name == "s":
            self.past_kv(seq)
        TS = seq.TS
        t = 0
        while t < seq.ntiles:
            pair = (TS == 128 and t + 1 < seq.ntiles)
            tiles = [t, t + 1] if pair else [t]
            for i, tt in enumerate(tiles):
                self.set_x(i)
                T.dma("sp", self.x[0:TS, :], seq.x_in[tt * 128:tt * 128 + TS, :], self.bufs["x"], [], self.B("x"))
            for l in range(self.nl):
                self.load_layer_consts(l)
                if pair:
                    for i, tt in enumerate(tiles):
                        self.set_x(i)
                        self.set_tile(i)
                        self.mixer_core(seq, tt, l, TS)
                    self.set_tile(1)
                    self.barrier()
                    self.merge2(l, TS)
                    self.barrier()
                    self.ffn2(l, TS)
                    self.barrier()
                else:
                    self.set_x(0)
                    self.set_tile(1)
                    self.mixer_core(seq, t, l, TS)
                    self.merge_out(l, TS)
                    self.ffn(l, TS)
            for i, tt in enumerate(tiles):
                self.set_x(i)
                T.dma("sp", seq.y_out[tt * 128:tt * 128 + TS, :], self.x[0:TS, :], self.bufs["x"], self.B("x"), [])
            t += len(tiles)
        self.set_x(0)
        self.set_tile(1)
        self.store_states(seq)

    def build(self):
        self.prologue()
        seqs = []
        p = Seq()
        p.name, p.ntiles, p.TS, p.tab0, p.kt0, p.causal = "p", self.S_P // 128, 128, 0, 0, True
        p.x_in, p.y_out, p.o_ckv, p.o_kr = self.xp, self.yp, self.o_pckv, self.o_pkr
        p.o_states = (self.o_pc, self.o_pn, self.o_pm, self.o_pr)
        seqs.append(p)
        if self.do_sample:
            s = Seq()
            s.name, s.ntiles, s.TS, s.tab0, s.kt0, s.causal = "s", 1, 64, self.S_P, 8, False
            s.x_in, s.y_out, s.o_ckv, s.o_kr = self.xs, self.ys, self.o_sckv, self.o_skr
            s.o_states = (self.o_sc, self.o_sn, self.o_sm, self.o_sr)
            seqs.append(s)
        for s in seqs:
            self.run_seq(s)
        self.T.wait_all("sp", list(Buf.REG))
        return self.nc


def const_tables(S_P):
    pos = np.concatenate([np.arange(S_P), PAST + np.arange(SS)]).astype(np.float32)

    def tabs(d):
        inv = (1.0 / (10000.0 ** (np.arange(0, d, 2, dtype=np.float32) / np.float32(d)))).astype(np.float32)
        ang = pos[:, None] * inv[None, :]
        return np.cos(ang).astype(np.float32), np.sin(ang).astype(np.float32)
    c128, s128 = tabs(128)
    c64, s64 = tabs(64)
    i = np.arange(128)
    ident = np.eye(128, dtype=np.float32)
    tri = (i[:, None] <= i[None, :]).astype(np.float32)
    negmask = np.where(i[None, :] <= i[:, None], 0.0, NEG).astype(np.float32)
    negmaskT = negmask.T.copy()
    sel = np.zeros((128, 128), np.float32)
    sel[127, :] = 1.0
    sel[63, :] = 1.0
    lg = np.log1p(-np.exp2(-5.0 - np.arange(4, dtype=np.float64)))
    diff = (i[None, :] - i[:, None]).astype(np.float64)
    DT = np.stack([np.where(diff >= 0, np.exp(np.maximum(diff, 0) * lg[h]), 0.0) for h in range(4)], 1)
    xi = np.exp((i[:, None] + 1.0) * lg[None, :])
    z128 = np.exp((127.0 - i)[:, None] * lg[None, :])
    z64 = np.exp((63.0 - i)[:, None] * lg[None, :])
    d128 = np.broadcast_to(np.exp(128 * lg)[None, :], (128, 4))
    d64 = np.broadcast_to(np.exp(64 * lg)[None, :], (128, 4))
    sel128 = np.zeros((128, 128), np.float32)
    sel128[127, :] = 1.0
    sel64 = np.zeros((128, 128), np.float32)
    sel64[63, :] = 1.0
    c32 = np.concatenate([ident, tri, negmask, negmaskT, sel128, DT.reshape(128, 512), xi, z128, z64, d128, d64, sel64], 1).astype(np.float32)
    return c128, s128, c64, s64, c32


_CACHE = {}


def get_nc(S_P, do_sample=True):
    key = (S_P, do_sample)
    if key not in _CACHE:
        b = Builder(S_P, do_sample)
        _CACHE[key] = b.build()
    return _CACHE[key]


W_NAMES = ["g_mix_pre", "w_in", "b_in", "g_mlstm", "w_up_m", "g_qa", "w_uq", "g_kva", "w_ukv", "w_up_a", "g_ret",
           "w_up_r", "w_o", "g_mix_post", "g_ffn_pre", "w_gu", "w_down", "g_ffn_post"]


def kernel(**inp):
    x_prompt = np.asarray(inp["x_prompt"], np.float32)
    Bn, S_P, _ = x_prompt.shape
    nc = get_nc(S_P)
    c128, s128, c64, s64, c32 = const_tables(S_P)
    shared = {k: np.ascontiguousarray(np.asarray(inp[k], np.float32)) for k in W_NAMES}
    shared.update(t_cos128=c128, t_sin128=s128, t_cos64=c64, t_sin64=s64, t_c32=c32)
    in_maps = []
    for b in range(Bn):
        m = dict(shared)
        m["xp"] = np.ascontiguousarray(x_prompt[b])
        m["xs"] = np.ascontiguousarray(np.asarray(inp["x_sample"], np.float32)[b])
        m["cckv"] = np.ascontiguousarray(np.asarray(inp["cache_mla_ckv"], np.float32)[:, b])
        m["ckr"] = np.ascontiguousarray(np.asarray(inp["cache_mla_krope"], np.float32)[:, b])
        m["sc"] = np.ascontiguousarray(np.asarray(inp["state_mlstm_c"], np.float32)[:, b])
        m["sn"] = np.ascontiguousarray(np.asarray(inp["state_mlstm_n"], np.float32)[:, b])
        m["sm"] = np.ascontiguousarray(np.asarray(inp["state_mlstm_m"], np.float32)[:, b])
        m["sr"] = np.ascontiguousarray(np.asarray(inp["state_ret"], np.float32)[:, b])
        in_maps.append(m)
    res = run_bass_kernel_spmd(nc, in_maps, core_ids=list(range(Bn)))
    R = res.results

    def st(k, axis):
        return np.stack([np.asarray(r[k], np.float32) for r in R], axis)
    return (st("yp", 0), st("ys", 0), st("o_pckv", 1), st("o_pkr", 1), st("o_pc", 1), st("o_pn", 1), st("o_pm", 1), st("o_pr", 1),
            st("o_sckv", 1), st("o_skr", 1), st("o_sc", 1), st("o_sn", 1), st("o_sm", 1), st("o_sr", 1))
```

```python
import numpy as np
import concourse.bass as bass
import concourse.mybir as mybir
from concourse.bass_utils import run_bass_kernel_spmd

F32 = mybir.dt.float32
BF16 = mybir.dt.bfloat16
AF = mybir.ActivationFunctionType
OP = mybir.AluOpType
AX = mybir.AxisListType

D = 2048
DIN = 13384
DFF = 5632
NL = 2
PAST = 1024
SS = 64
EPS = 1e-6
NEG = -1.0e30
C_MQ, C_MK, C_MV, C_MO, C_MI, C_ADQ, C_ADKV, C_AKR, C_RQ, C_RK, C_RV, C_RG, C_GM, C_GA, C_GR = (
    0, 512, 1024, 2048, 3072, 3080, 3592, 4104, 4168, 4680, 5192, 6216, 7240, 9288, 11336)
SLOT_ELEMS = 4352
NWG = 700
NSLOT = 4
PIPE_DEPTH = 1
ARENA_E = 29664


class Buf:
    __slots__ = ("name", "lw", "rd", "dsem", "dcnt", "lw_dma")

    REG = []

    def __init__(self, name):
        Buf.REG.append(self)
        self.name = name
        self.lw = None
        self.rd = {}
        self.dsem = None
        self.dcnt = 0
        self.lw_dma = False


class Tracker:
    def __init__(self, nc):
        self.nc = nc
        self.eng = {"pe": nc.tensor, "act": nc.scalar, "dve": nc.vector, "pool": nc.gpsimd, "sp": nc.sync}
        self.sem = {k: nc.alloc_semaphore("e_" + k) for k in ("pe", "act", "dve", "pool")}
        self.cnt = {k: 0 for k in self.sem}
        self.waited = {k: {} for k in self.eng}
        self.nsem = 4

    def _need(self, q, reads, writes, is_dma):
        need = {}

        def add(sv):
            if sv is None:
                return
            s, v = sv
            k = id(s)
            if k not in need or need[k][1] < v:
                need[k] = (s, v)

        for b in reads:
            add(b.lw)
        for b in writes:
            if not (is_dma and b.lw_dma and not b.rd):
                add(b.lw)
            for sv in b.rd.values():
                add(sv)
        E = self.eng[q]
        wd = self.waited[q]
        for k, (s, v) in need.items():
            if q == "pe" and s is self.sem["pe"]:
                continue
            if wd.get(k, 0) >= v:
                continue
            E.wait_ge(s, v)
            wd[k] = v

    def op(self, q, fn, reads=(), writes=()):
        self._need(q, reads, writes, False)
        ins = fn(self.eng[q])
        self.cnt[q] += 1
        s = self.sem[q]
        ins.then_inc(s, 1)
        sv = (s, self.cnt[q])
        for b in reads:
            b.rd[id(s)] = sv
        for b in writes:
            b.lw = sv
            b.rd = {}
            b.lw_dma = False
        return ins

    def dma(self, q, out, in_, sb, reads=(), writes=(), **kw):
        self._need(q, reads, writes, True)
        if sb.dsem is None:
            sb.dsem = self.nc.alloc_semaphore("d_" + sb.name)
            self.nsem += 1
        ins = self.eng[q].dma_start(out=out, in_=in_, **kw)
        sb.dcnt += 16
        ins.then_inc(sb.dsem, 16)
        sv = (sb.dsem, sb.dcnt)
        for b in reads:
            b.rd[id(sb.dsem)] = sv
        for b in writes:
            keep = b.lw_dma and not b.rd
            b.lw = sv
            if not keep:
                b.rd = {}
            b.lw_dma = True
        return ins

    def wait_all(self, q, bufs):
        self._need(q, bufs, bufs, False)


class Seq:
    pass


def CK(fn, key):
    fn.ckey = key
    return fn


class Builder:
    def __init__(self, S_P, do_sample=True, nl=NL):
        self.S_P = S_P
        self.do_sample = do_sample
        self.nl = nl
        Buf.REG = []
        nc = self.nc = bass.Bass("TRN2", target_bir_lowering=False)
        self.T = Tracker(nc)
        self.bufs = {}
        self.rr = 0
        self._decl()
        self._alloc()

    def din(self, name, shape, dt=F32):
        return self.nc.dram_tensor(name, list(shape), dt, kind="ExternalInput").ap()

    def dout(self, name, shape):
        return self.nc.dram_tensor(name, list(shape), F32, kind="ExternalOutput").ap()

    def _decl(self):
        S_P = self.S_P
        nc = self.nc
        self.xp = self.din("xp", [S_P, D])
        self.xs = self.din("xs", [SS, D])
        self.cckv = self.din("cckv", [NL, PAST, 512])
        self.ckr = self.din("ckr", [NL, PAST, 64])
        self.sc = self.din("sc", [NL, 4, 256, 128])
        self.sn = self.din("sn", [NL, 4, 128])
        self.sm = self.din("sm", [NL, 4])
        self.sr = self.din("sr", [NL, 4, 128, 256])
        self.g_mix_pre = self.din("g_mix_pre", [NL, D])
        self.w_in = self.din("w_in", [NL, D, DIN])
        self.b_in = self.din("b_in", [NL, DIN])
        self.g_mlstm = self.din("g_mlstm", [NL, 1024])
        self.w_up_m = self.din("w_up_m", [NL, 1024, D])
        self.g_qa = self.din("g_qa", [NL, 512])
        self.w_uq = self.din("w_uq", [NL, 512, 1536])
        self.g_kva = self.din("g_kva", [NL, 512])
        self.w_ukv = self.din("w_ukv", [NL, 512, 2048])
        self.w_up_a = self.din("w_up_a", [NL, 1024, D])
        self.g_ret = self.din("g_ret", [NL, 1024])
        self.w_up_r = self.din("w_up_r", [NL, 1024, D])
        self.w_o = self.din("w_o", [NL, D, D])
        self.g_mix_post = self.din("g_mix_post", [NL, D])
        self.g_ffn_pre = self.din("g_ffn_pre", [NL, D])
        self.w_gu = self.din("w_gu", [NL, D, 2 * DFF])
        self.w_down = self.din("w_down", [NL, DFF, D])
        self.g_ffn_post = self.din("g_ffn_post", [NL, D])
        NP = S_P + SS
        self.t_cos128 = self.din("t_cos128", [NP, 64])
        self.t_sin128 = self.din("t_sin128", [NP, 64])
        self.t_cos64 = self.din("t_cos64", [NP, 32])
        self.t_sin64 = self.din("t_sin64", [NP, 32])
        self.t_c32 = self.din("t_c32", [128, 1300])
        self.yp = self.dout("yp", [S_P, D])
        self.ys = self.dout("ys", [SS, D])
        self.o_pckv = self.dout("o_pckv", [NL, S_P, 512])
        self.o_pkr = self.dout("o_pkr", [NL, S_P, 64])
        self.o_pc = self.dout("o_pc", [NL, 4, 256, 128])
        self.o_pn = self.dout("o_pn", [NL, 4, 128])
        self.o_pm = self.dout("o_pm", [NL, 4])
        self.o_pr = self.dout("o_pr", [NL, 4, 128, 256])
        self.o_sckv = self.dout("o_sckv", [NL, SS, 512])
        self.o_skr = self.dout("o_skr", [NL, SS, 64])
        self.o_sc = self.dout("o_sc", [NL, 4, 256, 128])
        self.o_sn = self.dout("o_sn", [NL, 4, 128])
        self.o_sm = self.dout("o_sm", [NL, 4])
        self.o_sr = self.dout("o_sr", [NL, 4, 128, 256])
        self.bhl = nc.dram_tensor("bhl", [2, NL * DIN], BF16).ap()
        self.wscr_t = [nc.dram_tensor("wscr%d" % i, [100, 128, SLOT_ELEMS], BF16).ap() for i in range((NWG + 99) // 100)]
        self.wcache = {}
        nktp = S_P // 128
        self.kv = {}
        for nm, nkt in (("p", nktp), ("s", 9)):
            self.kv[nm] = (
                nc.dram_tensor("kn_" + nm, [NL, nkt, 128, 1024], BF16).ap(),
                nc.dram_tensor("kr_" + nm, [NL, nkt, 128, 128], BF16).ap(),
                nc.dram_tensor("vv_" + nm, [NL, nkt, 128, 8 * 129], BF16).ap(),
            )

    def sb(self, name, shape, dt=F32):
        t = self.nc.alloc_sbuf_tensor(name, list(shape), dt)
        self.bufs[name] = Buf(name)
        return t

    def B(self, *names):
        return [self.bufs[n] for n in names]

    def _alloc(self):
        nc = self.nc
        sb = self.sb
        self.arena = nc.alloc_sbuf_tensor("arena", [128, ARENA_E], BF16)
        self._ao = 0

        def carve(name, shape, dt, mkbuf=True):
            n = 1
            for d_ in shape[1:]:
                n *= d_
            nb = n * (4 if dt == F32 else 2)
            o = self._ao
            self._ao += (nb + 31) // 32 * 32
            assert self._ao <= ARENA_E * 2, (name, self._ao)
            v = self.arena[:, o // 2:o // 2 + nb // 2]
            if dt == F32:
                v = v.bitcast(F32)
            if len(shape) == 3:
                v = v.rearrange("p (a b) -> p a b", a=shape[1])
            if mkbuf:
                self.bufs[name] = Buf(name)
            return v
        self.carve = carve
        self.c32 = sb("c32", [128, 1300])
        self.identb = sb("identb", [128, 128], BF16)
        self.ones2 = sb("ones2", [2, 128], BF16)
        self.onesf = sb("onesf", [128, 128])
        self.epsc = sb("epsc", [128, 1])
        self.x = sb("x", [128, D])
        self.gbuf = sb("gbuf", [128, D])
        self.gsm = sb("gsm", [128, 3, 1024])
        self.tmpA = sb("tmpA", [128, D])
        self.hb = sb("hb", [128, D], BF16)
        self.hT = sb("hT", [128, 16, 128], BF16)
        self.wslot = [sb("ws%d" % i, [128, SLOT_ELEMS], BF16) for i in range(NSLOT)]
        self.wbias = [nc.alloc_sbuf_tensor("wb%d" % i, [2, 520], BF16) for i in range(NSLOT)]
        self.wsi = 0
        self.rt128 = sb("rt128", [128, 2, 64])
        self.rt64 = sb("rt64", [128, 2, 32])
        self.qmT = None
        self.kmT = None
        self.km_tm = None
        self.vm = sb("vm", [128, 4, 257], BF16)
        self.gs_m = carve("gs_m", [128, 4, 256], F32)
        self.igf = sb("igf", [128, 8])
        self.hqT = None
        self.ckvT = None
        self.ckv_o = sb("ckv_o", [128, 512])
        self.kr_o = sb("kr_o", [128, 64])
        self.krd = sb("krd", [128, 128], BF16)
        self.krT = sb("krT", [128, 128], BF16)
        self.rqT = None
        self.rkT = None
        self.rk_tm = None
        self.vr = None
        self.gs_r = carve("gs_r", [128, 4, 256], F32)
        self.QnT = carve("QnT", [128, 8, 128], BF16)
        self.QrT = carve("QrT", [128, 4, 128], BF16)
        self.KnT = carve("KnT", [128, 8, 128], BF16)
        self.Vst = sb("Vst", [128, 8, 129], BF16)
        self.kvs = []
        for i in range(3):
            k = carve("kvk%d" % i, [128, 8, 128], BF16, False)
            r = carve("kvr%d" % i, [128, 128], BF16, False)
            v = carve("kvv%d" % i, [128, 8, 129], BF16, False)
            self.bufs["kvs%d" % i] = Buf("kvs%d" % i)
            self.kvs.append((k, r, v, self.bufs["kvs%d" % i]))
        self.kvi = 0
        self.PT = carve("PT", [128, 8, 128], BF16)
        self.PT2 = sb("PT2", [128, 8, 128], BF16)
        self.ytm = carve("ytm", [128, 1024], BF16)
        self.sm4 = {n: sb("s4_" + n, [128, 4]) for n in
                    ("b", "a", "cm", "mx", "nmx", "ps", "emt", "den", "rden", "ss", "sc", "ws", "mean", "t1", "t2")}
        self.sm8 = sb("sm8", [128, 8])
        self.bc8 = sb("bc8", [128, 8])
        self.carry = sb("carry", [128, 4])
        self.diag = carve("diag", [128, 4, 128], F32)
        self.Am = carve("Am", [128, 4, 128], F32)
        self.dmT = carve("dmT", [128, 4, 128], F32)
        self.wT = None
        self.hsum = carve("hsum", [128, 4, 257], F32)
        self.sqf = sb("sq", [128, 4, 257])
        self.inter = self.sqf
        self.bufs["inter"] = self.bufs["sq"]
        self.sq = self.sqf[:, :, 0:256]
        self.kws = None
        self.cn = [sb("cn%d" % l, [128, 4, 257]) for l in range(NL)]
        self.cnb = [sb("cnb%d" % l, [128, 4, 257], BF16) for l in range(NL)]
        self.mbc = [sb("mbc%d" % l, [128, 4]) for l in range(NL)]
        self.rs = [sb("rs%d" % l, [128, 4, 256]) for l in range(NL)]
        self.rb = [sb("rb%d" % l, [128, 4, 256], BF16) for l in range(NL)]
        self.stg = sb("stg", [128, 2, 128])
        self.sg = sb("sg", [128, 512])
        self.tmpB = sb("tmpB", [128, 512])

        for nm_, shp_ in (("qmT", [128, 4, 128]), ("kmT", [128, 4, 128]), ("km_tm", [128, 4, 128]), ("hqT", [128, 4, 128]), ("ckvT", [128, 4, 128]),
                          ("rqT", [128, 4, 128]), ("rkT", [128, 4, 128]), ("rk_tm", [128, 4, 128]), ("vr", [128, 4, 256]), ("wT", [128, 4, 128]), ("kws", [128, 4, 128])):
            setattr(self, nm_, carve(nm_, shp_, BF16))
        self.yT = {b: carve("yT_" + b, [128, 8, 128], BF16) for b in "mar"}
        ao_ = self._ao
        self._ao = 35840
        self.actT = carve("actT", [128, 44, 128], BF16)
        assert self._ao <= ao_ - 6144, (self._ao, ao_)
        self._ao = ao_
        self.hT2 = sb("hT2", [128, 16, 128], BF16)
        self.yT2 = {b: sb("yT2_" + b, [128, 8, 128], BF16) for b in "mar"}
        self.tiles_ = [
            dict(hT=(self.hT2, self.bufs["hT2"]), yT={b: (self.yT2[b], self.bufs["yT2_" + b]) for b in "mar"}),
            dict(hT=(self.hT, self.bufs["hT"]), yT={b: (self.yT[b], self.bufs["yT_" + b]) for b in "mar"}),
        ]
        self.bprep = self.tmpA[0:16, 0:1673]
        self.bprep2 = self.gbuf[0:16, 0:1673]
        self.bph = self.hb[0:16, 0:1673]
        self.bpl = self.actT[:, :, :].rearrange("p a b -> p (a b)")[0:16, 0:1673]
        for a_, b_ in (("bprep", "tmpA"), ("bprep2", "gbuf"), ("bph", "hb"), ("bpl", "actT")):
            self.bufs[a_] = self.bufs[b_]
        print("SBUF bytes remaining", nc.sbuf_bytes_remaining)
        ao = self._ao
        self._ao = 0
        self.hT_B = carve("hT_B", [128, 16, 128], BF16)
        self.act_tm_B = carve("act_tm_B", [128, 3072], BF16)
        self.actT_B = carve("actT_B", [128, 44, 128], BF16)
        self.tmpA_B = carve("tmpA_B", [128, D], F32)
        self.act_tm = carve("act_tm", [128, 3072], BF16)
        assert self._ao <= ao, (self._ao, ao)
        self.x2 = sb("x2", [128, D])
        self.xb = [(self.x, self.bufs["x"]), (self.x2, self.bufs["x2"])]
        print("SBUF bytes remaining (final)", nc.sbuf_bytes_remaining)
        self.ps = []
        for i in range(8):
            t = nc.alloc_psum_tensor("ps%d" % i, [128, 512], F32)
            self.bufs["ps%d" % i] = Buf("ps%d" % i)
            self.ps.append((t, self.bufs["ps%d" % i]))
        self.psi = 0
        self.attn_active = False
        self.held = set()
        self.pending = []
        self.defer_on = False

    def bank(self):
        nb = 5 if self.attn_active else 8
        while (self.psi % nb) in self.held:
            self.psi += 1
        t, b = self.ps[self.psi % nb]
        self.psi += 1
        return t, b

    def bank_i(self):
        nb = 5 if self.attn_active else 8
        while (self.psi % nb) in self.held:
            self.psi += 1
        i = self.psi % nb
        self.psi += 1
        return i

    def flush(self):
        while self.pending:
            self.pending.pop(0)()

    def set_x(self, i):
        self.x, self.bufs["x"] = self.xb[i]

    def set_tile(self, i):
        tl = self.tiles_[i]
        self.hT, self.bufs["hT"] = tl["hT"]
        self.yT = {b: tl["yT"][b][0] for b in "mar"}
        for b in "mar":
            self.bufs["yT_" + b] = tl["yT"][b][1]

    def barrier(self):
        allb = list(Buf.REG)
        for q in ("pe", "act", "dve", "sp"):
            self.T.wait_all(q, allb)

    def ev(self):
        self.rr += 1
        return "act" if self.rr % 2 else "dve"

    def copy(self, out, in_, reads, writes, q=None):
        q = q or self.ev()
        if q == "act":
            self.T.op("act", lambda e: e.activation(out, in_, AF.Identity), reads, writes)
        else:
            self.T.op(q, lambda e: e.tensor_copy(out, in_), reads, writes)

    def transposes(self, src, src_buf, n, TS, dst, dst_buf, dst_k0=0):
        T = self.T
        j = 0
        while j < n:
            g = min(4, n - j)
            pt, pb = self.bank()
            pv = pt[:, :].bitcast(BF16)
            for i in range(g):
                T.op("pe", lambda e, i=i: e.transpose(pv[:, i * 128:i * 128 + TS], src[0:TS, (j + i) * 128:(j + i + 1) * 128],
                                                      self.identb[0:TS, 0:TS]), [src_buf, self.bufs["identb"]], [pb])
            o = dst[:, dst_k0 + j:dst_k0 + j + g, 0:TS]
            i_ = pv[:, 0:g * 128].rearrange("p (g t) -> p g t", g=g)[:, :, 0:TS]
            self.copy(o, i_, [pb], [dst_buf])
            j += g

    def load_w(self, w2d, k0, nkc, cols):
        i = self.wsi % NSLOT
        self.wsi += 1
        slot, sbuf, bt = self.wslot[i], self.bufs["ws%d" % i], self.wbias[i]
        src = w2d[k0 * 128:(k0 + nkc) * 128]
        if callable(cols):
            ck = cols.ckey
            src = cols(src).rearrange("(k p) a b -> p k a b", p=128)
        else:
            ck = tuple(cols)
            src = src[:, cols[0]:cols[0] + cols[1]].rearrange("(k p) n -> p k n", p=128)
        ncols = 1
        for s_ in src.shape[2:]:
            ncols *= s_
        n = nkc * ncols
        view = slot[:, 0:n].rearrange("p (k n) -> p k n", k=nkc)
        key = (w2d.name, str(w2d.offset), k0, nkc, ck)
        if key in self.wcache:
            gi, gb = self.wcache[key]
            self.T.dma("pool", slot[:, 0:n], self.wscr_t[gi // 100][gi % 100, :, 0:n], sbuf, reads=[gb], writes=[sbuf])
            return view, sbuf, bt
        gi = len(self.wcache)
        gb = Buf("wg%d" % gi)
        self.wcache[key] = (gi, gb)
        sview = self.wscr_t[gi // 100][gi % 100, :, 0:n].rearrange("p (k n) -> p k n", k=nkc)
        dstv = view
        if len(src.shape) == 4:
            dstv = view.rearrange("p k (a b) -> p k a b", a=src.shape[2])
            sview = sview.rearrange("p k (a b) -> p k a b", a=src.shape[2])
            for k in range(nkc):
                self.T.dma("pool", dstv[:, k], src[:, k], sbuf, reads=[], writes=[sbuf])
            for k in range(nkc):
                self.T.dma("pool", sview[:, k], src[:, k], sbuf, reads=[], writes=[gb, sbuf])
        else:
            self.T.dma("pool", dstv, src, sbuf, reads=[], writes=[sbuf])
            self.T.dma("pool", sview, src, sbuf, reads=[], writes=[gb, sbuf])
        return view, sbuf, bt

    def gemm(self, xT, xbuf, nkc, TS, w2d, cols, subs, bias_off=None, ksplit=None):
        T = self.T
        multi = isinstance(xT, list)
        xs = xT if multi else [(xT, xbuf)]
        ncols = cols[1] if not callable(cols) else sum(s[1] for s in subs)
        kmax = SLOT_ELEMS // ncols
        chunks = []
        k = 0
        while k < nkc:
            c = min(kmax, nkc - k)
            chunks.append((k, c))
            k += c
        bidx = [[self.bank_i() for _ in subs] for _ in xs]
        banks = [[self.ps[i] for i in row] for row in bidx]
        bt0 = None
        for ci, (k0, kn) in enumerate(chunks):
            view, wbuf, bt = self.load_w(w2d, k0, kn, cols)
            if ci == 0 and bias_off is not None:
                T.dma("pool", bt[0:2, 0:ncols], self.bhl[:, bias_off:bias_off + ncols], wbuf, reads=[self.bufs["bhl"]], writes=[wbuf])
            for xi, (xT_, xb_) in enumerate(xs):
                for (off, n, _), (pt, pb) in zip(subs, banks[xi]):
                    first = ci == 0
                    if first and bias_off is not None:
                        T.op("pe", lambda e, pt=pt, off=off, n=n: e.matmul(pt[0:TS, 0:n], self.ones2[0:2, 0:TS], bt[0:2, off:off + n], start=True, stop=False),
                             [wbuf, self.bufs["ones2"]], [pb])
                        first = False
                    for kk in range(kn):
                        last = (ci == len(chunks) - 1) and kk == kn - 1
                        T.op("pe", lambda e, kk=kk, first=first, last=last, pt=pt, off=off, n=n, xT_=xT_: e.matmul(
                            pt[0:TS, 0:n], xT_[:, k0 + kk, 0:TS], view[:, kk, off:off + n], start=first, stop=last),
                            [wbuf, xb_], [pb])
                        first = False
        mine = set(i for row in bidx for i in row)

        def run_handlers():
            for xi in range(len(xs)):
                for (off, n, h), (pt, pb) in zip(subs, banks[xi]):
                    if multi:
                        h(pt[0:TS, 0:n], pb, xi)
                    else:
                        h(pt[0:TS, 0:n], pb)
            self.held -= mine
        if self.defer_on:
            self.held |= mine
            self.pending.append(run_handlers)
            while len(self.pending) > PIPE_DEPTH:
                self.pending.pop(0)()
        else:
            self.flush()
            run_handlers()

    def rms_scale(self, src, src_buf, TS, Dn, rstd):
        T = self.T
        jb = self.bufs["sq"]
        junk = self.sqf[:, :, :].rearrange("p a b -> p (a b)")
        T.op("dve", lambda e: e.memset(self.sm4["t2"][:, :], 0.0), [], [self.bufs["s4_t2"]])
        n = 0
        while n < Dn:
            c = min(1024, Dn - n)
            T.op("act", lambda e, n=n, c=c: e.activation(junk[0:TS, 0:c], src[0:TS, n:n + c], AF.Square,
                                                          accum_out=self.sm4["t2"][0:TS, (n // 1024):(n // 1024) + 1]),
                 [src_buf], [jb, self.bufs["s4_t2"]])
            n += c
        nch = (Dn + 1023) // 1024
        if nch > 1:
            T.op("dve", lambda e: e.tensor_reduce(rstd, self.sm4["t2"][0:TS, 0:nch], AX.X, OP.add),
                 [self.bufs["s4_t2"]], [self.bufs["s4_t1"]])
        else:
            T.op("dve", lambda e: e.tensor_copy(rstd, self.sm4["t2"][0:TS, 0:1]), [self.bufs["s4_t2"]], [self.bufs["s4_t1"]])
        T.op("act", lambda e: e.activation(rstd, rstd, AF.Sqrt, bias=self.epsc[0:TS, 0:1], scale=1.0 / Dn), [self.bufs["s4_t1"], self.bufs["epsc"]], [self.bufs["s4_t1"]])
        T.op("dve", lambda e: e.reciprocal(rstd, rstd), [self.bufs["s4_t1"]], [self.bufs["s4_t1"]])

    def load_g(self, g_ap):
        self.T.dma("sp", self.gbuf[:, :], g_ap.partition_broadcast(128), self.bufs["gbuf"], [], [self.bufs["gbuf"]])

    def norm_to_hT(self, l, TS, g_ap, dst=None):
        T = self.T
        dT, dTb = dst if dst is not None else (self.hT, self.bufs["hT"])
        rstd = self.sm4["t1"][0:TS, 0:1]
        self.load_g(g_ap)
        self.rms_scale(self.x, self.bufs["x"], TS, D, rstd)
        T.op("dve", lambda e: e.scalar_tensor_tensor(self.hb[0:TS, :], self.x[0:TS, :], rstd, self.gbuf[0:TS, :], OP.mult, OP.mult),
             self.B("x", "s4_t1", "gbuf"), self.B("hb"))
        self.transposes(self.hb, self.bufs["hb"], 16, TS, dT, dTb)

    def resid_add(self, l, TS, g_ap, src=None):
        T = self.T
        tA, tAb = src if src is not None else (self.tmpA, self.bufs["tmpA"])
        x, xB = self.x, self.bufs["x"]
        rstd = self.sm4["t1"][0:TS, 0:1]
        self.load_g(g_ap)
        self.rms_scale(tA, tAb, TS, D, rstd)
        T.op("dve", lambda e: e.scalar_tensor_tensor(tA[0:TS, :], tA[0:TS, :], rstd, self.gbuf[0:TS, :], OP.mult, OP.mult),
             [tAb] + self.B("s4_t1", "gbuf"), [tAb])
        T.op("dve", lambda e: e.tensor_tensor(x[0:TS, :], x[0:TS, :], tA[0:TS, :], OP.add), [xB, tAb], [xB])

    def rope(self, src, src_buf, TS, H, d, tab, tab_buf, out, out_buf, scale=None):
        T = self.T
        hd = d // 2
        x1, x2 = src[0:TS, :, 0:hd], src[0:TS, :, hd:d]
        cos = tab[0:TS, 0, :].unsqueeze(1).broadcast_to([TS, H, hd])
        sin = tab[0:TS, 1, :].unsqueeze(1).broadcast_to([TS, H, hd])
        t1 = self.sqf[:, :, :].rearrange("p a b -> p (a b)")[0:TS, 0:H * hd].rearrange("p (h d) -> p h d", h=H)
        t2 = self.sqf[:, :, :].rearrange("p a b -> p (a b)")[0:TS, 512:512 + H * hd].rearrange("p (h d) -> p h d", h=H)
        jb = self.bufs["sq"]
        T.op("dve", lambda e: e.tensor_tensor(t1, x1, cos, OP.mult), [src_buf, tab_buf], [jb])
        T.op("dve", lambda e: e.tensor_tensor(t2, x2, sin, OP.mult), [src_buf, tab_buf], [jb])
        T.op("dve", lambda e: e.tensor_tensor(out[0:TS, :, 0:hd], t1, t2, OP.subtract), [jb], [out_buf])
        T.op("dve", lambda e: e.tensor_tensor(t1, x1, sin, OP.mult), [src_buf, tab_buf], [jb])
        T.op("dve", lambda e: e.tensor_tensor(t2, x2, cos, OP.mult), [src_buf, tab_buf], [jb])
        T.op("dve", lambda e: e.tensor_tensor(out[0:TS, :, hd:d], t1, t2, OP.add), [jb], [out_buf])
        if scale is not None:
            T.op("dve", lambda e: e.tensor_scalar(out[0:TS], out[0:TS], scale, None, OP.mult), [out_buf], [out_buf])

    def prologue(self):
        T = self.T
        nc = self.nc
        self.bufs["bhl"] = Buf("bhl")
        T.dma("sp", self.c32[:, :], self.t_c32, self.bufs["c32"], [], [self.bufs["c32"]])
        C = self.c32
        self.ident = C[:, 0:128]
        self.tri = C[:, 128:256]
        self.negmask = C[:, 256:384]
        self.negmaskT = C[:, 384:512]
        self.DT = C[:, 640:1152].rearrange("p (h l) -> p h l", h=4)
        self.misc = C[:, 1152:1172]
        T.op("dve", lambda e: e.tensor_copy(self.identb[:, :], self.ident), self.B("c32"), self.B("identb"))
        T.op("dve", lambda e: e.memset(self.ones2[:, :], 1.0), [], self.B("ones2"))
        T.op("dve", lambda e: e.memset(self.onesf[:, :], 1.0), [], self.B("onesf"))
        T.op("dve", lambda e: e.memset(self.epsc[:, :], EPS), [], self.B("epsc"))
        T.op("dve", lambda e: e.memset(self.vm[:, :, :], 1.0), [], self.B("vm"))
        T.op("dve", lambda e: e.memset(self.Vst[:, :, :], 1.0), [], self.B("Vst"))
        bflat = self.b_in.rearrange("l n -> (l n)").rearrange("(a b) -> a b", a=16)
        T.dma("sp", self.bprep, bflat, self.bufs["bprep"], [], self.B("bprep"))
        T.op("dve", lambda e: e.tensor_copy(self.bph, self.bprep), self.B("bprep"), self.B("bph"))
        T.op("dve", lambda e: e.tensor_copy(self.bprep2, self.bph), self.B("bph"), self.B("bprep2"))
        T.op("dve", lambda e: e.tensor_tensor(self.bprep2, self.bprep, self.bprep2, OP.subtract), self.B("bprep", "bprep2"), self.B("bprep2"))
        T.op("dve", lambda e: e.tensor_copy(self.bpl, self.bprep2), self.B("bprep2"), self.B("bpl"))
        T.dma("sp", self.bhl[0].rearrange("(a b) -> a b", a=16), self.bph, self.bufs["bph"], self.B("bph"), [self.bufs["bhl"]])
        T.dma("sp", self.bhl[1].rearrange("(a b) -> a b", a=16), self.bpl, self.bufs["bph"], self.B("bpl"), [self.bufs["bhl"]])
        for nm in ("p", "s"):
            for j, t in enumerate(("kn", "kr", "vv")):
                self.bufs[t + "_" + nm] = Buf(t + "_" + nm)

    def load_layer_consts(self, l):
        T = self.T
        g = self.gsm
        b = self.bufs["gsm"]
        T.dma("sp", g[:, 0, :], self.g_mlstm[l].partition_broadcast(128), b, [], [b])
        T.dma("sp", g[:, 1, :], self.g_ret[l].partition_broadcast(128), b, [], [b])
        T.dma("sp", g[:, 2, 0:512], self.g_qa[l].partition_broadcast(128), b, [], [b])
        T.dma("sp", g[:, 2, 512:1024], self.g_kva[l].partition_broadcast(128), b, [], [b])

    def _stage_in(self, seq, t, l, TS):
        T = self.T
        w = self.w_in[l]
        bo = l * DIN
        r0 = seq.tab0 + t * 128
        T.dma("sp", self.rt128[0:TS, 0, :], self.t_cos128[r0:r0 + TS, :], self.bufs["rt128"], [], self.B("rt128"))
        T.dma("sp", self.rt128[0:TS, 1, :], self.t_sin128[r0:r0 + TS, :], self.bufs["rt128"], [], self.B("rt128"))
        T.dma("sp", self.rt64[0:TS, 0, :], self.t_cos64[r0:r0 + TS, :], self.bufs["rt64"], [], self.B("rt64"))
        T.dma("sp", self.rt64[0:TS, 1, :], self.t_sin64[r0:r0 + TS, :], self.bufs["rt64"], [], self.B("rt64"))
        hT, hTb = self.hT, self.bufs["hT"]
        tA = self.tmpA
        tAb = self.bufs["tmpA"]

        def G(c0, subs, n=None):
            n = n or sum(s[1] for s in subs)
            self.gemm(hT, hTb, 16, TS, w, (c0, n), subs, bias_off=bo + c0)

        def h_mq(ps, pb):
            self.copy(self.hb[0:TS, 0:512], ps, [pb], self.B("hb"))
            self.later.append(lambda: self.transposes(self.hb[:, 0:512], self.bufs["hb"], 4, TS, self.qmT, self.bufs["qmT"]))
        G(C_MQ, [(0, 512, h_mq)])

        def h_mk(ps, pb):
            T.op("act", lambda e: e.activation(self.km_tm[0:TS, :, :].rearrange("p a b -> p (a b)"), ps, AF.Identity, scale=128 ** -0.5), [pb], self.B("km_tm"))
            self.later.append(lambda: self.transposes(self.km_tm[:, :, :].rearrange("p a b -> p (a b)"), self.bufs["km_tm"], 4, TS, self.kmT, self.bufs["kmT"]))
        G(C_MK, [(0, 512, h_mk)])
        for j in range(2):
            def h_mv(ps, pb, j=j):
                self.copy(self.vm[0:TS, 2 * j:2 * j + 2, 0:256], ps.rearrange("p (h v) -> p h v", h=2), [pb], self.B("vm"))
            G(C_MV + 512 * j, [(0, 512, h_mv)])
        for j in range(2):
            def h_mo(ps, pb, j=j):
                o = self.gs_m[0:TS, 2 * j:2 * j + 2, :].rearrange("p a b -> p (a b)")
                T.op("act", lambda e: e.activation(o, ps, AF.Sigmoid), [pb], self.B("gs_m"))
                T.op("dve", lambda e: e.tensor_tensor(o, o, self.gsm[0:TS, 0, 512 * j:512 * j + 512], OP.mult), self.B("gs_m", "gsm"), self.B("gs_m"))
            G(C_MO + 512 * j, [(0, 512, h_mo)])

        def h_if(ps, pb):
            T.op("dve", lambda e: e.tensor_copy(self.igf[0:TS, 0:4], ps[:, 0:4]), [pb], self.B("igf"))
            T.op("act", lambda e: e.activation(self.igf[0:TS, 4:8], ps[:, 4:8], AF.Exp, scale=-1.0), [pb], self.B("igf"))
            T.op("act", lambda e: e.activation(self.igf[0:TS, 4:8], self.igf[0:TS, 4:8], AF.Ln, bias=1.0), self.B("igf"), self.B("igf"))
            T.op("dve", lambda e: e.tensor_scalar(self.igf[0:TS, 4:8], self.igf[0:TS, 4:8], -1.0, None, OP.mult), self.B("igf"), self.B("igf"))

        def h_adq(ps, pb):
            self.copy(tA[0:TS, 0:512], ps, [pb], [tAb])
            rstd = self.sm4["t1"][0:TS, 0:1]
            self.rms_scale(tA, tAb, TS, 512, rstd)
            T.op("dve", lambda e: e.scalar_tensor_tensor(self.hb[0:TS, 512:1024], tA[0:TS, 0:512], rstd, self.gsm[0:TS, 2, 0:512], OP.mult, OP.mult),
                 [tAb] + self.B("s4_t1", "gsm"), self.B("hb"))
            self.later.append(lambda: self.transposes(self.hb[:, 512:1024], self.bufs["hb"], 4, TS, self.hqT, self.bufs["hqT"]))
        G(C_MI, [(0, 8, h_if), (8, 512, h_adq)])

        def h_adkv(ps, pb):
            self.copy(tA[0:TS, 0:512], ps, [pb], [tAb])
            rstd = self.sm4["t1"][0:TS, 0:1]
            self.rms_scale(tA, tAb, TS, 512, rstd)
            T.op("dve", lambda e: e.scalar_tensor_tensor(self.ckv_o[0:TS, :], tA[0:TS, 0:512], rstd, self.gsm[0:TS, 2, 512:1024], OP.mult, OP.mult),
                 [tAb] + self.B("s4_t1", "gsm"), self.B("ckv_o"))
            T.dma("sp", seq.o_ckv[l, t * 128:t * 128 + TS, :], self.ckv_o[0:TS, :], self.bufs["ckv_o"], self.B("ckv_o"), [])
            T.op("act", lambda e: e.activation(self.hb[0:TS, 1024:1536], self.ckv_o[0:TS, :], AF.Identity), self.B("ckv_o"), self.B("hb"))
            self.later.append(lambda: self.transposes(self.hb[:, 1024:1536], self.bufs["hb"], 4, TS, self.ckvT, self.bufs["ckvT"]))
        G(C_ADKV, [(0, 512, h_adkv)])

        def h_akr(ps, pb):
            self.copy(tA[0:TS, 0:64], ps, [pb], [tAb])
            self.rope(tA[:, 0:64].rearrange("p (h d) -> p h d", h=1), tAb, TS, 1, 64, self.rt64, self.bufs["rt64"],
                      self.kr_o[:, :].rearrange("p (h d) -> p h d", h=1), self.bufs["kr_o"])
            T.dma("sp", seq.o_kr[l, t * 128:t * 128 + TS, :], self.kr_o[0:TS, :], self.bufs["kr_o"], self.B("kr_o"), [])
            T.op("act", lambda e: e.activation(self.krd[0:TS, 0:64], self.kr_o[0:TS, :], AF.Identity), self.B("kr_o"), self.B("krd"))
            T.op("act", lambda e: e.activation(self.krd[0:TS, 64:128], self.kr_o[0:TS, :], AF.Identity), self.B("kr_o"), self.B("krd"))
            self.later.append(lambda: self.transposes(self.krd, self.bufs["krd"], 1, TS, self.krT[:, :].rearrange("p (o t) -> p o t", o=1), self.bufs["krT"]))
        G(C_AKR, [(0, 64, h_akr)])

        def h_rq(ps, pb):
            self.copy(tA[0:TS, 0:512], ps, [pb], [tAb])
            self.rope(tA[:, 0:512].rearrange("p (h d) -> p h d", h=4), tAb, TS, 4, 128, self.rt128, self.bufs["rt128"],
                      self.hb[:, 1536:2048].rearrange("p (h d) -> p h d", h=4), self.bufs["hb"])
            self.later.append(lambda: self.transposes(self.hb[:, 1536:2048], self.bufs["hb"], 4, TS, self.rqT, self.bufs["rqT"]))
        G(C_RQ, [(0, 512, h_rq)])

        def h_rk(ps, pb):
            self.copy(tA[0:TS, 0:512], ps, [pb], [tAb])
            self.rope(tA[:, 0:512].rearrange("p (h d) -> p h d", h=4), tAb, TS, 4, 128, self.rt128, self.bufs["rt128"],
                      self.rk_tm, self.bufs["rk_tm"], scale=128 ** -0.5)
            self.later.append(lambda: self.transposes(self.rk_tm[:, :, :].rearrange("p a b -> p (a b)"), self.bufs["rk_tm"], 4, TS, self.rkT, self.bufs["rkT"]))
        G(C_RK, [(0, 512, h_rk)])
        for j in range(2):
            def h_rv(ps, pb, j=j):
                self.copy(self.vr[0:TS, 2 * j:2 * j + 2, :].rearrange("p a b -> p (a b)"), ps, [pb], self.B("vr"))
            G(C_RV + 512 * j, [(0, 512, h_rv)])
        for j in range(2):
            def h_rg(ps, pb, j=j):
                o = self.gs_r[0:TS, 2 * j:2 * j + 2, :].rearrange("p a b -> p (a b)")
                T.op("act", lambda e: e.activation(o, ps, AF.Silu), [pb], self.B("gs_r"))
                T.op("dve", lambda e: e.tensor_tensor(o, o, self.gsm[0:TS, 1, 512 * j:512 * j + 512], OP.mult), self.B("gs_r", "gsm"), self.B("gs_r"))
            G(C_RG + 512 * j, [(0, 512, h_rg)])

    def stage_in(self, seq, t, l, TS):
        self.defer_on = True
        self.later = []
        self._stage_in(seq, t, l, TS)
        self.flush()
        self.defer_on = False
        for f in self.later:
            f()
        self.later = []

    def bcast_rows(self, col, col_buf, L):
        T = self.T
        idb = self.ident[0:L, 0:L].unsqueeze(1).broadcast_to([L, 4, L])
        cb = col[0:L, 0:4].unsqueeze(2).broadcast_to([L, 4, L])
        T.op("dve", lambda e: e.tensor_tensor(self.diag[0:L, :, 0:L], idb, cb, OP.mult), [col_buf] + self.B("c32"), self.B("diag"))
        pt, pb = self.bank()
        for h in range(4):
            T.op("pe", lambda e, h=h: e.matmul(pt[0:L, h * L:(h + 1) * L], self.onesf[0:L, 0:L], self.diag[0:L, h, 0:L], start=True, stop=True),
                 self.B("diag", "onesf"), [pb])
        return pt[0:L, 0:4 * L].rearrange("p (h l) -> p h l", h=4), pb

    def mlstm(self, l, L):
        T = self.T
        s4 = self.sm4
        sb4 = {k: self.bufs["s4_" + k] for k in s4}
        ig, lf = self.igf[0:L, 0:4], self.igf[0:L, 4:8]
        cn, cnb, mbc = self.cn[l], self.cnb[l], self.mbc[l]
        cnB, cnbB, mbcB = self.bufs["cn%d" % l], self.bufs["cnb%d" % l], self.bufs["mbc%d" % l]
        b, a, cm, mx, nmx = (s4[k][0:L, :] for k in ("b", "a", "cm", "mx", "nmx"))
        pt, pb = self.bank()
        T.op("pe", lambda e: e.matmul(pt[0:L, 0:4], self.tri[0:L, 0:L], lf, start=True, stop=True), self.B("c32", "igf"), [pb])
        T.op("dve", lambda e: e.tensor_copy(b, pt[0:L, 0:4]), [pb], [sb4["b"]])
        T.op("dve", lambda e: e.tensor_tensor(a, ig, b, OP.subtract), [sb4["b"]] + self.B("igf"), [sb4["a"]])
        Abc, Ab = self.bcast_rows(s4["a"], sb4["a"], L)
        nm = self.negmask[0:L, 0:L].unsqueeze(1).broadcast_to([L, 4, L])
        T.op("dve", lambda e: e.tensor_tensor(self.Am[0:L, :, 0:L], Abc, nm, OP.add), [Ab] + self.B("c32"), self.B("Am"))
        T.op("dve", lambda e: e.tensor_reduce(cm, self.Am[0:L, :, 0:L], AX.X, OP.max), self.B("Am"), [sb4["cm"]])
        T.op("dve", lambda e: e.tensor_tensor(mx, cm, mbc[0:L, :], OP.max), [sb4["cm"], mbcB], [sb4["mx"]])
        T.op("dve", lambda e: e.tensor_scalar(nmx, mx, -1.0, None, OP.mult), [sb4["mx"]], [sb4["nmx"]])
        mt = self.sm8[0:L, 4:8]
        T.op("dve", lambda e: e.tensor_copy(self.sm8[0:L, 0:4], b), [sb4["b"]], self.B("sm8"))
        T.op("dve", lambda e: e.tensor_tensor(mt, b, mx, OP.add), [sb4["b"], sb4["mx"]], self.B("sm8"))
        ps_, emt = s4["ps"][0:L, :], s4["emt"][0:L, :]
        T.op("dve", lambda e: e.tensor_tensor(ps_, mbc[0:L, :], mx, OP.subtract), [mbcB, sb4["mx"]], [sb4["ps"]])
        T.op("act", lambda e: e.activation(ps_, ps_, AF.Exp), [sb4["ps"]], [sb4["ps"]])
        T.op("act", lambda e: e.activation(emt, mt, AF.Exp, scale=-1.0), self.B("sm8"), [sb4["emt"]])
        Bbc, Bb = self.bcast_rows(s4["nmx"], sb4["nmx"], L)
        nmT = self.negmaskT[0:L, 0:L].unsqueeze(1).broadcast_to([L, 4, L])
        T.op("dve", lambda e: e.tensor_tensor(self.Am[0:L, :, 0:L], Bbc, nmT, OP.add), [Bb] + self.B("c32"), self.B("Am"))
        for h in range(4):
            T.op("act", lambda e, h=h: e.activation(self.dmT[0:L, h, 0:L], self.Am[0:L, h, 0:L], AF.Exp, bias=s4["a"][0:L, h:h + 1]),
                 self.B("Am") + [sb4["a"]], self.B("dmT"))
        pt, pb = self.bank()
        for h in range(4):
            T.op("pe", lambda e, h=h: e.matmul(pt[0:L, h * L:(h + 1) * L], self.kmT[:, h, 0:L], self.qmT[:, h, 0:L], start=True, stop=True),
                 self.B("kmT", "qmT"), [pb])
        T.op("dve", lambda e: e.tensor_tensor(self.wT[0:L, :, 0:L], pt[0:L, 0:4 * L].rearrange("p (h l) -> p h l", h=4), self.dmT[0:L, :, 0:L], OP.mult),
             [pb] + self.B("dmT"), self.B("wT"))
        for h in range(4):
            p1, b1 = self.bank()
            p2, b2 = self.bank()
            T.op("pe", lambda e, h=h: e.matmul(p1[0:L, 0:257], self.wT[0:L, h, 0:L], self.vm[0:L, h, :], start=True, stop=True), self.B("wT", "vm"), [b1])
            T.op("pe", lambda e, h=h: e.matmul(p2[0:L, 0:257], self.qmT[:, h, 0:L], cnb[:, h, :], start=True, stop=True), [self.bufs["qmT"], cnbB], [b2])
            T.op("act", lambda e, h=h: e.activation(self.inter[0:L, h, :], p2[0:L, 0:257], AF.Identity, scale=s4["ps"][0:L, h:h + 1]), [b2, sb4["ps"]], self.B("inter"))
            T.op("dve", lambda e, h=h: e.tensor_tensor(self.hsum[0:L, h, :], p1[0:L, 0:257], self.inter[0:L, h, :], OP.add), [b1] + self.B("inter"), self.B("hsum"))
        den, rden, ss, sc = (s4[k][0:L, :] for k in ("den", "rden", "ss", "sc"))
        T.op("act", lambda e: e.activation(den, self.hsum[0:L, :, 256], AF.Abs), self.B("hsum"), [sb4["den"]])
        T.op("dve", lambda e: e.tensor_tensor(den, den, emt, OP.max), [sb4["den"], sb4["emt"]], [sb4["den"]])
        T.op("dve", lambda e: e.reciprocal(rden, den), [sb4["den"]], [sb4["rden"]])
        hv = self.hsum[0:L, :, 0:256]
        T.op("dve", lambda e: e.tensor_tensor(self.sq[0:L, :, :], hv, hv, OP.mult), self.B("hsum"), self.B("sq"))
        T.op("dve", lambda e: e.tensor_reduce(ss, self.sq[0:L, :, :], AX.X, OP.add), self.B("sq"), [sb4["ss"]])
        T.op("dve", lambda e: e.tensor_tensor(ss, ss, rden, OP.mult), [sb4["ss"], sb4["rden"]], [sb4["ss"]])
        T.op("dve", lambda e: e.tensor_tensor(ss, ss, rden, OP.mult), [sb4["ss"], sb4["rden"]], [sb4["ss"]])
        T.op("act", lambda e: e.activation(ss, ss, AF.Sqrt, bias=self.epsc[0:L, 0:1], scale=1.0 / 256), [sb4["ss"], self.bufs["epsc"]], [sb4["ss"]])
        T.op("dve", lambda e: e.reciprocal(ss, ss), [sb4["ss"]], [sb4["ss"]])
        T.op("dve", lambda e: e.tensor_tensor(sc, ss, rden, OP.mult), [sb4["ss"], sb4["rden"]], [sb4["sc"]])
        T.op("dve", lambda e: e.tensor_tensor(self.sq[0:L, :, :], hv, s4["sc"][0:L, :].unsqueeze(2).broadcast_to([L, 4, 256]), OP.mult),
             self.B("hsum") + [sb4["sc"]], self.B("sq"))
        T.op("dve", lambda e: e.tensor_tensor(self.ytm[0:L, :].rearrange("p (h v) -> p h v", h=4), self.sq[0:L, :, :], self.gs_m[0:L, :, :], OP.mult),
             self.B("sq", "gs_m"), self.B("ytm"))
        self.transposes(self.ytm, self.bufs["ytm"], 8, L, self.yT["m"], self.bufs["yT_m"])
        sel = self.c32[0:L, 512:640] if L == 128 else self.c32[0:L, 1172:1300]
        pt, pb = self.bank()
        T.op("pe", lambda e: e.matmul(pt[:, 0:8], sel, self.sm8[0:L, :], start=True, stop=True), self.B("c32", "sm8"), [pb])
        T.op("dve", lambda e: e.tensor_copy(self.bc8[:, :], pt[:, 0:8]), [pb], self.B("bc8"))
        blast, mnew = self.bc8[:, 0:4], self.bc8[:, 4:8]
        t1 = s4["t1"]
        T.op("dve", lambda e: e.tensor_tensor(t1[:, :], blast, mnew, OP.subtract), self.B("bc8"), [sb4["t1"]])
        ws = s4["ws"][0:L, :]
        T.op("dve", lambda e: e.tensor_tensor(ws, a, t1[0:L, :], OP.add), [sb4["a"], sb4["t1"]], [sb4["ws"]])
        T.op("act", lambda e: e.activation(ws, ws, AF.Exp), [sb4["ws"]], [sb4["ws"]])
        T.op("dve", lambda e: e.tensor_tensor(self.carry[:, :], t1[:, :], mbc[:, :], OP.add), [sb4["t1"], mbcB], self.B("carry"))
        T.op("act", lambda e: e.activation(self.carry[:, :], self.carry[:, :], AF.Exp), self.B("carry"), self.B("carry"))
        T.op("dve", lambda e: e.tensor_tensor(self.kws[0:L, :, :], self.km_tm[0:L, :, :], s4["ws"][0:L, :].unsqueeze(2).broadcast_to([L, 4, 128]), OP.mult),
             self.B("km_tm") + [sb4["ws"]], self.B("kws"))
        for h in range(4):
            p1, b1 = self.bank()
            T.op("pe", lambda e, h=h: e.matmul(p1[:, 0:257], self.kws[0:L, h, :], self.vm[0:L, h, :], start=True, stop=True), self.B("kws", "vm"), [b1])
            T.op("dve", lambda e, h=h: e.scalar_tensor_tensor(cn[:, h, :], cn[:, h, :], self.carry[:, h:h + 1], p1[:, 0:257], OP.mult, OP.add),
                 [cnB, b1] + self.B("carry"), [cnB])
        T.op("act", lambda e: e.activation(cnb[:, :, :], cn[:, :, :], AF.Identity), [cnB], [cnbB])
        T.op("dve", lambda e: e.tensor_copy(mbc[:, :], mnew), self.B("bc8"), [mbcB])

    def retention(self, l, L):
        T = self.T
        s4 = self.sm4
        sb4 = {k: self.bufs["s4_" + k] for k in s4}
        rs, rb = self.rs[l], self.rb[l]
        rsB, rbB = self.bufs["rs%d" % l], self.bufs["rb%d" % l]
        pt, pb = self.bank()
        for h in range(4):
            T.op("pe", lambda e, h=h: e.matmul(pt[0:L, h * L:(h + 1) * L], self.rkT[:, h, 0:L], self.rqT[:, h, 0:L], start=True, stop=True),
                 self.B("rkT", "rqT"), [pb])
        T.op("dve", lambda e: e.tensor_tensor(self.wT[0:L, :, 0:L], pt[0:L, 0:4 * L].rearrange("p (h l) -> p h l", h=4), self.DT[0:L, :, 0:L], OP.mult),
             [pb] + self.B("c32"), self.B("wT"))
        o = self.hsum[0:L, :, 0:256]
        for h in range(4):
            p1, b1 = self.bank()
            p2, b2 = self.bank()
            T.op("pe", lambda e, h=h: e.matmul(p1[0:L, 0:256], self.wT[0:L, h, 0:L], self.vr[0:L, h, :], start=True, stop=True), self.B("wT", "vr"), [b1])
            T.op("pe", lambda e, h=h: e.matmul(p2[0:L, 0:256], self.rqT[:, h, 0:L], rb[:, h, :], start=True, stop=True), [self.bufs["rqT"], rbB], [b2])
            T.op("act", lambda e, h=h: e.activation(self.inter[0:L, h, 0:256], p2[0:L, 0:256], AF.Identity, scale=self.misc[0:L, h:h + 1]), [b2] + self.B("c32"), self.B("inter"))
            T.op("dve", lambda e, h=h: e.tensor_tensor(self.hsum[0:L, h, 0:256], p1[0:L, 0:256], self.inter[0:L, h, 0:256], OP.add), [b1] + self.B("inter"), self.B("hsum"))
        mean, ss = s4["mean"][0:L, :], s4["ss"][0:L, :]
        T.op("dve", lambda e: e.tensor_reduce(mean, o, AX.X, OP.add), self.B("hsum"), [sb4["mean"]])
        T.op("dve", lambda e: e.tensor_scalar(mean, mean, 1.0 / 256, None, OP.mult), [sb4["mean"]], [sb4["mean"]])
        T.op("dve", lambda e: e.tensor_tensor(o, o, s4["mean"][0:L, :].unsqueeze(2).broadcast_to([L, 4, 256]), OP.subtract), self.B("hsum") + [sb4["mean"]], self.B("hsum"))
        T.op("dve", lambda e: e.tensor_tensor(self.sq[0:L, :, :], o, o, OP.mult), self.B("hsum"), self.B("sq"))
        T.op("dve", lambda e: e.tensor_reduce(ss, self.sq[0:L, :, :], AX.X, OP.add), self.B("sq"), [sb4["ss"]])
        T.op("act", lambda e: e.activation(ss, ss, AF.Sqrt, bias=self.epsc[0:L, 0:1], scale=1.0 / 256), [sb4["ss"], self.bufs["epsc"]], [sb4["ss"]])
        T.op("dve", lambda e: e.reciprocal(ss, ss), [sb4["ss"]], [sb4["ss"]])
        T.op("dve", lambda e: e.tensor_tensor(self.sq[0:L, :, :], o, s4["ss"][0:L, :].unsqueeze(2).broadcast_to([L, 4, 256]), OP.mult),
             self.B("hsum") + [sb4["ss"]], self.B("sq"))
        T.op("dve", lambda e: e.tensor_tensor(self.ytm[0:L, :].rearrange("p (h v) -> p h v", h=4), self.sq[0:L, :, :], self.gs_r[0:L, :, :], OP.mult),
             self.B("sq", "gs_r"), self.B("ytm"))
        self.transposes(self.ytm, self.bufs["ytm"], 8, L, self.yT["r"], self.bufs["yT_r"])
        zoff = 4 if L == 128 else 8
        doff = 12 if L == 128 else 16
        zeta = self.misc[0:L, zoff:zoff + 4].unsqueeze(2).broadcast_to([L, 4, 128])
        T.op("dve", lambda e: e.tensor_tensor(self.kws[0:L, :, :], self.rk_tm[0:L, :, :], zeta, OP.mult), self.B("rk_tm", "c32"), self.B("kws"))
        for h in range(4):
            p1, b1 = self.bank()
            T.op("pe", lambda e, h=h: e.matmul(p1[:, 0:256], self.kws[0:L, h, :], self.vr[0:L, h, :], start=True, stop=True), self.B("kws", "vr"), [b1])
            T.op("dve", lambda e, h=h: e.scalar_tensor_tensor(rs[:, h, :], rs[:, h, :], self.misc[:, doff + h:doff + h + 1], p1[:, 0:256], OP.mult, OP.add),
                 [rsB, b1] + self.B("c32"), [rsB])
        T.op("act", lambda e: e.activation(rb[:, :, :], rs[:, :, :], AF.Identity), [rsB], [rbB])

    def kv_up(self, l, TS, seqname, kt):
        T = self.T
        kn_d, kr_d, vv_d = self.kv[seqname]
        bkn, bkr, bvv = (self.bufs[n + "_" + seqname] for n in ("kn", "kr", "vv"))
        w = self.w_ukv[l]
        for half in range(2):
            view, wbuf, _ = self.load_w(w, 0, 4, CK(lambda s, half=half: s.rearrange("r (h c) -> r h c", h=8)[:, 4 * half:4 * half + 4, 0:128], ("n", half)))
            pt, pb = self.bank()
            for hh in range(4):
                for kc in range(4):
                    T.op("pe", lambda e, hh=hh, kc=kc: e.matmul(pt[:, hh * 128:hh * 128 + TS], view[:, kc, hh * 128:(hh + 1) * 128], self.ckvT[:, kc, 0:TS],
                                                                start=(kc == 0), stop=(kc == 3)), [wbuf] + self.B("ckvT"), [pb])
            self.copy(self.KnT[:, 4 * half:4 * half + 4, 0:TS], pt[:, :].rearrange("p (h t) -> p h t", h=4)[:, :, 0:TS], [pb], self.B("KnT"))
        for half in range(2):
            def hv(ps, pb, half=half):
                self.copy(self.Vst[0:TS, 4 * half:4 * half + 4, 0:128], ps.rearrange("p (h v) -> p h v", h=4), [pb], self.B("Vst"))
            self.gemm(self.ckvT, self.bufs["ckvT"], 4, TS, w,
                      CK(lambda s, half=half: s.rearrange("r (h c) -> r h c", h=8)[:, 4 * half:4 * half + 4, 128:256], ("v", half)), [(0, 512, hv)])
        T.dma("sp", kn_d[l, kt].rearrange("p (h t) -> p h t", h=8)[:, :, 0:TS], self.KnT[:, :, 0:TS], self.bufs["KnT"], self.B("KnT"), [bkn])
        T.dma("sp", kr_d[l, kt][:, 0:TS], self.krT[:, 0:TS], self.bufs["krT"], self.B("krT"), [bkr])
        T.dma("sp", vv_d[l, kt][0:TS, :], self.Vst[0:TS, :, :].rearrange("p h v -> p (h v)"), self.bufs["Vst"], self.B("Vst"), [bvv])

    def mla(self, seq, t, l, TS):
        T = self.T
        w = self.w_uq[l]
        for half in range(2):
            view, wbuf, _ = self.load_w(w, 0, 4, CK(lambda s, half=half: s.rearrange("r (h c) -> r h c", h=8)[:, 4 * half:4 * half + 4, 0:128], ("n", half)))
            pt, pb = self.bank()
            for hh in range(4):
                for kc in range(4):
                    T.op("pe", lambda e, hh=hh, kc=kc: e.matmul(pt[:, hh * 128:hh * 128 + TS], view[:, kc, hh * 128:(hh + 1) * 128], self.hqT[:, kc, 0:TS],
                                                                start=(kc == 0), stop=(kc == 3)), [wbuf] + self.B("hqT"), [pb])
            self.copy(self.QnT[:, 4 * half:4 * half + 4, 0:TS], pt[:, :].rearrange("p (h t) -> p h t", h=4)[:, :, 0:TS], [pb], self.B("QnT"))
        tA, tAb = self.tmpA, self.bufs["tmpA"]

        def hqr(ps, pb):
            self.copy(tA[0:TS, 0:512], ps, [pb], [tAb])
            self.rope(tA[:, 0:512].rearrange("p (h d) -> p h d", h=8), tAb, TS, 8, 64, self.rt64, self.bufs["rt64"],
                      self.hb[:, 0:512].rearrange("p (h d) -> p h d", h=8), self.bufs["hb"])
            self.transposes(self.hb, self.bufs["hb"], 4, TS, self.QrT, self.bufs["QrT"])
        self.gemm(self.hqT, self.bufs["hqT"], 4, TS, w, CK(lambda s: s.rearrange("r (h c) -> r h c", h=8)[:, :, 128:192], ("r",)), [(0, 512, hqr)])
        kt_own = seq.kt0 + t
        kn_d, kr_d, vv_d = self.kv[seq.name]
        bkn, bkr, bvv = (self.bufs[n + "_" + seq.name] for n in ("kn", "kr", "vv"))

        def load_kv(kt, KS):
            ks, rs_, vs, sbuf = self.kvs[self.kvi % 3]
            self.kvi += 1
            T.dma("sp", ks[:, :, 0:KS], kn_d[l, kt].rearrange("p (h t) -> p h t", h=8)[:, :, 0:KS], sbuf, [bkn], [sbuf])
            T.dma("sp", rs_[:, 0:KS], kr_d[l, kt][:, 0:KS], sbuf, [bkr], [sbuf])
            T.dma("sp", vs[0:KS, :, :].rearrange("p h v -> p (h v)"), vv_d[l, kt][0:KS, :], sbuf, [bvv], [sbuf])
            return ks, rs_, vs, sbuf
        pre = {}
        for kt in range(min(2, kt_own)):
            pre[kt] = load_kv(kt, 128)
        self.kv_up(l, TS, seq.name, kt_own)
        kn_d, kr_d, vv_d = self.kv[seq.name]
        bkn, bkr, bvv = (self.bufs[n + "_" + seq.name] for n in ("kn", "kr", "vv"))
        nkt = kt_own + 1
        scale = 192 ** -0.5
        self.attn_active = True
        accs = [self.ps[5], self.ps[6], self.ps[7]]
        hb_of = lambda h: (accs[h // 3], (h % 3) * 129)
        PTs = [(self.PT, self.bufs["PT"]), (self.PT2, self.bufs["PT2"])]

        def pv(kt, KS, PTt, PTb, vs, sbuf):
            for h in range(8):
                (at, ab), co = hb_of(h)
                T.op("pe", lambda e, h=h, at=at, co=co: e.matmul(at[0:TS, co:co + 129], PTt[0:KS, h, 0:TS], vs[0:KS, h, :], start=(kt == 0 and h % 3 == 0), stop=(kt == nkt - 1)),
                     [sbuf, PTb], [ab])
        prev = None
        for kt in range(nkt):
            KS = TS if kt == kt_own else 128
            ks, rs_, vs, sbuf = pre[kt] if kt in pre else load_kv(kt, KS)
            PTt, PTb = PTs[kt % 2]
            for half in range(2):
                pt, pb = self.bank()
                for hh in range(4):
                    h = 4 * half + hh
                    po = (h % 2) * 64
                    T.op("pe", lambda e, h=h, hh=hh: e.matmul(pt[0:KS, hh * 128:hh * 128 + TS], ks[:, h, 0:KS], self.QnT[:, h, 0:TS], start=True, stop=False),
                         [sbuf] + self.B("QnT"), [pb])
                    T.op("pe", lambda e, h=h, hh=hh, po=po: e.matmul(pt[0:KS, hh * 128:hh * 128 + TS], rs_[po:po + 64, 0:KS], self.QrT[po:po + 64, h // 2, 0:TS],
                                                                       start=False, stop=True), [sbuf] + self.B("QrT"), [pb])
                T.op("act", lambda e, half=half: e.activation(PTt[0:KS, 4 * half:4 * half + 4, 0:TS],
                                                               pt[0:KS, :].rearrange("p (h t) -> p h t", h=4)[:, :, 0:TS], AF.Exp, scale=scale),
                     [pb], [PTb])
            if seq.causal and kt == kt_own and TS == 128:
                T.op("dve", lambda e: e.memset(PTt[64:128, :, 0:64], 0.0), [], [PTb])
            if prev is not None:
                pv(*prev)
            prev = (kt, KS, PTt, PTb, vs, sbuf)
        pv(*prev)
        for i, (at, ab) in enumerate(accs):
            nh = 3 if i < 2 else 2
            v3 = at[0:TS, 0:nh * 129].rearrange("p (h v) -> p h v", h=nh)
            rd = self.sm8[0:TS, 0:nh]
            T.op("dve", lambda e, v3=v3, rd=rd: e.reciprocal(rd, v3[:, :, 128]), [ab], self.B("sm8"))
            T.op("dve", lambda e, v3=v3, rd=rd, i=i, nh=nh: e.tensor_tensor(
                self.ytm[0:TS, i * 384:i * 384 + nh * 128].rearrange("p (h v) -> p h v", h=nh), v3[:, :, 0:128],
                rd.unsqueeze(2).broadcast_to([TS, nh, 128]), OP.mult), [ab] + self.B("sm8"), self.B("ytm"))
        self.transposes(self.ytm, self.bufs["ytm"], 8, TS, self.yT["a"], self.bufs["yT_a"])
        self.attn_active = False

    def merge_out(self, l, TS):
        T = self.T
        tA, tAb = self.tmpA, self.bufs["tmpA"]
        mg = self.gbuf
        for cb in range(4):
            for bi, (br, goff, wup) in enumerate((("m", C_GM, self.w_up_m), ("a", C_GA, self.w_up_a), ("r", C_GR, self.w_up_r))):
                hold = {}

                def hg(ps, pb):
                    T.op("act", lambda e: e.activation(self.sg[0:TS, :], ps, AF.Sigmoid), [pb], self.B("sg"))
                self.gemm(self.hT, self.bufs["hT"], 16, TS, self.w_in[l], (goff + cb * 512, 512), [(0, 512, hg)], bias_off=l * DIN + goff + cb * 512)

                def hu(ps, pb, bi=bi, cb=cb):
                    dst = tA[0:TS, cb * 512:(cb + 1) * 512]
                    if bi == 0:
                        T.op("dve", lambda e: e.tensor_tensor(dst, ps, self.sg[0:TS, :], OP.mult), [pb] + self.B("sg"), [tAb])
                    else:
                        T.op("dve", lambda e: e.tensor_tensor(self.tmpB[0:TS, :], ps, self.sg[0:TS, :], OP.mult), [pb] + self.B("sg"), self.B("tmpB"))
                        T.op("dve", lambda e: e.tensor_tensor(dst, dst, self.tmpB[0:TS, :], OP.add), [tAb] + self.B("tmpB"), [tAb])
                self.gemm(self.yT[br], self.bufs["yT_" + br], 8, TS, wup[l], (cb * 512, 512), [(0, 512, hu)])
        T.op("act", lambda e: e.activation(self.hb[0:TS, :], tA[0:TS, :], AF.Identity), [tAb], self.B("hb"))
        self.transposes(self.hb, self.bufs["hb"], 16, TS, self.hT, self.bufs["hT"])
        for cb in range(4):
            def ho(ps, pb, cb=cb):
                self.copy(tA[0:TS, cb * 512:(cb + 1) * 512], ps, [pb], [tAb])
            self.gemm(self.hT, self.bufs["hT"], 16, TS, self.w_o[l], (cb * 512, 512), [(0, 512, ho)])
        self.resid_add(l, TS, self.g_mix_post[l])

    def ffn(self, l, TS):
        T = self.T
        tA, tAb = self.tmpA, self.bufs["tmpA"]
        self.norm_to_hT(l, TS, self.g_ffn_pre[l])
        for j in range(11):
            def hg(ps, pb):
                T.op("act", lambda e: e.activation(self.sg[0:TS, :], ps, AF.Silu), [pb], self.B("sg"))
            self.gemm(self.hT, self.bufs["hT"], 16, TS, self.w_gu[l], (DFF + j * 512, 512), [(0, 512, hg)])

            def ha(ps, pb, j=j):
                jj = j if j < 6 else j - 6
                T.op("dve", lambda e: e.tensor_tensor(self.act_tm[0:TS, jj * 512:(jj + 1) * 512], ps, self.sg[0:TS, :], OP.mult), [pb] + self.B("sg"), self.B("act_tm"))
            self.gemm(self.hT, self.bufs["hT"], 16, TS, self.w_gu[l], (j * 512, 512), [(0, 512, ha)])
            if j == 5:
                self.transposes(self.act_tm, self.bufs["act_tm"], 24, TS, self.actT, self.bufs["actT"])
            if j == 10:
                self.transposes(self.act_tm, self.bufs["act_tm"], 20, TS, self.actT, self.bufs["actT"], dst_k0=24)
        for cb in range(4):
            def ho(ps, pb, cb=cb):
                self.copy(tA[0:TS, cb * 512:(cb + 1) * 512], ps, [pb], [tAb])
            self.gemm(self.actT, self.bufs["actT"], 44, TS, self.w_down[l], (cb * 512, 512), [(0, 512, ho)])
        self.resid_add(l, TS, self.g_ffn_post[l])

    def merge2(self, l, TS):
        T = self.T
        Bf = self.bufs
        sets = []
        for i, (tA, tAb, sg, sgb) in enumerate(((self.tmpA, Bf["tmpA"], self.sg, Bf["sg"]), (self.tmpA_B, Bf["tmpA_B"], self.tmpB, Bf["tmpB"]))):
            tl = self.tiles_[i]
            sets.append(dict(hT=tl["hT"][0], hTb=tl["hT"][1], yT=tl["yT"], tA=tA, tAb=tAb, sg=sg, sgb=sgb))
        tmp = self.sqf[:, :, :].rearrange("p a b -> p (a b)")[:, 0:512]
        tmpb = Bf["sq"]
        xs = [(st["hT"], st["hTb"]) for st in sets]
        self.defer_on = True
        for cb in range(4):
            for bi, (br, goff, wup) in enumerate((("m", C_GM, self.w_up_m), ("a", C_GA, self.w_up_a), ("r", C_GR, self.w_up_r))):
                def hg(ps, pb, xi):
                    st = sets[xi]
                    T.op("act", lambda e: e.activation(st["sg"][0:TS, :], ps, AF.Sigmoid), [pb], [st["sgb"]])
                self.gemm(xs, None, 16, TS, self.w_in[l], (goff + cb * 512, 512), [(0, 512, hg)], bias_off=l * DIN + goff + cb * 512)

                def hu(ps, pb, xi, bi=bi, cb=cb):
                    st = sets[xi]
                    dst = st["tA"][0:TS, cb * 512:(cb + 1) * 512]
                    if bi == 0:
                        T.op("dve", lambda e: e.tensor_tensor(dst, ps, st["sg"][0:TS, :], OP.mult), [pb, st["sgb"]], [st["tAb"]])
                    else:
                        T.op("dve", lambda e: e.tensor_tensor(tmp[0:TS, :], ps, st["sg"][0:TS, :], OP.mult), [pb, st["sgb"]], [tmpb])
                        T.op("dve", lambda e: e.tensor_tensor(dst, dst, tmp[0:TS, :], OP.add), [st["tAb"], tmpb], [st["tAb"]])
                self.gemm([(st["yT"][br][0], st["yT"][br][1]) for st in sets], None, 8, TS, wup[l], (cb * 512, 512), [(0, 512, hu)])
        self.flush()
        for st in sets:
            T.op("act", lambda e, st=st: e.activation(self.hb[0:TS, :], st["tA"][0:TS, :], AF.Identity), [st["tAb"]], self.B("hb"))
            self.transposes(self.hb, Bf["hb"], 16, TS, st["hT"], st["hTb"])
        for cb in range(4):
            def ho(ps, pb, xi, cb=cb):
                st = sets[xi]
                self.copy(st["tA"][0:TS, cb * 512:(cb + 1) * 512], ps, [pb], [st["tAb"]])
            self.gemm(xs, None, 16, TS, self.w_o[l], (cb * 512, 512), [(0, 512, ho)])
        self.flush()
        self.defer_on = False
        for i in range(2):
            self.set_x(i)
            self.resid_add(l, TS, self.g_mix_post[l], src=(sets[i]["tA"], sets[i]["tAb"]))

    def ffn2(self, l, TS):
        T = self.T
        Bf = self.bufs
        sets = [
            dict(hT=self.hT, hTb=Bf["hT"], act=self.act_tm, actb=Bf["act_tm"], actT=self.actT, actTb=Bf["actT"], tA=self.tmpA, tAb=Bf["tmpA"], sg=self.sg, sgb=Bf["sg"]),
            dict(hT=self.hT_B, hTb=Bf["hT_B"], act=self.act_tm_B, actb=Bf["act_tm_B"], actT=self.actT_B, actTb=Bf["actT_B"], tA=self.tmpA_B, tAb=Bf["tmpA_B"], sg=self.tmpB, sgb=Bf["tmpB"]),
        ]
        for i in range(2):
            self.set_x(i)
            self.norm_to_hT(l, TS, self.g_ffn_pre[l], dst=(sets[i]["hT"], sets[i]["hTb"]))
        xs = [(st["hT"], st["hTb"]) for st in sets]
        self.defer_on = True
        for j in range(11):
            def hg(ps, pb, xi):
                st = sets[xi]
                T.op("act", lambda e: e.activation(st["sg"][0:TS, :], ps, AF.Silu), [pb], [st["sgb"]])
            self.gemm(xs, None, 16, TS, self.w_gu[l], (DFF + j * 512, 512), [(0, 512, hg)])

            def ha(ps, pb, xi, j=j):
                st = sets[xi]
                jj = j if j < 6 else j - 6
                T.op("dve", lambda e: e.tensor_tensor(st["act"][0:TS, jj * 512:(jj + 1) * 512], ps, st["sg"][0:TS, :], OP.mult), [pb, st["sgb"]], [st["actb"]])
            self.gemm(xs, None, 16, TS, self.w_gu[l], (j * 512, 512), [(0, 512, ha)])
            if j == 5:
                self.flush()
                for st in sets:
                    self.transposes(st["act"], st["actb"], 24, TS, st["actT"], st["actTb"])
            if j == 10:
                self.flush()
                for st in sets:
                    self.transposes(st["act"], st["actb"], 20, TS, st["actT"], st["actTb"], dst_k0=24)
        xs2 = [(st["actT"], st["actTb"]) for st in sets]
        for cb in range(4):
            def ho(ps, pb, xi, cb=cb):
                st = sets[xi]
                self.copy(st["tA"][0:TS, cb * 512:(cb + 1) * 512], ps, [pb], [st["tAb"]])
            self.gemm(xs2, None, 44, TS, self.w_down[l], (cb * 512, 512), [(0, 512, ho)])
        self.flush()
        self.defer_on = False
        for i in range(2):
            self.set_x(i)
            self.resid_add(l, TS, self.g_ffn_post[l], src=(sets[i]["tA"], sets[i]["tAb"]))

    def init_states(self, seq):
        T = self.T
        for l in range(self.nl):
            cnB, cnbB, mbcB = self.bufs["cn%d" % l], self.bufs["cnb%d" % l], self.bufs["mbc%d" % l]
            rsB, rbB = self.bufs["rs%d" % l], self.bufs["rb%d" % l]
            if seq.name == "p":
                T.op("dve", lambda e, l=l: e.memset(self.cn[l][:, :, :], 0.0), [], [cnB])
                T.op("dve", lambda e, l=l: e.memset(self.cnb[l][:, :, :], 0.0), [], [cnbB])
                T.op("dve", lambda e, l=l: e.memset(self.mbc[l][:, :], 0.0), [], [mbcB])
                T.op("dve", lambda e, l=l: e.memset(self.rs[l][:, :, :], 0.0), [], [rsB])
                T.op("dve", lambda e, l=l: e.memset(self.rb[l][:, :, :], 0.0), [], [rbB])
            else:
                for h in range(4):
                    T.dma("sp", self.stg[:, :, :], self.sc[l, h].rearrange("(a p) k -> p a k", p=128), self.bufs["stg"], [], self.B("stg"))
                    pt, pb = self.bank()
                    for a_ in range(2):
                        T.op("pe", lambda e, a_=a_: e.transpose(pt[:, a_ * 128:(a_ + 1) * 128], self.stg[:, a_, :], self.ident), self.B("stg", "c32"), [pb])
                    T.op("dve", lambda e, l=l, h=h: e.tensor_copy(self.cn[l][:, h, 0:256], pt[:, 0:256]), [pb], [cnB])
                T.dma("sp", self.cn[l][:, :, 256], self.sn[l].rearrange("h k -> k h"), cnB, [], [cnB], allow_slow_non_contiguous=True)
                T.dma("sp", self.mbc[l][:, :], self.sm[l].partition_broadcast(128), mbcB, [], [mbcB])
                T.dma("sp", self.rs[l][:, :, :], self.sr[l].rearrange("h k v -> k h v"), rsB, [], [rsB])
                T.op("act", lambda e, l=l: e.activation(self.cnb[l][:, :, :], self.cn[l][:, :, :], AF.Identity), [cnB], [cnbB])
                T.op("act", lambda e, l=l: e.activation(self.rb[l][:, :, :], self.rs[l][:, :, :], AF.Identity), [rsB], [rbB])

    def store_states(self, seq):
        T = self.T
        oc, on, om, orr = seq.o_states
        for l in range(self.nl):
            cnB, mbcB, rsB = self.bufs["cn%d" % l], self.bufs["mbc%d" % l], self.bufs["rs%d" % l]
            for h in range(4):
                pt, pb = self.bank()
                for a_ in range(2):
                    T.op("pe", lambda e, a_=a_, l=l, h=h: e.transpose(pt[:, a_ * 128:(a_ + 1) * 128], self.cn[l][:, h, a_ * 128:(a_ + 1) * 128], self.ident), [cnB] + self.B("c32"), [pb])
                T.op("dve", lambda e: e.tensor_copy(self.stg[:, :, :].rearrange("p a k -> p (a k)"), pt[:, 0:256]), [pb], self.B("stg"))
                T.dma("sp", oc[l, h].rearrange("(a p) k -> p a k", p=128), self.stg[:, :, :], self.bufs["stg"], self.B("stg"), [])
            T.dma("sp", on[l].rearrange("h k -> k h"), self.cn[l][:, :, 256], cnB, [cnB], [], allow_slow_non_contiguous=True)
            T.dma("sp", om[l:l + 1, :], self.mbc[l][0:1, :], mbcB, [mbcB], [])
            T.dma("sp", orr[l].rearrange("h k v -> k h v"), self.rs[l][:, :, :], rsB, [rsB], [])

    def past_kv(self, seq):
        T = self.T
        for l in range(self.nl):
            for kt in range(8):
                T.dma("sp", self.ckv_o[:, :], self.cckv[l, kt * 128:(kt + 1) * 128, :], self.bufs["ckv_o"], [], self.B("ckv_o"))
                T.op("act", lambda e: e.activation(self.hb[:, 0:512], self.ckv_o[:, :], AF.Identity), self.B("ckv_o"), self.B("hb"))
                self.transposes(self.hb, self.bufs["hb"], 4, 128, self.ckvT, self.bufs["ckvT"])
                T.dma("sp", self.kr_o[:, :], self.ckr[l, kt * 128:(kt + 1) * 128, :], self.bufs["kr_o"], [], self.B("kr_o"))
                T.op("act", lambda e: e.activation(self.krd[:, 0:64], self.kr_o[:, :], AF.Identity), self.B("kr_o"), self.B("krd"))
                T.op("act", lambda e: e.activation(self.krd[:, 64:128], self.kr_o[:, :], AF.Identity), self.B("kr_o"), self.B("krd"))
                self.transposes(self.krd, self.bufs["krd"], 1, 128, self.krT[:, :].rearrange("p (o t) -> p o t", o=1), self.bufs["krT"])
                self.kv_up(l, 128, "s", kt)

    def mixer_core(self, seq, t, l, TS):
        self.norm_to_hT(l, TS, self.g_mix_pre[l])
        self.stage_in(seq, t, l, TS)
        self.mlstm(l, TS)
        self.retention(l, TS)
        self.mla(seq, t, l, TS)

    def run_seq(self, seq):
        T = self.T
        self.init_states(seq)
        self.set_tile(1)
        if seq.name == "s":
            self.past_kv(seq)
        TS = seq.TS
        t = 0
        while t < seq.ntiles:
            pair = (TS == 128 and t + 1 < seq.ntiles)
            tiles = [t, t + 1] if pair else [t]
            for i, tt in enumerate(tiles):
                self.set_x(i)
                T.dma("sp", self.x[0:TS, :], seq.x_in[tt * 128:tt * 128 + TS, :], self.bufs["x"], [], self.B("x"))
            for l in range(self.nl):
                self.load_layer_consts(l)
                if pair:
                    for i, tt in enumerate(tiles):
                        self.set_x(i)
                        self.set_tile(i)
                        self.mixer_core(seq, tt, l, TS)
                    self.set_tile(1)
                    self.barrier()
                    self.merge2(l, TS)
                    self.barrier()
                    self.ffn2(l, TS)
                    self.barrier()
                else:
                    self.set_x(0)
                    self.set_tile(1)
                    self.mixer_core(seq, t, l, TS)
                    self.merge_out(l, TS)
                    self.ffn(l, TS)
            for i, tt in enumerate(tiles):
                self.set_x(i)
                T.dma("sp", seq.y_out[tt * 128:tt * 128 + TS, :], self.x[0:TS, :], self.bufs["x"], self.B("x"), [])
            t += len(tiles)
        self.set_x(0)
        self.set_tile(1)
        self.store_states(seq)

    def build(self):
        self.prologue()
        seqs = []
        p = Seq()
        p.name, p.ntiles, p.TS, p.tab0, p.kt0, p.causal = "p", self.S_P // 128, 128, 0, 0, True
        p.x_in, p.y_out, p.o_ckv, p.o_kr = self.xp, self.yp, self.o_pckv, self.o_pkr
        p.o_states = (self.o_pc, self.o_pn, self.o_pm, self.o_pr)
        seqs.append(p)
        if self.do_sample:
            s = Seq()
            s.name, s.ntiles, s.TS, s.tab0, s.kt0, s.causal = "s", 1, 64, self.S_P, 8, False
            s.x_in, s.y_out, s.o_ckv, s.o_kr = self.xs, self.ys, self.o_sckv, self.o_skr
            s.o_states = (self.o_sc, self.o_sn, self.o_sm, self.o_sr)
            seqs.append(s)
        for s in seqs:
            self.run_seq(s)
        self.T.wait_all("sp", list(Buf.REG))
        return self.nc


def const_tables(S_P):
    pos = np.concatenate([np.arange(S_P), PAST + np.arange(SS)]).astype(np.float32)

    def tabs(d):
        inv = (1.0 / (10000.0 ** (np.arange(0, d, 2, dtype=np.float32) / np.float32(d)))).astype(np.float32)
        ang = pos[:, None] * inv[None, :]
        return np.cos(ang).astype(np.float32), np.sin(ang).astype(np.float32)
    c128, s128 = tabs(128)
    c64, s64 = tabs(64)
    i = np.arange(128)
    ident = np.eye(128, dtype=np.float32)
    tri = (i[:, None] <= i[None, :]).astype(np.float32)
    negmask = np.where(i[None, :] <= i[:, None], 0.0, NEG).astype(np.float32)
    negmaskT = negmask.T.copy()
    sel = np.zeros((128, 128), np.float32)
    sel[127, :] = 1.0
    sel[63, :] = 1.0
    lg = np.log1p(-np.exp2(-5.0 - np.arange(4, dtype=np.float64)))
    diff = (i[None, :] - i[:, None]).astype(np.float64)
    DT = np.stack([np.where(diff >= 0, np.exp(np.maximum(diff, 0) * lg[h]), 0.0) for h in range(4)], 1)
    xi = np.exp((i[:, None] + 1.0) * lg[None, :])
    z128 = np.exp((127.0 - i)[:, None] * lg[None, :])
    z64 = np.exp((63.0 - i)[:, None] * lg[None, :])
    d128 = np.broadcast_to(np.exp(128 * lg)[None, :], (128, 4))
    d64 = np.broadcast_to(np.exp(64 * lg)[None, :], (128, 4))
    sel128 = np.zeros((128, 128), np.float32)
    sel128[127, :] = 1.0
    sel64 = np.zeros((128, 128), np.float32)
    sel64[63, :] = 1.0
    c32 = np.concatenate([ident, tri, negmask, negmaskT, sel128, DT.reshape(128, 512), xi, z128, z64, d128, d64, sel64], 1).astype(np.float32)
    return c128, s128, c64, s64, c32


_CACHE = {}


def get_nc(S_P, do_sample=True):
    key = (S_P, do_sample)
    if key not in _CACHE:
        b = Builder(S_P, do_sample)
        _CACHE[key] = b.build()
    return _CACHE[key]


W_NAMES = ["g_mix_pre", "w_in", "b_in", "g_mlstm", "w_up_m", "g_qa", "w_uq", "g_kva", "w_ukv", "w_up_a", "g_ret",
           "w_up_r", "w_o", "g_mix_post", "g_ffn_pre", "w_gu", "w_down", "g_ffn_post"]


def kernel(**inp):
    x_prompt = np.asarray(inp["x_prompt"], np.float32)
    Bn, S_P, _ = x_prompt.shape
    nc = get_nc(S_P)
    c128, s128, c64, s64, c32 = const_tables(S_P)
    shared = {k: np.ascontiguousarray(np.asarray(inp[k], np.float32)) for k in W_NAMES}
    shared.update(t_cos128=c128, t_sin128=s128, t_cos64=c64, t_sin64=s64, t_c32=c32)
    in_maps = []
    for b in range(Bn):
        m = dict(shared)
        m["xp"] = np.ascontiguousarray(x_prompt[b])
        m["xs"] = np.ascontiguousarray(np.asarray(inp["x_sample"], np.float32)[b])
        m["cckv"] = np.ascontiguousarray(np.asarray(inp["cache_mla_ckv"], np.float32)[:, b])
        m["ckr"] = np.ascontiguousarray(np.asarray(inp["cache_mla_krope"], np.float32)[:, b])
        m["sc"] = np.ascontiguousarray(np.asarray(inp["state_mlstm_c"], np.float32)[:, b])
        m["sn"] = np.ascontiguousarray(np.asarray(inp["state_mlstm_n"], np.float32)[:, b])
        m["sm"] = np.ascontiguousarray(np.asarray(inp["state_mlstm_m"], np.float32)[:, b])
        m["sr"] = np.ascontiguousarray(np.asarray(inp["state_ret"], np.float32)[:, b])
        in_maps.append(m)
    res = run_bass_kernel_spmd(nc, in_maps, core_ids=list(range(Bn)))
    R = res.results

    def st(k, axis):
        return np.stack([np.asarray(r[k], np.float32) for r in R], axis)
    return (st("yp", 0), st("ys", 0), st("o_pckv", 1), st("o_pkr", 1), st("o_pc", 1), st("o_pn", 1), st("o_pm", 1), st("o_pr", 1),
            st("o_sckv", 1), st("o_skr", 1), st("o_sc", 1), st("o_sn", 1), st("o_sm", 1), st("o_sr", 1))
```

```python
import numpy as np
import concourse.bass as bass
import concourse.mybir as mybir
from concourse.bass_utils import run_bass_kernel_spmd

F32 = mybir.dt.float32
BF16 = mybir.dt.bfloat16
AF = mybir.ActivationFunctionType
OP = mybir.AluOpType
AX = mybir.AxisListType

D = 2048
DIN = 13384
DFF = 5632
NL = 2
PAST = 1024
SS = 64
EPS = 1e-6
NEG = -1.0e30
C_MQ, C_MK, C_MV, C_MO, C_MI, C_ADQ, C_ADKV, C_AKR, C_RQ, C_RK, C_RV, C_RG, C_GM, C_GA, C_GR = (
    0, 512, 1024, 2048, 3072, 3080, 3592, 4104, 4168, 4680, 5192, 6216, 7240, 9288, 11336)
SLOT_ELEMS = 4352
NWG = 700
NSLOT = 4
ARENA_E = 27456


class Buf:
    __slots__ = ("name", "lw", "rd", "dsem", "dcnt", "lw_dma")

    REG = []

    def __init__(self, name):
        Buf.REG.append(self)
        self.name = name
        self.lw = None
        self.rd = {}
        self.dsem = None
        self.dcnt = 0
        self.lw_dma = False


class Tracker:
    def __init__(self, nc):
        self.nc = nc
        self.eng = {"pe": nc.tensor, "act": nc.scalar, "dve": nc.vector, "pool": nc.gpsimd, "sp": nc.sync}
        self.sem = {k: nc.alloc_semaphore("e_" + k) for k in ("pe", "act", "dve", "pool")}
        self.cnt = {k: 0 for k in self.sem}
        self.waited = {k: {} for k in self.eng}
        self.nsem = 4

    def _need(self, q, reads, writes, is_dma):
        need = {}

        def add(sv):
            if sv is None:
                return
            s, v = sv
            k = id(s)
            if k not in need or need[k][1] < v:
                need[k] = (s, v)

        for b in reads:
            add(b.lw)
        for b in writes:
            if not (is_dma and b.lw_dma and not b.rd):
                add(b.lw)
            for sv in b.rd.values():
                add(sv)
        E = self.eng[q]
        wd = self.waited[q]
        for k, (s, v) in need.items():
            if q == "pe" and s is self.sem["pe"]:
                continue
            if wd.get(k, 0) >= v:
                continue
            E.wait_ge(s, v)
            wd[k] = v

    def op(self, q, fn, reads=(), writes=()):
        self._need(q, reads, writes, False)
        ins = fn(self.eng[q])
        self.cnt[q] += 1
        s = self.sem[q]
        ins.then_inc(s, 1)
        sv = (s, self.cnt[q])
        for b in reads:
            b.rd[id(s)] = sv
        for b in writes:
            b.lw = sv
            b.rd = {}
            b.lw_dma = False
        return ins

    def dma(self, q, out, in_, sb, reads=(), writes=(), **kw):
        self._need(q, reads, writes, True)
        if sb.dsem is None:
            sb.dsem = self.nc.alloc_semaphore("d_" + sb.name)
            self.nsem += 1
        ins = self.eng[q].dma_start(out=out, in_=in_, **kw)
        sb.dcnt += 16
        ins.then_inc(sb.dsem, 16)
        sv = (sb.dsem, sb.dcnt)
        for b in reads:
            b.rd[id(sb.dsem)] = sv
        for b in writes:
            keep = b.lw_dma and not b.rd
            b.lw = sv
            if not keep:
                b.rd = {}
            b.lw_dma = True
        return ins

    def wait_all(self, q, bufs):
        self._need(q, bufs, bufs, False)


class Seq:
    pass


def CK(fn, key):
    fn.ckey = key
    return fn


class Builder:
    def __init__(self, S_P, do_sample=True, nl=NL):
        self.S_P = S_P
        self.do_sample = do_sample
        self.nl = nl
        Buf.REG = []
        nc = self.nc = bass.Bass("TRN2", target_bir_lowering=False)
        self.T = Tracker(nc)
        self.bufs = {}
        self.rr = 0
        self._decl()
        self._alloc()

    def din(self, name, shape, dt=F32):
        return self.nc.dram_tensor(name, list(shape), dt, kind="ExternalInput").ap()

    def dout(self, name, shape):
        return self.nc.dram_tensor(name, list(shape), F32, kind="ExternalOutput").ap()

    def _decl(self):
        S_P = self.S_P
        nc = self.nc
        self.xp = self.din("xp", [S_P, D])
        self.xs = self.din("xs", [SS, D])
        self.cckv = self.din("cckv", [NL, PAST, 512])
        self.ckr = self.din("ckr", [NL, PAST, 64])
        self.sc = self.din("sc", [NL, 4, 256, 128])
        self.sn = self.din("sn", [NL, 4, 128])
        self.sm = self.din("sm", [NL, 4])
        self.sr = self.din("sr", [NL, 4, 128, 256])
        self.g_mix_pre = self.din("g_mix_pre", [NL, D])
        self.w_in = self.din("w_in", [NL, D, DIN])
        self.b_in = self.din("b_in", [NL, DIN])
        self.g_mlstm = self.din("g_mlstm", [NL, 1024])
        self.w_up_m = self.din("w_up_m", [NL, 1024, D])
        self.g_qa = self.din("g_qa", [NL, 512])
        self.w_uq = self.din("w_uq", [NL, 512, 1536])
        self.g_kva = self.din("g_kva", [NL, 512])
        self.w_ukv = self.din("w_ukv", [NL, 512, 2048])
        self.w_up_a = self.din("w_up_a", [NL, 1024, D])
        self.g_ret = self.din("g_ret", [NL, 1024])
        self.w_up_r = self.din("w_up_r", [NL, 1024, D])
        self.w_o = self.din("w_o", [NL, D, D])
        self.g_mix_post = self.din("g_mix_post", [NL, D])
        self.g_ffn_pre = self.din("g_ffn_pre", [NL, D])
        self.w_gu = self.din("w_gu", [NL, D, 2 * DFF])
        self.w_down = self.din("w_down", [NL, DFF, D])
        self.g_ffn_post = self.din("g_ffn_post", [NL, D])
        NP = S_P + SS
        self.t_cos128 = self.din("t_cos128", [NP, 64])
        self.t_sin128 = self.din("t_sin128", [NP, 64])
        self.t_cos64 = self.din("t_cos64", [NP, 32])
        self.t_sin64 = self.din("t_sin64", [NP, 32])
        self.t_c32 = self.din("t_c32", [128, 1300])
        self.yp = self.dout("yp", [S_P, D])
        self.ys = self.dout("ys", [SS, D])
        self.o_pckv = self.dout("o_pckv", [NL, S_P, 512])
        self.o_pkr = self.dout("o_pkr", [NL, S_P, 64])
        self.o_pc = self.dout("o_pc", [NL, 4, 256, 128])
        self.o_pn = self.dout("o_pn", [NL, 4, 128])
        self.o_pm = self.dout("o_pm", [NL, 4])
        self.o_pr = self.dout("o_pr", [NL, 4, 128, 256])
        self.o_sckv = self.dout("o_sckv", [NL, SS, 512])
        self.o_skr = self.dout("o_skr", [NL, SS, 64])
        self.o_sc = self.dout("o_sc", [NL, 4, 256, 128])
        self.o_sn = self.dout("o_sn", [NL, 4, 128])
        self.o_sm = self.dout("o_sm", [NL, 4])
        self.o_sr = self.dout("o_sr", [NL, 4, 128, 256])
        self.bhl = nc.dram_tensor("bhl", [2, NL * DIN], BF16).ap()
        self.wscr_t = [nc.dram_tensor("wscr%d" % i, [100, 128, SLOT_ELEMS], BF16).ap() for i in range((NWG + 99) // 100)]
        self.wcache = {}
        nktp = S_P // 128
        self.kv = {}
        for nm, nkt in (("p", nktp), ("s", 9)):
            self.kv[nm] = (
                nc.dram_tensor("kn_" + nm, [NL, nkt, 128, 1024], BF16).ap(),
                nc.dram_tensor("kr_" + nm, [NL, nkt, 128, 128], BF16).ap(),
                nc.dram_tensor("vv_" + nm, [NL, nkt, 128, 8 * 129], BF16).ap(),
            )

    def sb(self, name, shape, dt=F32):
        t = self.nc.alloc_sbuf_tensor(name, list(shape), dt)
        self.bufs[name] = Buf(name)
        return t

    def B(self, *names):
        return [self.bufs[n] for n in names]

    def _alloc(self):
        nc = self.nc
        sb = self.sb
        self.arena = nc.alloc_sbuf_tensor("arena", [128, ARENA_E], BF16)
        self._ao = 0

        def carve(name, shape, dt, mkbuf=True):
            n = 1
            for d_ in shape[1:]:
                n *= d_
            nb = n * (4 if dt == F32 else 2)
            o = self._ao
            self._ao += (nb + 31) // 32 * 32
            assert self._ao <= ARENA_E * 2, (name, self._ao)
            v = self.arena[:, o // 2:o // 2 + nb // 2]
            if dt == F32:
                v = v.bitcast(F32)
            if len(shape) == 3:
                v = v.rearrange("p (a b) -> p a b", a=shape[1])
            if mkbuf:
                self.bufs[name] = Buf(name)
            return v
        self.carve = carve
        self.c32 = sb("c32", [128, 1300])
        self.identb = sb("identb", [128, 128], BF16)
        self.ones2 = sb("ones2", [2, 128], BF16)
        self.onesf = sb("onesf", [128, 128])
        self.epsc = sb("epsc", [128, 1])
        self.x = sb("x", [128, D])
        self.gbuf = sb("gbuf", [128, D])
        self.gsm = sb("gsm", [128, 3, 1024])
        self.tmpA = sb("tmpA", [128, D])
        self.hb = sb("hb", [128, D], BF16)
        self.hT = sb("hT", [128, 16, 128], BF16)
        self.wslot = [sb("ws%d" % i, [128, SLOT_ELEMS], BF16) for i in range(NSLOT)]
        self.wbias = [nc.alloc_sbuf_tensor("wb%d" % i, [2, 520], BF16) for i in range(NSLOT)]
        self.wsi = 0
        self.rt128 = sb("rt128", [128, 2, 64])
        self.rt64 = sb("rt64", [128, 2, 32])
        self.qmT = None
        self.kmT = None
        self.km_tm = None
        self.vm = sb("vm", [128, 4, 257], BF16)
        self.gs_m = carve("gs_m", [128, 4, 256], F32)
        self.igf = sb("igf", [128, 8])
        self.hqT = None
        self.ckvT = None
        self.ckv_o = sb("ckv_o", [128, 512])
        self.kr_o = sb("kr_o", [128, 64])
        self.krd = sb("krd", [128, 128], BF16)
        self.krT = sb("krT", [128, 128], BF16)
        self.rqT = None
        self.rkT = None
        self.rk_tm = None
        self.vr = None
        self.gs_r = carve("gs_r", [128, 4, 256], F32)
        self.QnT = carve("QnT", [128, 8, 128], BF16)
        self.QrT = carve("QrT", [128, 4, 128], BF16)
        self.KnT = carve("KnT", [128, 8, 128], BF16)
        self.Vst = sb("Vst", [128, 8, 129], BF16)
        self.kvs = []
        for i in range(2):
            k = carve("kvk%d" % i, [128, 8, 128], BF16, False)
            r = carve("kvr%d" % i, [128, 128], BF16, False)
            v = carve("kvv%d" % i, [128, 8, 129], BF16, False)
            self.bufs["kvs%d" % i] = Buf("kvs%d" % i)
            self.kvs.append((k, r, v, self.bufs["kvs%d" % i]))
        self.kvi = 0
        self.PT = carve("PT", [128, 8, 128], BF16)
        self.ytm = carve("ytm", [128, 1024], BF16)
        self.ytm_r = sb("ytm_r", [128, 1024], BF16)
        self.sm4 = {n: sb("s4_" + n, [128, 4]) for n in
                    ("b", "a", "cm", "mx", "nmx", "ps", "emt", "den", "rden", "ss", "sc", "ws", "mean", "t1", "t2")}
        self.sm8 = sb("sm8", [128, 8])
        self.bc8 = sb("bc8", [128, 8])
        self.carry = sb("carry", [128, 4])
        self.diag = carve("diag", [128, 4, 128], F32)
        self.Am = carve("Am", [128, 4, 128], F32)
        self.dmT = carve("dmT", [128, 4, 128], F32)
        self.wT = None
        self.hsum = carve("hsum", [128, 4, 257], F32)
        self.sqf = sb("sq", [128, 4, 257])
        self.inter = self.sqf
        self.bufs["inter"] = self.bufs["sq"]
        self.sq = self.sqf[:, :, 0:256]
        self.kws = None
        self.cn = [sb("cn%d" % l, [128, 4, 257]) for l in range(NL)]
        self.cnb = [sb("cnb%d" % l, [128, 4, 257], BF16) for l in range(NL)]
        self.mbc = [sb("mbc%d" % l, [128, 4]) for l in range(NL)]
        self.rs = [sb("rs%d" % l, [128, 4, 256]) for l in range(NL)]
        self.rb = [sb("rb%d" % l, [128, 4, 256], BF16) for l in range(NL)]
        self.stg = sb("stg", [128, 2, 128])
        self.sg = sb("sg", [128, 512])
        self.tmpB = sb("tmpB", [128, 512])

        for nm_, shp_ in (("qmT", [128, 4, 128]), ("kmT", [128, 4, 128]), ("km_tm", [128, 4, 128]), ("hqT", [128, 4, 128]), ("ckvT", [128, 4, 128]),
                          ("rqT", [128, 4, 128]), ("rkT", [128, 4, 128]), ("rk_tm", [128, 4, 128]), ("vr", [128, 4, 256]), ("wT", [128, 4, 128]), ("kws", [128, 4, 128])):
            setattr(self, nm_, carve(nm_, shp_, BF16))
        self.yT = {b: carve("yT_" + b, [128, 8, 128], BF16) for b in "mar"}
        ao_ = self._ao
        self._ao = 35840
        self.actT = carve("actT", [128, 44, 128], BF16)
        assert self._ao <= ao_ - 6144, (self._ao, ao_)
        self._ao = ao_
        self.hT2 = sb("hT2", [128, 16, 128], BF16)
        self.yT2 = {b: sb("yT2_" + b, [128, 8, 128], BF16) for b in "mar"}
        self.tiles_ = [
            dict(hT=(self.hT2, self.bufs["hT2"]), yT={b: (self.yT2[b], self.bufs["yT2_" + b]) for b in "mar"}),
            dict(hT=(self.hT, self.bufs["hT"]), yT={b: (self.yT[b], self.bufs["yT_" + b]) for b in "mar"}),
        ]
        self.bprep = self.tmpA[0:16, 0:1673]
        self.bprep2 = self.gbuf[0:16, 0:1673]
        self.bph = self.hb[0:16, 0:1673]
        self.bpl = self.actT[:, :, :].rearrange("p a b -> p (a b)")[0:16, 0:1673]
        for a_, b_ in (("bprep", "tmpA"), ("bprep2", "gbuf"), ("bph", "hb"), ("bpl", "actT")):
            self.bufs[a_] = self.bufs[b_]
        print("SBUF bytes remaining", nc.sbuf_bytes_remaining)
        ao = self._ao
        self._ao = 0
        self.hT_B = carve("hT_B", [128, 16, 128], BF16)
        self.act_tm_B = carve("act_tm_B", [128, 3072], BF16)
        self.actT_B = carve("actT_B", [128, 44, 128], BF16)
        self.tmpA_B = carve("tmpA_B", [128, D], F32)
        self.act_tm = carve("act_tm", [128, 3072], BF16)
        assert self._ao <= ao, (self._ao, ao)
        self.x2 = sb("x2", [128, D])
        self.xb = [(self.x, self.bufs["x"]), (self.x2, self.bufs["x2"])]
        print("SBUF bytes remaining (final)", nc.sbuf_bytes_remaining)
        self.ps = []
        for i in range(8):
            t = nc.alloc_psum_tensor("ps%d" % i, [128, 512], F32)
            self.bufs["ps%d" % i] = Buf("ps%d" % i)
            self.ps.append((t, self.bufs["ps%d" % i]))
        self.psi = 0
        self.attn_active = False
        self.held = set()
        self.pending = None
        self.defer_on = False

    def bank(self):
        nb = 5 if self.attn_active else 8
        while (self.psi % nb) in self.held:
            self.psi += 1
        t, b = self.ps[self.psi % nb]
        self.psi += 1
        return t, b

    def bank_i(self):
        nb = 5 if self.attn_active else 8
        while (self.psi % nb) in self.held:
            self.psi += 1
        i = self.psi % nb
        self.psi += 1
        return i

    def flush(self):
        p = self.pending
        self.pending = None
        if p is not None:
            p()

    def set_x(self, i):
        self.x, self.bufs["x"] = self.xb[i]

    def set_tile(self, i):
        tl = self.tiles_[i]
        self.hT, self.bufs["hT"] = tl["hT"]
        self.yT = {b: tl["yT"][b][0] for b in "mar"}
        for b in "mar":
            self.bufs["yT_" + b] = tl["yT"][b][1]

    def barrier(self):
        allb = list(Buf.REG)
        for q in ("pe", "act", "dve", "sp"):
            self.T.wait_all(q, allb)

    def ev(self):
        self.rr += 1
        return "act" if self.rr % 2 else "dve"

    def copy(self, out, in_, reads, writes, q=None):
        q = q or self.ev()
        if q == "act":
            self.T.op("act", lambda e: e.activation(out, in_, AF.Identity), reads, writes)
        else:
            self.T.op(q, lambda e: e.tensor_copy(out, in_), reads, writes)

    def transposes(self, src, src_buf, n, TS, dst, dst_buf, dst_k0=0):
        T = self.T
        j = 0
        while j < n:
            g = min(4, n - j)
            pt, pb = self.bank()
            pv = pt[:, :].bitcast(BF16)
            for i in range(g):
                T.op("pe", lambda e, i=i: e.transpose(pv[:, i * 128:i * 128 + TS], src[0:TS, (j + i) * 128:(j + i + 1) * 128],
                                                      self.identb[0:TS, 0:TS]), [src_buf, self.bufs["identb"]], [pb])
            o = dst[:, dst_k0 + j:dst_k0 + j + g, 0:TS]
            i_ = pv[:, 0:g * 128].rearrange("p (g t) -> p g t", g=g)[:, :, 0:TS]
            self.copy(o, i_, [pb], [dst_buf])
            j += g

    def load_w(self, w2d, k0, nkc, cols):
        i = self.wsi % NSLOT
        self.wsi += 1
        slot, sbuf, bt = self.wslot[i], self.bufs["ws%d" % i], self.wbias[i]
        src = w2d[k0 * 128:(k0 + nkc) * 128]
        if callable(cols):
            ck = cols.ckey
            src = cols(src).rearrange("(k p) a b -> p k a b", p=128)
        else:
            ck = tuple(cols)
            src = src[:, cols[0]:cols[0] + cols[1]].rearrange("(k p) n -> p k n", p=128)
        ncols = 1
        for s_ in src.shape[2:]:
            ncols *= s_
        n = nkc * ncols
        view = slot[:, 0:n].rearrange("p (k n) -> p k n", k=nkc)
        key = (w2d.name, str(w2d.offset), k0, nkc, ck)
        if key in self.wcache:
            gi, gb = self.wcache[key]
            self.T.dma("pool", slot[:, 0:n], self.wscr_t[gi // 100][gi % 100, :, 0:n], sbuf, reads=[gb], writes=[sbuf])
            return view, sbuf, bt
        gi = len(self.wcache)
        gb = Buf("wg%d" % gi)
        self.wcache[key] = (gi, gb)
        sview = self.wscr_t[gi // 100][gi % 100, :, 0:n].rearrange("p (k n) -> p k n", k=nkc)
        dstv = view
        if len(src.shape) == 4:
            dstv = view.rearrange("p k (a b) -> p k a b", a=src.shape[2])
            sview = sview.rearrange("p k (a b) -> p k a b", a=src.shape[2])
            for k in range(nkc):
                self.T.dma("pool", dstv[:, k], src[:, k], sbuf, reads=[], writes=[sbuf])
            for k in range(nkc):
                self.T.dma("pool", sview[:, k], src[:, k], sbuf, reads=[], writes=[gb, sbuf])
        else:
            self.T.dma("pool", dstv, src, sbuf, reads=[], writes=[sbuf])
            self.T.dma("pool", sview, src, sbuf, reads=[], writes=[gb, sbuf])
        return view, sbuf, bt

    def gemm(self, xT, xbuf, nkc, TS, w2d, cols, subs, bias_off=None, ksplit=None):
        T = self.T
        multi = isinstance(xT, list)
        xs = xT if multi else [(xT, xbuf)]
        ncols = cols[1] if not callable(cols) else sum(s[1] for s in subs)
        kmax = SLOT_ELEMS // ncols
        chunks = []
        k = 0
        while k < nkc:
            c = min(kmax, nkc - k)
            chunks.append((k, c))
            k += c
        bidx = [[self.bank_i() for _ in subs] for _ in xs]
        banks = [[self.ps[i] for i in row] for row in bidx]
        bt0 = None
        for ci, (k0, kn) in enumerate(chunks):
            view, wbuf, bt = self.load_w(w2d, k0, kn, cols)
            if ci == 0 and bias_off is not None:
                T.dma("pool", bt[0:2, 0:ncols], self.bhl[:, bias_off:bias_off + ncols], wbuf, reads=[self.bufs["bhl"]], writes=[wbuf])
            for xi, (xT_, xb_) in enumerate(xs):
                for (off, n, _), (pt, pb) in zip(subs, banks[xi]):
                    first = ci == 0
                    if first and bias_off is not None:
                        T.op("pe", lambda e, pt=pt, off=off, n=n: e.matmul(pt[0:TS, 0:n], self.ones2[0:2, 0:TS], bt[0:2, off:off + n], start=True, stop=False),
                             [wbuf, self.bufs["ones2"]], [pb])
                        first = False
                    for kk in range(kn):
                        last = (ci == len(chunks) - 1) and kk == kn - 1
                        T.op("pe", lambda e, kk=kk, first=first, last=last, pt=pt, off=off, n=n, xT_=xT_: e.matmul(
                            pt[0:TS, 0:n], xT_[:, k0 + kk, 0:TS], view[:, kk, off:off + n], start=first, stop=last),
                            [wbuf, xb_], [pb])
                        first = False
        mine = set(i for row in bidx for i in row)

        def run_handlers():
            for xi in range(len(xs)):
                for (off, n, h), (pt, pb) in zip(subs, banks[xi]):
                    if multi:
                        h(pt[0:TS, 0:n], pb, xi)
                    else:
                        h(pt[0:TS, 0:n], pb)
            self.held -= mine
        if self.defer_on:
            self.held |= mine
            prev = self.pending
            self.pending = run_handlers
            if prev is not None:
                prev()
        else:
            self.flush()
            run_handlers()

    def rms_scale(self, src, src_buf, TS, Dn, rstd):
        T = self.T
        jb = self.bufs["sq"]
        junk = self.sqf[:, :, :].rearrange("p a b -> p (a b)")
        T.op("dve", lambda e: e.memset(self.sm4["t2"][:, :], 0.0), [], [self.bufs["s4_t2"]])
        n = 0
        while n < Dn:
            c = min(1024, Dn - n)
            T.op("act", lambda e, n=n, c=c: e.activation(junk[0:TS, 0:c], src[0:TS, n:n + c], AF.Square,
                                                          accum_out=self.sm4["t2"][0:TS, (n // 1024):(n // 1024) + 1]),
                 [src_buf], [jb, self.bufs["s4_t2"]])
            n += c
        nch = (Dn + 1023) // 1024
        if nch > 1:
            T.op("dve", lambda e: e.tensor_reduce(rstd, self.sm4["t2"][0:TS, 0:nch], AX.X, OP.add),
                 [self.bufs["s4_t2"]], [self.bufs["s4_t1"]])
        else:
            T.op("dve", lambda e: e.tensor_copy(rstd, self.sm4["t2"][0:TS, 0:1]), [self.bufs["s4_t2"]], [self.bufs["s4_t1"]])
        T.op("act", lambda e: e.activation(rstd, rstd, AF.Sqrt, bias=self.epsc[0:TS, 0:1], scale=1.0 / Dn), [self.bufs["s4_t1"], self.bufs["epsc"]], [self.bufs["s4_t1"]])
        T.op("dve", lambda e: e.reciprocal(rstd, rstd), [self.bufs["s4_t1"]], [self.bufs["s4_t1"]])

    def load_g(self, g_ap):
        self.T.dma("sp", self.gbuf[:, :], g_ap.partition_broadcast(128), self.bufs["gbuf"], [], [self.bufs["gbuf"]])

    def norm_to_hT(self, l, TS, g_ap, dst=None):
        T = self.T
        dT, dTb = dst if dst is not None else (self.hT, self.bufs["hT"])
        rstd = self.sm4["t1"][0:TS, 0:1]
        self.load_g(g_ap)
        self.rms_scale(self.x, self.bufs["x"], TS, D, rstd)
        T.op("dve", lambda e: e.scalar_tensor_tensor(self.hb[0:TS, :], self.x[0:TS, :], rstd, self.gbuf[0:TS, :], OP.mult, OP.mult),
             self.B("x", "s4_t1", "gbuf"), self.B("hb"))
        self.transposes(self.hb, self.bufs["hb"], 16, TS, dT, dTb)

    def resid_add(self, l, TS, g_ap, src=None):
        T = self.T
        tA, tAb = src if src is not None else (self.tmpA, self.bufs["tmpA"])
        x, xB = self.x, self.bufs["x"]
        rstd = self.sm4["t1"][0:TS, 0:1]
        self.load_g(g_ap)
        self.rms_scale(tA, tAb, TS, D, rstd)
        T.op("dve", lambda e: e.scalar_tensor_tensor(tA[0:TS, :], tA[0:TS, :], rstd, self.gbuf[0:TS, :], OP.mult, OP.mult),
             [tAb] + self.B("s4_t1", "gbuf"), [tAb])
        T.op("dve", lambda e: e.tensor_tensor(x[0:TS, :], x[0:TS, :], tA[0:TS, :], OP.add), [xB, tAb], [xB])

    def rope(self, src, src_buf, TS, H, d, tab, tab_buf, out, out_buf, scale=None):
        T = self.T
        hd = d // 2
        x1, x2 = src[0:TS, :, 0:hd], src[0:TS, :, hd:d]
        cos = tab[0:TS, 0, :].unsqueeze(1).broadcast_to([TS, H, hd])
        sin = tab[0:TS, 1, :].unsqueeze(1).broadcast_to([TS, H, hd])
        t1 = self.sqf[:, :, :].rearrange("p a b -> p (a b)")[0:TS, 0:H * hd].rearrange("p (h d) -> p h d", h=H)
        t2 = self.sqf[:, :, :].rearrange("p a b -> p (a b)")[0:TS, 512:512 + H * hd].rearrange("p (h d) -> p h d", h=H)
        jb = self.bufs["sq"]
        T.op("dve", lambda e: e.tensor_tensor(t1, x1, cos, OP.mult), [src_buf, tab_buf], [jb])
        T.op("dve", lambda e: e.tensor_tensor(t2, x2, sin, OP.mult), [src_buf, tab_buf], [jb])
        T.op("dve", lambda e: e.tensor_tensor(out[0:TS, :, 0:hd], t1, t2, OP.subtract), [jb], [out_buf])
        T.op("dve", lambda e: e.tensor_tensor(t1, x1, sin, OP.mult), [src_buf, tab_buf], [jb])
        T.op("dve", lambda e: e.tensor_tensor(t2, x2, cos, OP.mult), [src_buf, tab_buf], [jb])
        T.op("dve", lambda e: e.tensor_tensor(out[0:TS, :, hd:d], t1, t2, OP.add), [jb], [out_buf])
        if scale is not None:
            T.op("dve", lambda e: e.tensor_scalar(out[0:TS], out[0:TS], scale, None, OP.mult), [out_buf], [out_buf])

    def prologue(self):
        T = self.T
        nc = self.nc
        self.bufs["bhl"] = Buf("bhl")
        T.dma("sp", self.c32[:, :], self.t_c32, self.bufs["c32"], [], [self.bufs["c32"]])
        C = self.c32
        self.ident = C[:, 0:128]
        self.tri = C[:, 128:256]
        self.negmask = C[:, 256:384]
        self.negmaskT = C[:, 384:512]
        self.DT = C[:, 640:1152].rearrange("p (h l) -> p h l", h=4)
        self.misc = C[:, 1152:1172]
        T.op("dve", lambda e: e.tensor_copy(self.identb[:, :], self.ident), self.B("c32"), self.B("identb"))
        T.op("dve", lambda e: e.memset(self.ones2[:, :], 1.0), [], self.B("ones2"))
        T.op("dve", lambda e: e.memset(self.onesf[:, :], 1.0), [], self.B("onesf"))
        T.op("dve", lambda e: e.memset(self.epsc[:, :], EPS), [], self.B("epsc"))
        T.op("dve", lambda e: e.memset(self.vm[:, :, :], 1.0), [], self.B("vm"))
        T.op("dve", lambda e: e.memset(self.Vst[:, :, :], 1.0), [], self.B("Vst"))
        bflat = self.b_in.rearrange("l n -> (l n)").rearrange("(a b) -> a b", a=16)
        T.dma("sp", self.bprep, bflat, self.bufs["bprep"], [], self.B("bprep"))
        T.op("dve", lambda e: e.tensor_copy(self.bph, self.bprep), self.B("bprep"), self.B("bph"))
        T.op("dve", lambda e: e.tensor_copy(self.bprep2, self.bph), self.B("bph"), self.B("bprep2"))
        T.op("dve", lambda e: e.tensor_tensor(self.bprep2, self.bprep, self.bprep2, OP.subtract), self.B("bprep", "bprep2"), self.B("bprep2"))
        T.op("dve", lambda e: e.tensor_copy(self.bpl, self.bprep2), self.B("bprep2"), self.B("bpl"))
        T.dma("sp", self.bhl[0].rearrange("(a b) -> a b", a=16), self.bph, self.bufs["bph"], self.B("bph"), [self.bufs["bhl"]])
        T.dma("sp", self.bhl[1].rearrange("(a b) -> a b", a=16), self.bpl, self.bufs["bph"], self.B("bpl"), [self.bufs["bhl"]])
        for nm in ("p", "s"):
            for j, t in enumerate(("kn", "kr", "vv")):
                self.bufs[t + "_" + nm] = Buf(t + "_" + nm)

    def load_layer_consts(self, l):
        T = self.T
        g = self.gsm
        b = self.bufs["gsm"]
        T.dma("sp", g[:, 0, :], self.g_mlstm[l].partition_broadcast(128), b, [], [b])
        T.dma("sp", g[:, 1, :], self.g_ret[l].partition_broadcast(128), b, [], [b])
        T.dma("sp", g[:, 2, 0:512], self.g_qa[l].partition_broadcast(128), b, [], [b])
        T.dma("sp", g[:, 2, 512:1024], self.g_kva[l].partition_broadcast(128), b, [], [b])

    def _stage_in(self, seq, t, l, TS):
        T = self.T
        w = self.w_in[l]
        bo = l * DIN
        r0 = seq.tab0 + t * 128
        T.dma("sp", self.rt128[0:TS, 0, :], self.t_cos128[r0:r0 + TS, :], self.bufs["rt128"], [], self.B("rt128"))
        T.dma("sp", self.rt128[0:TS, 1, :], self.t_sin128[r0:r0 + TS, :], self.bufs["rt128"], [], self.B("rt128"))
        T.dma("sp", self.rt64[0:TS, 0, :], self.t_cos64[r0:r0 + TS, :], self.bufs["rt64"], [], self.B("rt64"))
        T.dma("sp", self.rt64[0:TS, 1, :], self.t_sin64[r0:r0 + TS, :], self.bufs["rt64"], [], self.B("rt64"))
        hT, hTb = self.hT, self.bufs["hT"]
        tA = self.tmpA
        tAb = self.bufs["tmpA"]

        def G(c0, subs, n=None):
            n = n or sum(s[1] for s in subs)
            self.gemm(hT, hTb, 16, TS, w, (c0, n), subs, bias_off=bo + c0)

        def h_mq(ps, pb):
            self.copy(self.hb[0:TS, 0:512], ps, [pb], self.B("hb"))
            self.transposes(self.hb, self.bufs["hb"], 4, TS, self.qmT, self.bufs["qmT"])
        G(C_MQ, [(0, 512, h_mq)])

        def h_mk(ps, pb):
            T.op("act", lambda e: e.activation(self.km_tm[0:TS, :, :].rearrange("p a b -> p (a b)"), ps, AF.Identity, scale=128 ** -0.5), [pb], self.B("km_tm"))
            self.transposes(self.km_tm[:, :, :].rearrange("p a b -> p (a b)"), self.bufs["km_tm"], 4, TS, self.kmT, self.bufs["kmT"])
        G(C_MK, [(0, 512, h_mk)])
        for j in range(2):
            def h_mv(ps, pb, j=j):
                self.copy(self.vm[0:TS, 2 * j:2 * j + 2, 0:256], ps.rearrange("p (h v) -> p h v", h=2), [pb], self.B("vm"))
            G(C_MV + 512 * j, [(0, 512, h_mv)])
        for j in range(2):
            def h_mo(ps, pb, j=j):
                o = self.gs_m[0:TS, 2 * j:2 * j + 2, :].rearrange("p a b -> p (a b)")
                T.op("act", lambda e: e.activation(o, ps, AF.Sigmoid), [pb], self.B("gs_m"))
                T.op("dve", lambda e: e.tensor_tensor(o, o, self.gsm[0:TS, 0, 512 * j:512 * j + 512], OP.mult), self.B("gs_m", "gsm"), self.B("gs_m"))
            G(C_MO + 512 * j, [(0, 512, h_mo)])

        def h_if(ps, pb):
            T.op("dve", lambda e: e.tensor_copy(self.igf[0:TS, 0:4], ps[:, 0:4]), [pb], self.B("igf"))
            T.op("act", lambda e: e.activation(self.igf[0:TS, 4:8], ps[:, 4:8], AF.Exp, scale=-1.0), [pb], self.B("igf"))
            T.op("act", lambda e: e.activation(self.igf[0:TS, 4:8], self.igf[0:TS, 4:8], AF.Ln, bias=1.0), self.B("igf"), self.B("igf"))
            T.op("dve", lambda e: e.tensor_scalar(self.igf[0:TS, 4:8], self.igf[0:TS, 4:8], -1.0, None, OP.mult), self.B("igf"), self.B("igf"))

        def h_adq(ps, pb):
            self.copy(tA[0:TS, 0:512], ps, [pb], [tAb])
            rstd = self.sm4["t1"][0:TS, 0:1]
            self.rms_scale(tA, tAb, TS, 512, rstd)
            T.op("dve", lambda e: e.scalar_tensor_tensor(self.hb[0:TS, 0:512], tA[0:TS, 0:512], rstd, self.gsm[0:TS, 2, 0:512], OP.mult, OP.mult),
                 [tAb] + self.B("s4_t1", "gsm"), self.B("hb"))
            self.transposes(self.hb, self.bufs["hb"], 4, TS, self.hqT, self.bufs["hqT"])
        G(C_MI, [(0, 8, h_if), (8, 512, h_adq)])

        def h_adkv(ps, pb):
            self.copy(tA[0:TS, 0:512], ps, [pb], [tAb])
            rstd = self.sm4["t1"][0:TS, 0:1]
            self.rms_scale(tA, tAb, TS, 512, rstd)
            T.op("dve", lambda e: e.scalar_tensor_tensor(self.ckv_o[0:TS, :], tA[0:TS, 0:512], rstd, self.gsm[0:TS, 2, 512:1024], OP.mult, OP.mult),
                 [tAb] + self.B("s4_t1", "gsm"), self.B("ckv_o"))
            T.dma("sp", seq.o_ckv[l, t * 128:t * 128 + TS, :], self.ckv_o[0:TS, :], self.bufs["ckv_o"], self.B("ckv_o"), [])
            T.op("act", lambda e: e.activation(self.hb[0:TS, 0:512], self.ckv_o[0:TS, :], AF.Identity), self.B("ckv_o"), self.B("hb"))
            self.transposes(self.hb, self.bufs["hb"], 4, TS, self.ckvT, self.bufs["ckvT"])
        G(C_ADKV, [(0, 512, h_adkv)])

        def h_akr(ps, pb):
            self.copy(tA[0:TS, 0:64], ps, [pb], [tAb])
            self.rope(tA[:, 0:64].rearrange("p (h d) -> p h d", h=1), tAb, TS, 1, 64, self.rt64, self.bufs["rt64"],
                      self.kr_o[:, :].rearrange("p (h d) -> p h d", h=1), self.bufs["kr_o"])
            T.dma("sp", seq.o_kr[l, t * 128:t * 128 + TS, :], self.kr_o[0:TS, :], self.bufs["kr_o"], self.B("kr_o"), [])
            T.op("act", lambda e: e.activation(self.krd[0:TS, 0:64], self.kr_o[0:TS, :], AF.Identity), self.B("kr_o"), self.B("krd"))
            T.op("act", lambda e: e.activation(self.krd[0:TS, 64:128], self.kr_o[0:TS, :], AF.Identity), self.B("kr_o"), self.B("krd"))
            self.transposes(self.krd, self.bufs["krd"], 1, TS, self.krT[:, :].rearrange("p (o t) -> p o t", o=1), self.bufs["krT"])
        G(C_AKR, [(0, 64, h_akr)])

        def h_rq(ps, pb):
            self.copy(tA[0:TS, 0:512], ps, [pb], [tAb])
            self.rope(tA[:, 0:512].rearrange("p (h d) -> p h d", h=4), tAb, TS, 4, 128, self.rt128, self.bufs["rt128"],
                      self.hb[:, 0:512].rearrange("p (h d) -> p h d", h=4), self.bufs["hb"])
            self.transposes(self.hb, self.bufs["hb"], 4, TS, self.rqT, self.bufs["rqT"])
        G(C_RQ, [(0, 512, h_rq)])

        def h_rk(ps, pb):
            self.copy(tA[0:TS, 0:512], ps, [pb], [tAb])
            self.rope(tA[:, 0:512].rearrange("p (h d) -> p h d", h=4), tAb, TS, 4, 128, self.rt128, self.bufs["rt128"],
                      self.rk_tm, self.bufs["rk_tm"], scale=128 ** -0.5)
            self.transposes(self.rk_tm[:, :, :].rearrange("p a b -> p (a b)"), self.bufs["rk_tm"], 4, TS, self.rkT, self.bufs["rkT"])
        G(C_RK, [(0, 512, h_rk)])
        for j in range(2):
            def h_rv(ps, pb, j=j):
                self.copy(self.vr[0:TS, 2 * j:2 * j + 2, :].rearrange("p a b -> p (a b)"), ps, [pb], self.B("vr"))
            G(C_RV + 512 * j, [(0, 512, h_rv)])
        for j in range(2):
            def h_rg(ps, pb, j=j):
                o = self.gs_r[0:TS, 2 * j:2 * j + 2, :].rearrange("p a b -> p (a b)")
                T.op("act", lambda e: e.activation(o, ps, AF.Silu), [pb], self.B("gs_r"))
                T.op("dve", lambda e: e.tensor_tensor(o, o, self.gsm[0:TS, 1, 512 * j:512 * j + 512], OP.mult), self.B("gs_r", "gsm"), self.B("gs_r"))
            G(C_RG + 512 * j, [(0, 512, h_rg)])

    def stage_in(self, seq, t, l, TS):
        self.defer_on = True
        self._stage_in(seq, t, l, TS)
        self.flush()
        self.defer_on = False

    def bcast_rows(self, col, col_buf, L):
        T = self.T
        idb = self.ident[0:L, 0:L].unsqueeze(1).broadcast_to([L, 4, L])
        cb = col[0:L, 0:4].unsqueeze(2).broadcast_to([L, 4, L])
        T.op("dve", lambda e: e.tensor_tensor(self.diag[0:L, :, 0:L], idb, cb, OP.mult), [col_buf] + self.B("c32"), self.B("diag"))
        pt, pb = self.bank()
        for h in range(4):
            T.op("pe", lambda e, h=h: e.matmul(pt[0:L, h * L:(h + 1) * L], self.onesf[0:L, 0:L], self.diag[0:L, h, 0:L], start=True, stop=True),
                 self.B("diag", "onesf"), [pb])
        return pt[0:L, 0:4 * L].rearrange("p (h l) -> p h l", h=4), pb

    def mlstm(self, l, L):
        T = self.T
        s4 = self.sm4
        sb4 = {k: self.bufs["s4_" + k] for k in s4}
        ig, lf = self.igf[0:L, 0:4], self.igf[0:L, 4:8]
        cn, cnb, mbc = self.cn[l], self.cnb[l], self.mbc[l]
        cnB, cnbB, mbcB = self.bufs["cn%d" % l], self.bufs["cnb%d" % l], self.bufs["mbc%d" % l]
        b, a, cm, mx, nmx = (s4[k][0:L, :] for k in ("b", "a", "cm", "mx", "nmx"))
        pt, pb = self.bank()
        T.op("pe", lambda e: e.matmul(pt[0:L, 0:4], self.tri[0:L, 0:L], lf, start=True, stop=True), self.B("c32", "igf"), [pb])
        T.op("dve", lambda e: e.tensor_copy(b, pt[0:L, 0:4]), [pb], [sb4["b"]])
        T.op("dve", lambda e: e.tensor_tensor(a, ig, b, OP.subtract), [sb4["b"]] + self.B("igf"), [sb4["a"]])
        Abc, Ab = self.bcast_rows(s4["a"], sb4["a"], L)
        nm = self.negmask[0:L, 0:L].unsqueeze(1).broadcast_to([L, 4, L])
        T.op("dve", lambda e: e.tensor_tensor(self.Am[0:L, :, 0:L], Abc, nm, OP.add), [Ab] + self.B("c32"), self.B("Am"))
        T.op("dve", lambda e: e.tensor_reduce(cm, self.Am[0:L, :, 0:L], AX.X, OP.max), self.B("Am"), [sb4["cm"]])
        T.op("dve", lambda e: e.tensor_tensor(mx, cm, mbc[0:L, :], OP.max), [sb4["cm"], mbcB], [sb4["mx"]])
        T.op("dve", lambda e: e.tensor_scalar(nmx, mx, -1.0, None, OP.mult), [sb4["mx"]], [sb4["nmx"]])
        mt = self.sm8[0:L, 4:8]
        T.op("dve", lambda e: e.tensor_copy(self.sm8[0:L, 0:4], b), [sb4["b"]], self.B("sm8"))
        T.op("dve", lambda e: e.tensor_tensor(mt, b, mx, OP.add), [sb4["b"], sb4["mx"]], self.B("sm8"))
        ps_, emt = s4["ps"][0:L, :], s4["emt"][0:L, :]
        T.op("dve", lambda e: e.tensor_tensor(ps_, mbc[0:L, :], mx, OP.subtract), [mbcB, sb4["mx"]], [sb4["ps"]])
        T.op("act", lambda e: e.activation(ps_, ps_, AF.Exp), [sb4["ps"]], [sb4["ps"]])
        T.op("act", lambda e: e.activation(emt, mt, AF.Exp, scale=-1.0), self.B("sm8"), [sb4["emt"]])
        Bbc, Bb = self.bcast_rows(s4["nmx"], sb4["nmx"], L)
        nmT = self.negmaskT[0:L, 0:L].unsqueeze(1).broadcast_to([L, 4, L])
        T.op("dve", lambda e: e.tensor_tensor(self.Am[0:L, :, 0:L], Bbc, nmT, OP.add), [Bb] + self.B("c32"), self.B("Am"))
        for h in range(4):
            T.op("act", lambda e, h=h: e.activation(self.dmT[0:L, h, 0:L], self.Am[0:L, h, 0:L], AF.Exp, bias=s4["a"][0:L, h:h + 1]),
                 self.B("Am") + [sb4["a"]], self.B("dmT"))
        pt, pb = self.bank()
        for h in range(4):
            T.op("pe", lambda e, h=h: e.matmul(pt[0:L, h * L:(h + 1) * L], self.kmT[:, h, 0:L], self.qmT[:, h, 0:L], start=True, stop=True),
                 self.B("kmT", "qmT"), [pb])
        T.op("dve", lambda e: e.tensor_tensor(self.wT[0:L, :, 0:L], pt[0:L, 0:4 * L].rearrange("p (h l) -> p h l", h=4), self.dmT[0:L, :, 0:L], OP.mult),
             [pb] + self.B("dmT"), self.B("wT"))
        for h in range(4):
            p1, b1 = self.bank()
            p2, b2 = self.bank()
            T.op("pe", lambda e, h=h: e.matmul(p1[0:L, 0:257], self.wT[0:L, h, 0:L], self.vm[0:L, h, :], start=True, stop=True), self.B("wT", "vm"), [b1])
            T.op("pe", lambda e, h=h: e.matmul(p2[0:L, 0:257], self.qmT[:, h, 0:L], cnb[:, h, :], start=True, stop=True), [self.bufs["qmT"], cnbB], [b2])
            T.op("act", lambda e, h=h: e.activation(self.inter[0:L, h, :], p2[0:L, 0:257], AF.Identity, scale=s4["ps"][0:L, h:h + 1]), [b2, sb4["ps"]], self.B("inter"))
            T.op("dve", lambda e, h=h: e.tensor_tensor(self.hsum[0:L, h, :], p1[0:L, 0:257], self.inter[0:L, h, :], OP.add), [b1] + self.B("inter"), self.B("hsum"))
        den, rden, ss, sc = (s4[k][0:L, :] for k in ("den", "rden", "ss", "sc"))
        T.op("act", lambda e: e.activation(den, self.hsum[0:L, :, 256], AF.Abs), self.B("hsum"), [sb4["den"]])
        T.op("dve", lambda e: e.tensor_tensor(den, den, emt, OP.max), [sb4["den"], sb4["emt"]], [sb4["den"]])
        T.op("dve", lambda e: e.reciprocal(rden, den), [sb4["den"]], [sb4["rden"]])
        hv = self.hsum[0:L, :, 0:256]
        T.op("dve", lambda e: e.tensor_tensor(self.sq[0:L, :, :], hv, hv, OP.mult), self.B("hsum"), self.B("sq"))
        T.op("dve", lambda e: e.tensor_reduce(ss, self.sq[0:L, :, :], AX.X, OP.add), self.B("sq"), [sb4["ss"]])
        T.op("dve", lambda e: e.tensor_tensor(ss, ss, rden, OP.mult), [sb4["ss"], sb4["rden"]], [sb4["ss"]])
        T.op("dve", lambda e: e.tensor_tensor(ss, ss, rden, OP.mult), [sb4["ss"], sb4["rden"]], [sb4["ss"]])
        T.op("act", lambda e: e.activation(ss, ss, AF.Sqrt, bias=self.epsc[0:L, 0:1], scale=1.0 / 256), [sb4["ss"], self.bufs["epsc"]], [sb4["ss"]])
        T.op("dve", lambda e: e.reciprocal(ss, ss), [sb4["ss"]], [sb4["ss"]])
        T.op("dve", lambda e: e.tensor_tensor(sc, ss, rden, OP.mult), [sb4["ss"], sb4["rden"]], [sb4["sc"]])
        T.op("dve", lambda e: e.tensor_tensor(self.sq[0:L, :, :], hv, s4["sc"][0:L, :].unsqueeze(2).broadcast_to([L, 4, 256]), OP.mult),
             self.B("hsum") + [sb4["sc"]], self.B("sq"))
        T.op("dve", lambda e: e.tensor_tensor(self.ytm[0:L, :].rearrange("p (h v) -> p h v", h=4), self.sq[0:L, :, :], self.gs_m[0:L, :, :], OP.mult),
             self.B("sq", "gs_m"), self.B("ytm"))
        yield
        self.transposes(self.ytm, self.bufs["ytm"], 8, L, self.yT["m"], self.bufs["yT_m"])
        sel = self.c32[0:L, 512:640] if L == 128 else self.c32[0:L, 1172:1300]
        pt, pb = self.bank()
        T.op("pe", lambda e: e.matmul(pt[:, 0:8], sel, self.sm8[0:L, :], start=True, stop=True), self.B("c32", "sm8"), [pb])
        T.op("dve", lambda e: e.tensor_copy(self.bc8[:, :], pt[:, 0:8]), [pb], self.B("bc8"))
        blast, mnew = self.bc8[:, 0:4], self.bc8[:, 4:8]
        t1 = s4["t1"]
        T.op("dve", lambda e: e.tensor_tensor(t1[:, :], blast, mnew, OP.subtract), self.B("bc8"), [sb4["t1"]])
        ws = s4["ws"][0:L, :]
        T.op("dve", lambda e: e.tensor_tensor(ws, a, t1[0:L, :], OP.add), [sb4["a"], sb4["t1"]], [sb4["ws"]])
        T.op("act", lambda e: e.activation(ws, ws, AF.Exp), [sb4["ws"]], [sb4["ws"]])
        T.op("dve", lambda e: e.tensor_tensor(self.carry[:, :], t1[:, :], mbc[:, :], OP.add), [sb4["t1"], mbcB], self.B("carry"))
        T.op("act", lambda e: e.activation(self.carry[:, :], self.carry[:, :], AF.Exp), self.B("carry"), self.B("carry"))
        T.op("dve", lambda e: e.tensor_tensor(self.kws[0:L, :, :], self.km_tm[0:L, :, :], s4["ws"][0:L, :].unsqueeze(2).broadcast_to([L, 4, 128]), OP.mult),
             self.B("km_tm") + [sb4["ws"]], self.B("kws"))
        for h in range(4):
            p1, b1 = self.bank()
            T.op("pe", lambda e, h=h: e.matmul(p1[:, 0:257], self.kws[0:L, h, :], self.vm[0:L, h, :], start=True, stop=True), self.B("kws", "vm"), [b1])
            T.op("dve", lambda e, h=h: e.scalar_tensor_tensor(cn[:, h, :], cn[:, h, :], self.carry[:, h:h + 1], p1[:, 0:257], OP.mult, OP.add),
                 [cnB, b1] + self.B("carry"), [cnB])
        T.op("act", lambda e: e.activation(cnb[:, :, :], cn[:, :, :], AF.Identity), [cnB], [cnbB])
        T.op("dve", lambda e: e.tensor_copy(mbc[:, :], mnew), self.B("bc8"), [mbcB])

    def retention(self, l, L):
        T = self.T
        s4 = self.sm4
        sb4 = {k: self.bufs["s4_" + k] for k in s4}
        rs, rb = self.rs[l], self.rb[l]
        rsB, rbB = self.bufs["rs%d" % l], self.bufs["rb%d" % l]
        pt, pb = self.bank()
        for h in range(4):
            T.op("pe", lambda e, h=h: e.matmul(pt[0:L, h * L:(h + 1) * L], self.rkT[:, h, 0:L], self.rqT[:, h, 0:L], start=True, stop=True),
                 self.B("rkT", "rqT"), [pb])
        T.op("dve", lambda e: e.tensor_tensor(self.wT[0:L, :, 0:L], pt[0:L, 0:4 * L].rearrange("p (h l) -> p h l", h=4), self.DT[0:L, :, 0:L], OP.mult),
             [pb] + self.B("c32"), self.B("wT"))
        o = self.hsum[0:L, :, 0:256]
        for h in range(4):
            p1, b1 = self.bank()
            p2, b2 = self.bank()
            T.op("pe", lambda e, h=h: e.matmul(p1[0:L, 0:256], self.wT[0:L, h, 0:L], self.vr[0:L, h, :], start=True, stop=True), self.B("wT", "vr"), [b1])
            T.op("pe", lambda e, h=h: e.matmul(p2[0:L, 0:256], self.rqT[:, h, 0:L], rb[:, h, :], start=True, stop=True), [self.bufs["rqT"], rbB], [b2])
            T.op("act", lambda e, h=h: e.activation(self.inter[0:L, h, 0:256], p2[0:L, 0:256], AF.Identity, scale=self.misc[0:L, h:h + 1]), [b2] + self.B("c32"), self.B("inter"))
            T.op("dve", lambda e, h=h: e.tensor_tensor(self.hsum[0:L, h, 0:256], p1[0:L, 0:256], self.inter[0:L, h, 0:256], OP.add), [b1] + self.B("inter"), self.B("hsum"))
        mean, ss = s4["mean"][0:L, :], s4["ss"][0:L, :]
        T.op("dve", lambda e: e.tensor_reduce(mean, o, AX.X, OP.add), self.B("hsum"), [sb4["mean"]])
        T.op("dve", lambda e: e.tensor_scalar(mean, mean, 1.0 / 256, None, OP.mult), [sb4["mean"]], [sb4["mean"]])
        T.op("dve", lambda e: e.tensor_tensor(o, o, s4["mean"][0:L, :].unsqueeze(2).broadcast_to([L, 4, 256]), OP.subtract), self.B("hsum") + [sb4["mean"]], self.B("hsum"))
        T.op("dve", lambda e: e.tensor_tensor(self.sq[0:L, :, :], o, o, OP.mult), self.B("hsum"), self.B("sq"))
        T.op("dve", lambda e: e.tensor_reduce(ss, self.sq[0:L, :, :], AX.X, OP.add), self.B("sq"), [sb4["ss"]])
        T.op("act", lambda e: e.activation(ss, ss, AF.Sqrt, bias=self.epsc[0:L, 0:1], scale=1.0 / 256), [sb4["ss"], self.bufs["epsc"]], [sb4["ss"]])
        T.op("dve", lambda e: e.reciprocal(ss, ss), [sb4["ss"]], [sb4["ss"]])
        T.op("dve", lambda e: e.tensor_tensor(self.sq[0:L, :, :], o, s4["ss"][0:L, :].unsqueeze(2).broadcast_to([L, 4, 256]), OP.mult),
             self.B("hsum") + [sb4["ss"]], self.B("sq"))
        T.op("dve", lambda e: e.tensor_tensor(self.ytm_r[0:L, :].rearrange("p (h v) -> p h v", h=4), self.sq[0:L, :, :], self.gs_r[0:L, :, :], OP.mult),
             self.B("sq", "gs_r"), self.B("ytm_r"))
        yield
        self.transposes(self.ytm_r, self.bufs["ytm_r"], 8, L, self.yT["r"], self.bufs["yT_r"])
        zoff = 4 if L == 128 else 8
        doff = 12 if L == 128 else 16
        zeta = self.misc[0:L, zoff:zoff + 4].unsqueeze(2).broadcast_to([L, 4, 128])
        T.op("dve", lambda e: e.tensor_tensor(self.kws[0:L, :, :], self.rk_tm[0:L, :, :], zeta, OP.mult), self.B("rk_tm", "c32"), self.B("kws"))
        for h in range(4):
            p1, b1 = self.bank()
            T.op("pe", lambda e, h=h: e.matmul(p1[:, 0:256], self.kws[0:L, h, :], self.vr[0:L, h, :], start=True, stop=True), self.B("kws", "vr"), [b1])
            T.op("dve", lambda e, h=h: e.scalar_tensor_tensor(rs[:, h, :], rs[:, h, :], self.misc[:, doff + h:doff + h + 1], p1[:, 0:256], OP.mult, OP.add),
                 [rsB, b1] + self.B("c32"), [rsB])
        T.op("act", lambda e: e.activation(rb[:, :, :], rs[:, :, :], AF.Identity), [rsB], [rbB])

    def kv_up(self, l, TS, seqname, kt):
        T = self.T
        kn_d, kr_d, vv_d = self.kv[seqname]
        bkn, bkr, bvv = (self.bufs[n + "_" + seqname] for n in ("kn", "kr", "vv"))
        w = self.w_ukv[l]
        for half in range(2):
            view, wbuf, _ = self.load_w(w, 0, 4, CK(lambda s, half=half: s.rearrange("r (h c) -> r h c", h=8)[:, 4 * half:4 * half + 4, 0:128], ("n", half)))
            pt, pb = self.bank()
            for hh in range(4):
                for kc in range(4):
                    T.op("pe", lambda e, hh=hh, kc=kc: e.matmul(pt[:, hh * 128:hh * 128 + TS], view[:, kc, hh * 128:(hh + 1) * 128], self.ckvT[:, kc, 0:TS],
                                                                start=(kc == 0), stop=(kc == 3)), [wbuf] + self.B("ckvT"), [pb])
            self.copy(self.KnT[:, 4 * half:4 * half + 4, 0:TS], pt[:, :].rearrange("p (h t) -> p h t", h=4)[:, :, 0:TS], [pb], self.B("KnT"))
        for half in range(2):
            def hv(ps, pb, half=half):
                self.copy(self.Vst[0:TS, 4 * half:4 * half + 4, 0:128], ps.rearrange("p (h v) -> p h v", h=4), [pb], self.B("Vst"))
            self.gemm(self.ckvT, self.bufs["ckvT"], 4, TS, w,
                      CK(lambda s, half=half: s.rearrange("r (h c) -> r h c", h=8)[:, 4 * half:4 * half + 4, 128:256], ("v", half)), [(0, 512, hv)])
        T.dma("sp", kn_d[l, kt].rearrange("p (h t) -> p h t", h=8)[:, :, 0:TS], self.KnT[:, :, 0:TS], self.bufs["KnT"], self.B("KnT"), [bkn])
        T.dma("sp", kr_d[l, kt][:, 0:TS], self.krT[:, 0:TS], self.bufs["krT"], self.B("krT"), [bkr])
        T.dma("sp", vv_d[l, kt][0:TS, :], self.Vst[0:TS, :, :].rearrange("p h v -> p (h v)"), self.bufs["Vst"], self.B("Vst"), [bvv])

    def mla(self, seq, t, l, TS):
        T = self.T
        w = self.w_uq[l]
        for half in range(2):
            view, wbuf, _ = self.load_w(w, 0, 4, CK(lambda s, half=half: s.rearrange("r (h c) -> r h c", h=8)[:, 4 * half:4 * half + 4, 0:128], ("n", half)))
            pt, pb = self.bank()
            for hh in range(4):
                for kc in range(4):
                    T.op("pe", lambda e, hh=hh, kc=kc: e.matmul(pt[:, hh * 128:hh * 128 + TS], view[:, kc, hh * 128:(hh + 1) * 128], self.hqT[:, kc, 0:TS],
                                                                start=(kc == 0), stop=(kc == 3)), [wbuf] + self.B("hqT"), [pb])
            self.copy(self.QnT[:, 4 * half:4 * half + 4, 0:TS], pt[:, :].rearrange("p (h t) -> p h t", h=4)[:, :, 0:TS], [pb], self.B("QnT"))
        tA, tAb = self.tmpA, self.bufs["tmpA"]

        def hqr(ps, pb):
            self.copy(tA[0:TS, 0:512], ps, [pb], [tAb])
            self.rope(tA[:, 0:512].rearrange("p (h d) -> p h d", h=8), tAb, TS, 8, 64, self.rt64, self.bufs["rt64"],
                      self.hb[:, 0:512].rearrange("p (h d) -> p h d", h=8), self.bufs["hb"])
            self.transposes(self.hb, self.bufs["hb"], 4, TS, self.QrT, self.bufs["QrT"])
        self.gemm(self.hqT, self.bufs["hqT"], 4, TS, w, CK(lambda s: s.rearrange("r (h c) -> r h c", h=8)[:, :, 128:192], ("r",)), [(0, 512, hqr)])
        kt_own = seq.kt0 + t
        kn_d, kr_d, vv_d = self.kv[seq.name]
        bkn, bkr, bvv = (self.bufs[n + "_" + seq.name] for n in ("kn", "kr", "vv"))

        def load_kv(kt, KS):
            ks, rs_, vs, sbuf = self.kvs[self.kvi % 2]
            self.kvi += 1
            T.dma("sp", ks[:, :, 0:KS], kn_d[l, kt].rearrange("p (h t) -> p h t", h=8)[:, :, 0:KS], sbuf, [bkn], [sbuf])
            T.dma("sp", rs_[:, 0:KS], kr_d[l, kt][:, 0:KS], sbuf, [bkr], [sbuf])
            T.dma("sp", vs[0:KS, :, :].rearrange("p h v -> p (h v)"), vv_d[l, kt][0:KS, :], sbuf, [bvv], [sbuf])
            return ks, rs_, vs, sbuf
        pre = {}
        for kt in range(min(2, kt_own)):
            pre[kt] = load_kv(kt, 128)
        self.kv_up(l, TS, seq.name, kt_own)
        yield
        kn_d, kr_d, vv_d = self.kv[seq.name]
        bkn, bkr, bvv = (self.bufs[n + "_" + seq.name] for n in ("kn", "kr", "vv"))
        nkt = kt_own + 1
        scale = 192 ** -0.5
        self.attn_active = True
        accs = [self.ps[5], self.ps[6], self.ps[7]]
        hb_of = lambda h: (accs[h // 3], (h % 3) * 129)
        for kt in range(nkt):
            KS = TS if kt == kt_own else 128
            ks, rs_, vs, sbuf = pre[kt] if kt in pre else load_kv(kt, KS)
            for half in range(2):
                pt, pb = self.bank()
                for hh in range(4):
                    h = 4 * half + hh
                    po = (h % 2) * 64
                    T.op("pe", lambda e, h=h, hh=hh: e.matmul(pt[0:KS, hh * 128:hh * 128 + TS], ks[:, h, 0:KS], self.QnT[:, h, 0:TS], start=True, stop=False),
                         [sbuf] + self.B("QnT"), [pb])
                    T.op("pe", lambda e, h=h, hh=hh, po=po: e.matmul(pt[0:KS, hh * 128:hh * 128 + TS], rs_[po:po + 64, 0:KS], self.QrT[po:po + 64, h // 2, 0:TS],
                                                                       start=False, stop=True), [sbuf] + self.B("QrT"), [pb])
                T.op("act", lambda e, half=half: e.activation(self.PT[0:KS, 4 * half:4 * half + 4, 0:TS],
                                                               pt[0:KS, :].rearrange("p (h t) -> p h t", h=4)[:, :, 0:TS], AF.Exp, scale=scale),
                     [pb], self.B("PT"))
            if seq.causal and kt == kt_own and TS == 128:
                T.op("dve", lambda e: e.memset(self.PT[64:128, :, 0:64], 0.0), [], self.B("PT"))
            for h in range(8):
                (at, ab), co = hb_of(h)
                T.op("pe", lambda e, h=h, at=at, co=co: e.matmul(at[0:TS, co:co + 129], self.PT[0:KS, h, 0:TS], vs[0:KS, h, :], start=(kt == 0 and h % 3 == 0), stop=(kt == nkt - 1)),
                     [sbuf] + self.B("PT"), [ab])
        for i, (at, ab) in enumerate(accs):
            nh = 3 if i < 2 else 2
            v3 = at[0:TS, 0:nh * 129].rearrange("p (h v) -> p h v", h=nh)
            rd = self.sm8[0:TS, 0:nh]
            T.op("dve", lambda e, v3=v3, rd=rd: e.reciprocal(rd, v3[:, :, 128]), [ab], self.B("sm8"))
            T.op("dve", lambda e, v3=v3, rd=rd, i=i, nh=nh: e.tensor_tensor(
                self.ytm[0:TS, i * 384:i * 384 + nh * 128].rearrange("p (h v) -> p h v", h=nh), v3[:, :, 0:128],
                rd.unsqueeze(2).broadcast_to([TS, nh, 128]), OP.mult), [ab] + self.B("sm8"), self.B("ytm"))
        self.transposes(self.ytm, self.bufs["ytm"], 8, TS, self.yT["a"], self.bufs["yT_a"])
        self.attn_active = False

    def merge_out(self, l, TS):
        T = self.T
        tA, tAb = self.tmpA, self.bufs["tmpA"]
        mg = self.gbuf
        for cb in range(4):
            for bi, (br, goff, wup) in enumerate((("m", C_GM, self.w_up_m), ("a", C_GA, self.w_up_a), ("r", C_GR, self.w_up_r))):
                hold = {}

                def hg(ps, pb):
                    T.op("act", lambda e: e.activation(self.sg[0:TS, :], ps, AF.Sigmoid), [pb], self.B("sg"))
                self.gemm(self.hT, self.bufs["hT"], 16, TS, self.w_in[l], (goff + cb * 512, 512), [(0, 512, hg)], bias_off=l * DIN + goff + cb * 512)

                def hu(ps, pb, bi=bi, cb=cb):
                    dst = tA[0:TS, cb * 512:(cb + 1) * 512]
                    if bi == 0:
                        T.op("dve", lambda e: e.tensor_tensor(dst, ps, self.sg[0:TS, :], OP.mult), [pb] + self.B("sg"), [tAb])
                    else:
                        T.op("dve", lambda e: e.tensor_tensor(self.tmpB[0:TS, :], ps, self.sg[0:TS, :], OP.mult), [pb] + self.B("sg"), self.B("tmpB"))
                        T.op("dve", lambda e: e.tensor_tensor(dst, dst, self.tmpB[0:TS, :], OP.add), [tAb] + self.B("tmpB"), [tAb])
                self.gemm(self.yT[br], self.bufs["yT_" + br], 8, TS, wup[l], (cb * 512, 512), [(0, 512, hu)])
        T.op("act", lambda e: e.activation(self.hb[0:TS, :], tA[0:TS, :], AF.Identity), [tAb], self.B("hb"))
        self.transposes(self.hb, self.bufs["hb"], 16, TS, self.hT, self.bufs["hT"])
        for cb in range(4):
            def ho(ps, pb, cb=cb):
                self.copy(tA[0:TS, cb * 512:(cb + 1) * 512], ps, [pb], [tAb])
            self.gemm(self.hT, self.bufs["hT"], 16, TS, self.w_o[l], (cb * 512, 512), [(0, 512, ho)])
        self.resid_add(l, TS, self.g_mix_post[l])

    def ffn(self, l, TS):
        T = self.T
        tA, tAb = self.tmpA, self.bufs["tmpA"]
        self.norm_to_hT(l, TS, self.g_ffn_pre[l])
        for j in range(11):
            def hg(ps, pb):
                T.op("act", lambda e: e.activation(self.sg[0:TS, :], ps, AF.Silu), [pb], self.B("sg"))
            self.gemm(self.hT, self.bufs["hT"], 16, TS, self.w_gu[l], (DFF + j * 512, 512), [(0, 512, hg)])

            def ha(ps, pb, j=j):
                jj = j if j < 6 else j - 6
                T.op("dve", lambda e: e.tensor_tensor(self.act_tm[0:TS, jj * 512:(jj + 1) * 512], ps, self.sg[0:TS, :], OP.mult), [pb] + self.B("sg"), self.B("act_tm"))
            self.gemm(self.hT, self.bufs["hT"], 16, TS, self.w_gu[l], (j * 512, 512), [(0, 512, ha)])
            if j == 5:
                self.transposes(self.act_tm, self.bufs["act_tm"], 24, TS, self.actT, self.bufs["actT"])
            if j == 10:
                self.transposes(self.act_tm, self.bufs["act_tm"], 20, TS, self.actT, self.bufs["actT"], dst_k0=24)
        for cb in range(4):
            def ho(ps, pb, cb=cb):
                self.copy(tA[0:TS, cb * 512:(cb + 1) * 512], ps, [pb], [tAb])
            self.gemm(self.actT, self.bufs["actT"], 44, TS, self.w_down[l], (cb * 512, 512), [(0, 512, ho)])
        self.resid_add(l, TS, self.g_ffn_post[l])

    def merge2(self, l, TS):
        T = self.T
        Bf = self.bufs
        sets = []
        for i, (tA, tAb, sg, sgb) in enumerate(((self.tmpA, Bf["tmpA"], self.sg, Bf["sg"]), (self.tmpA_B, Bf["tmpA_B"], self.tmpB, Bf["tmpB"]))):
            tl = self.tiles_[i]
            sets.append(dict(hT=tl["hT"][0], hTb=tl["hT"][1], yT=tl["yT"], tA=tA, tAb=tAb, sg=sg, sgb=sgb))
        tmp = self.sqf[:, :, :].rearrange("p a b -> p (a b)")[:, 0:512]
        tmpb = Bf["sq"]
        xs = [(st["hT"], st["hTb"]) for st in sets]
        self.defer_on = True
        for cb in range(4):
            for bi, (br, goff, wup) in enumerate((("m", C_GM, self.w_up_m), ("a", C_GA, self.w_up_a), ("r", C_GR, self.w_up_r))):
                def hg(ps, pb, xi):
                    st = sets[xi]
                    T.op("act", lambda e: e.activation(st["sg"][0:TS, :], ps, AF.Sigmoid), [pb], [st["sgb"]])
                self.gemm(xs, None, 16, TS, self.w_in[l], (goff + cb * 512, 512), [(0, 512, hg)], bias_off=l * DIN + goff + cb * 512)

                def hu(ps, pb, xi, bi=bi, cb=cb):
                    st = sets[xi]
                    dst = st["tA"][0:TS, cb * 512:(cb + 1) * 512]
                    if bi == 0:
                        T.op("dve", lambda e: e.tensor_tensor(dst, ps, st["sg"][0:TS, :], OP.mult), [pb, st["sgb"]], [st["tAb"]])
                    else:
                        T.op("dve", lambda e: e.tensor_tensor(tmp[0:TS, :], ps, st["sg"][0:TS, :], OP.mult), [pb, st["sgb"]], [tmpb])
                        T.op("dve", lambda e: e.tensor_tensor(dst, dst, tmp[0:TS, :], OP.add), [st["tAb"], tmpb], [st["tAb"]])
                self.gemm([(st["yT"][br][0], st["yT"][br][1]) for st in sets], None, 8, TS, wup[l], (cb * 512, 512), [(0, 512, hu)])
        self.flush()
        for st in sets:
            T.op("act", lambda e, st=st: e.activation(self.hb[0:TS, :], st["tA"][0:TS, :], AF.Identity), [st["tAb"]], self.B("hb"))
            self.transposes(self.hb, Bf["hb"], 16, TS, st["hT"], st["hTb"])
        for cb in range(4):
            def ho(ps, pb, xi, cb=cb):
                st = sets[xi]
                self.copy(st["tA"][0:TS, cb * 512:(cb + 1) * 512], ps, [pb], [st["tAb"]])
            self.gemm(xs, None, 16, TS, self.w_o[l], (cb * 512, 512), [(0, 512, ho)])
        self.flush()
        self.defer_on = False
        for i in range(2):
            self.set_x(i)
            self.resid_add(l, TS, self.g_mix_post[l], src=(sets[i]["tA"], sets[i]["tAb"]))

    def ffn2(self, l, TS):
        T = self.T
        Bf = self.bufs
        sets = [
            dict(hT=self.hT, hTb=Bf["hT"], act=self.act_tm, actb=Bf["act_tm"], actT=self.actT, actTb=Bf["actT"], tA=self.tmpA, tAb=Bf["tmpA"], sg=self.sg, sgb=Bf["sg"]),
            dict(hT=self.hT_B, hTb=Bf["hT_B"], act=self.act_tm_B, actb=Bf["act_tm_B"], actT=self.actT_B, actTb=Bf["actT_B"], tA=self.tmpA_B, tAb=Bf["tmpA_B"], sg=self.tmpB, sgb=Bf["tmpB"]),
        ]
        for i in range(2):
            self.set_x(i)
            self.norm_to_hT(l, TS, self.g_ffn_pre[l], dst=(sets[i]["hT"], sets[i]["hTb"]))
        xs = [(st["hT"], st["hTb"]) for st in sets]
        self.defer_on = True
        for j in range(11):
            def hg(ps, pb, xi):
                st = sets[xi]
                T.op("act", lambda e: e.activation(st["sg"][0:TS, :], ps, AF.Silu), [pb], [st["sgb"]])
            self.gemm(xs, None, 16, TS, self.w_gu[l], (DFF + j * 512, 512), [(0, 512, hg)])

            def ha(ps, pb, xi, j=j):
                st = sets[xi]
                jj = j if j < 6 else j - 6
                T.op("dve", lambda e: e.tensor_tensor(st["act"][0:TS, jj * 512:(jj + 1) * 512], ps, st["sg"][0:TS, :], OP.mult), [pb, st["sgb"]], [st["actb"]])
            self.gemm(xs, None, 16, TS, self.w_gu[l], (j * 512, 512), [(0, 512, ha)])
            if j == 5:
                self.flush()
                for st in sets:
                    self.transposes(st["act"], st["actb"], 24, TS, st["actT"], st["actTb"])
            if j == 10:
                self.flush()
                for st in sets:
                    self.transposes(st["act"], st["actb"], 20, TS, st["actT"], st["actTb"], dst_k0=24)
        xs2 = [(st["actT"], st["actTb"]) for st in sets]
        for cb in range(4):
            def ho(ps, pb, xi, cb=cb):
                st = sets[xi]
                self.copy(st["tA"][0:TS, cb * 512:(cb + 1) * 512], ps, [pb], [st["tAb"]])
            self.gemm(xs2, None, 44, TS, self.w_down[l], (cb * 512, 512), [(0, 512, ho)])
        self.flush()
        self.defer_on = False
        for i in range(2):
            self.set_x(i)
            self.resid_add(l, TS, self.g_ffn_post[l], src=(sets[i]["tA"], sets[i]["tAb"]))

    def init_states(self, seq):
        T = self.T
        for l in range(self.nl):
            cnB, cnbB, mbcB = self.bufs["cn%d" % l], self.bufs["cnb%d" % l], self.bufs["mbc%d" % l]
            rsB, rbB = self.bufs["rs%d" % l], self.bufs["rb%d" % l]
            if seq.name == "p":
                T.op("dve", lambda e, l=l: e.memset(self.cn[l][:, :, :], 0.0), [], [cnB])
                T.op("dve", lambda e, l=l: e.memset(self.cnb[l][:, :, :], 0.0), [], [cnbB])
                T.op("dve", lambda e, l=l: e.memset(self.mbc[l][:, :], 0.0), [], [mbcB])
                T.op("dve", lambda e, l=l: e.memset(self.rs[l][:, :, :], 0.0), [], [rsB])
                T.op("dve", lambda e, l=l: e.memset(self.rb[l][:, :, :], 0.0), [], [rbB])
            else:
                for h in range(4):
                    T.dma("sp", self.stg[:, :, :], self.sc[l, h].rearrange("(a p) k -> p a k", p=128), self.bufs["stg"], [], self.B("stg"))
                    pt, pb = self.bank()
                    for a_ in range(2):
                        T.op("pe", lambda e, a_=a_: e.transpose(pt[:, a_ * 128:(a_ + 1) * 128], self.stg[:, a_, :], self.ident), self.B("stg", "c32"), [pb])
                    T.op("dve", lambda e, l=l, h=h: e.tensor_copy(self.cn[l][:, h, 0:256], pt[:, 0:256]), [pb], [cnB])
                T.dma("sp", self.cn[l][:, :, 256], self.sn[l].rearrange("h k -> k h"), cnB, [], [cnB], allow_slow_non_contiguous=True)
                T.dma("sp", self.mbc[l][:, :], self.sm[l].partition_broadcast(128), mbcB, [], [mbcB])
                T.dma("sp", self.rs[l][:, :, :], self.sr[l].rearrange("h k v -> k h v"), rsB, [], [rsB])
                T.op("act", lambda e, l=l: e.activation(self.cnb[l][:, :, :], self.cn[l][:, :, :], AF.Identity), [cnB], [cnbB])
                T.op("act", lambda e, l=l: e.activation(self.rb[l][:, :, :], self.rs[l][:, :, :], AF.Identity), [rsB], [rbB])

    def store_states(self, seq):
        T = self.T
        oc, on, om, orr = seq.o_states
        for l in range(self.nl):
            cnB, mbcB, rsB = self.bufs["cn%d" % l], self.bufs["mbc%d" % l], self.bufs["rs%d" % l]
            for h in range(4):
                pt, pb = self.bank()
                for a_ in range(2):
                    T.op("pe", lambda e, a_=a_, l=l, h=h: e.transpose(pt[:, a_ * 128:(a_ + 1) * 128], self.cn[l][:, h, a_ * 128:(a_ + 1) * 128], self.ident), [cnB] + self.B("c32"), [pb])
                T.op("dve", lambda e: e.tensor_copy(self.stg[:, :, :].rearrange("p a k -> p (a k)"), pt[:, 0:256]), [pb], self.B("stg"))
                T.dma("sp", oc[l, h].rearrange("(a p) k -> p a k", p=128), self.stg[:, :, :], self.bufs["stg"], self.B("stg"), [])
            T.dma("sp", on[l].rearrange("h k -> k h"), self.cn[l][:, :, 256], cnB, [cnB], [], allow_slow_non_contiguous=True)
            T.dma("sp", om[l:l + 1, :], self.mbc[l][0:1, :], mbcB, [mbcB], [])
            T.dma("sp", orr[l].rearrange("h k v -> k h v"), self.rs[l][:, :, :], rsB, [rsB], [])

    def past_kv(self, seq):
        T = self.T
        for l in range(self.nl):
            for kt in range(8):
                T.dma("sp", self.ckv_o[:, :], self.cckv[l, kt * 128:(kt + 1) * 128, :], self.bufs["ckv_o"], [], self.B("ckv_o"))
                T.op("act", lambda e: e.activation(self.hb[:, 0:512], self.ckv_o[:, :], AF.Identity), self.B("ckv_o"), self.B("hb"))
                self.transposes(self.hb, self.bufs["hb"], 4, 128, self.ckvT, self.bufs["ckvT"])
                T.dma("sp", self.kr_o[:, :], self.ckr[l, kt * 128:(kt + 1) * 128, :], self.bufs["kr_o"], [], self.B("kr_o"))
                T.op("act", lambda e: e.activation(self.krd[:, 0:64], self.kr_o[:, :], AF.Identity), self.B("kr_o"), self.B("krd"))
                T.op("act", lambda e: e.activation(self.krd[:, 64:128], self.kr_o[:, :], AF.Identity), self.B("kr_o"), self.B("krd"))
                self.transposes(self.krd, self.bufs["krd"], 1, 128, self.krT[:, :].rearrange("p (o t) -> p o t", o=1), self.bufs["krT"])
                self.kv_up(l, 128, "s", kt)

    def mixer_core(self, seq, t, l, TS):
        self.norm_to_hT(l, TS, self.g_mix_pre[l])
        self.stage_in(seq, t, l, TS)
        gm, gr, ga = self.mlstm(l, TS), self.retention(l, TS), self.mla(seq, t, l, TS)
        next(gm)
        next(gr)
        next(ga)
        next(gm, None)
        next(gr, None)
        next(ga, None)

    def run_seq(self, seq):
        T = self.T
        self.init_states(seq)
        self.set_tile(1)
        if seq.name == "s":
            self.past_kv(seq)
        TS = seq.TS
        t = 0
        while t < seq.ntiles:
            pair = (TS == 128 and t + 1 < seq.ntiles)
            tiles = [t, t + 1] if pair else [t]
            for i, tt in enumerate(tiles):
                self.set_x(i)
                T.dma("sp", self.x[0:TS, :], seq.x_in[tt * 128:tt * 128 + TS, :], self.bufs["x"], [], self.B("x"))
            for l in range(self.nl):
                self.load_layer_consts(l)
                if pair:
                    for i, tt in enumerate(tiles):
                        self.set_x(i)
                        self.set_tile(i)
                        self.mixer_core(seq, tt, l, TS)
                    self.set_tile(1)
                    self.barrier()
                    self.merge2(l, TS)
                    self.barrier()
                    self.ffn2(l, TS)
                    self.barrier()
                else:
                    self.set_x(0)
                    self.set_tile(1)
                    self.mixer_core(seq, t, l, TS)
                    self.merge_out(l, TS)
                    self.ffn(l, TS)
            for i, tt in enumerate(tiles):
                self.set_x(i)
                T.dma("sp", seq.y_out[tt * 128:tt * 128 + TS, :], self.x[0:TS, :], self.bufs["x"], self.B("x"), [])
            t += len(tiles)
        self.set_x(0)
        self.set_tile(1)
        self.store_states(seq)

    def build(self):
        self.prologue()
        seqs = []
        p = Seq()
        p.name, p.ntiles, p.TS, p.tab0, p.kt0, p.causal = "p", self.S_P // 128, 128, 0, 0, True
        p.x_in, p.y_out, p.o_ckv, p.o_kr = self.xp, self.yp, self.o_pckv, self.o_pkr
        p.o_states = (self.o_pc, self.o_pn, self.o_pm, self.o_pr)
        seqs.append(p)
        if self.do_sample:
            s = Seq()
            s.name, s.ntiles, s.TS, s.tab0, s.kt0, s.causal = "s", 1, 64, self.S_P, 8, False
            s.x_in, s.y_out, s.o_ckv, s.o_kr = self.xs, self.ys, self.o_sckv, self.o_skr
            s.o_states = (self.o_sc, self.o_sn, self.o_sm, self.o_sr)
            seqs.append(s)
        for s in seqs:
            self.run_seq(s)
        self.T.wait_all("sp", list(Buf.REG))
        return self.nc


def const_tables(S_P):
    pos = np.concatenate([np.arange(S_P), PAST + np.arange(SS)]).astype(np.float32)

    def tabs(d):
        inv = (1.0 / (10000.0 ** (np.arange(0, d, 2, dtype=np.float32) / np.float32(d)))).astype(np.float32)
        ang = pos[:, None] * inv[None, :]
        return np.cos(ang).astype(np.float32), np.sin(ang).astype(np.float32)
    c128, s128 = tabs(128)
    c64, s64 = tabs(64)
    i = np.arange(128)
    ident = np.eye(128, dtype=np.float32)
    tri = (i[:, None] <= i[None, :]).astype(np.float32)
    negmask = np.where(i[None, :] <= i[:, None], 0.0, NEG).astype(np.float32)
    negmaskT = negmask.T.copy()
    sel = np.zeros((128, 128), np.float32)
    sel[127, :] = 1.0
    sel[63, :] = 1.0
    lg = np.log1p(-np.exp2(-5.0 - np.arange(4, dtype=np.float64)))
    diff = (i[None, :] - i[:, None]).astype(np.float64)
    DT = np.stack([np.where(diff >= 0, np.exp(np.maximum(diff, 0) * lg[h]), 0.0) for h in range(4)], 1)
    xi = np.exp((i[:, None] + 1.0) * lg[None, :])
    z128 = np.exp((127.0 - i)[:, None] * lg[None, :])
    z64 = np.exp((63.0 - i)[:, None] * lg[None, :])
    d128 = np.broadcast_to(np.exp(128 * lg)[None, :], (128, 4))
    d64 = np.broadcast_to(np.exp(64 * lg)[None, :], (128, 4))
    sel128 = np.zeros((128, 128), np.float32)
    sel128[127, :] = 1.0
    sel64 = np.zeros((128, 128), np.float32)
    sel64[63, :] = 1.0
    c32 = np.concatenate([ident, tri, negmask, negmaskT, sel128, DT.reshape(128, 512), xi, z128, z64, d128, d64, sel64], 1).astype(np.float32)
    return c128, s128, c64, s64, c32


_CACHE = {}


def get_nc(S_P, do_sample=True):
    key = (S_P, do_sample)
    if key not in _CACHE:
        b = Builder(S_P, do_sample)
        _CACHE[key] = b.build()
    return _CACHE[key]


W_NAMES = ["g_mix_pre", "w_in", "b_in", "g_mlstm", "w_up_m", "g_qa", "w_uq", "g_kva", "w_ukv", "w_up_a", "g_ret",
           "w_up_r", "w_o", "g_mix_post", "g_ffn_pre", "w_gu", "w_down", "g_ffn_post"]


def kernel(**inp):
    x_prompt = np.asarray(inp["x_prompt"], np.float32)
    Bn, S_P, _ = x_prompt.shape
    nc = get_nc(S_P)
    c128, s128, c64, s64, c32 = const_tables(S_P)
    shared = {k: np.ascontiguousarray(np.asarray(inp[k], np.float32)) for k in W_NAMES}
    shared.update(t_cos128=c128, t_sin128=s128, t_cos64=c64, t_sin64=s64, t_c32=c32)
    in_maps = []
    for b in range(Bn):
        m = dict(shared)
        m["xp"] = np.ascontiguousarray(x_prompt[b])
        m["xs"] = np.ascontiguousarray(np.asarray(inp["x_sample"], np.float32)[b])
        m["cckv"] = np.ascontiguousarray(np.asarray(inp["cache_mla_ckv"], np.float32)[:, b])
        m["ckr"] = np.ascontiguousarray(np.asarray(inp["cache_mla_krope"], np.float32)[:, b])
        m["sc"] = np.ascontiguousarray(np.asarray(inp["state_mlstm_c"], np.float32)[:, b])
        m["sn"] = np.ascontiguousarray(np.asarray(inp["state_mlstm_n"], np.float32)[:, b])
        m["sm"] = np.ascontiguousarray(np.asarray(inp["state_mlstm_m"], np.float32)[:, b])
        m["sr"] = np.ascontiguousarray(np.asarray(inp["state_ret"], np.float32)[:, b])
        in_maps.append(m)
    res = run_bass_kernel_spmd(nc, in_maps, core_ids=list(range(Bn)))
    R = res.results

    def st(k, axis):
        return np.stack([np.asarray(r[k], np.float32) for r in R], axis)
    return (st("yp", 0), st("ys", 0), st("o_pckv", 1), st("o_pkr", 1), st("o_pc", 1), st("o_pn", 1), st("o_pm", 1), st("o_pr", 1),
            st("o_sckv", 1), st("o_skr", 1), st("o_sc", 1), st("o_sn", 1), st("o_sm", 1), st("o_sr", 1))
```
